# Optimizing a Trainium2 kernel written in Bass

```python
import math, functools
import jax, jax.numpy as jnp
from jax import lax
import numpy as np

D_MODEL = 1024
BATCH = 16
SEQ = 2048
DEPTH = 2
DEC_BATCH = 128
DEC_SEQ = 4
PAST_LEN = 16384
PAGE_SIZE = 128

HEAD_DIM = 64
N_Q_HEADS = 8
N_KV_HEADS = 2
GQA = N_Q_HEADS // N_KV_HEADS
ATT_W = N_Q_HEADS * HEAD_DIM
KV_W = N_KV_HEADS * HEAD_DIM
WINDOW = 128
BLOCK = 128
NUM_BUCKETS = 32
MAX_DISTANCE = 128
RW_HEADS = 8
RW_W = RW_HEADS * HEAD_DIM
DECAY_LORA = 64
ICLR_LORA = 64
GATE_LORA = 128
RW_IN_W = 3 * RW_W + DECAY_LORA + ICLR_LORA + GATE_LORA
ATT_IN_W = ATT_W + 2 * KV_W
IN_W = ATT_IN_W + RW_IN_W
MIX_W = ATT_W + RW_W
D_FF = -(-8 * D_MODEL // (3 * 256)) * 256
NORM_EPS = 1e-6
GN_EPS = 64e-5
NEG = -1e30

kernel_name = "hymba_swa_sink_rwkv7_decode_step"


def rmsnorm(x, g):
    xf = x.astype(jnp.float32)
    y = xf * lax.rsqrt(jnp.mean(xf * xf, -1, keepdims=True) + NORM_EPS)
    return (y * g.astype(jnp.float32)).astype(x.dtype)


def t5_bucket(d):
    max_exact = NUM_BUCKETS // 2
    n = np.maximum(d, 0)
    large = max_exact + (np.log(np.maximum(n, 1) / max_exact) / np.log(MAX_DISTANCE / max_exact)
                         * (NUM_BUCKETS - max_exact)).astype(np.int32)
    large = np.minimum(large, NUM_BUCKETS - 1)
    return np.where(n < max_exact, n, large).astype(np.int32)


def rel_bias(rel_table, d):
    b = rel_table[t5_bucket(d)].astype(jnp.float32)
    return jnp.transpose(b, (2, 0, 1)).reshape((N_KV_HEADS, GQA) + d.shape)


def sink_attention(q, k, v, bias, mask, sink):
    s = jnp.einsum('...qkgd,...skd->...kgqs', q, k).astype(jnp.float32) * (HEAD_DIM ** -0.5)
    s = jnp.where(mask, s + bias, NEG)
    sink_col = jnp.broadcast_to(sink.astype(jnp.float32).reshape(N_KV_HEADS, GQA, 1, 1), s.shape[:-1] + (1,))
    p = jax.nn.softmax(jnp.concatenate([s, sink_col], -1), axis=-1)[..., :-1]
    o = jnp.einsum('...kgqs,...skd->...qkgd', p.astype(v.dtype), v)
    return o.reshape(o.shape[:-3] + (ATT_W,))


def swa_prompt(q, k, v, rel_table, sink):
    B, T = q.shape[:2]
    nb = T // BLOCK
    qb = q.reshape(B, nb, BLOCK, N_KV_HEADS, GQA, HEAD_DIM)

    def band(t):
        tb = t.reshape(B, nb, BLOCK, N_KV_HEADS, HEAD_DIM)
        prev = jnp.concatenate([jnp.zeros_like(tb[:, :1]), tb[:, :-1]], 1)
        return jnp.concatenate([prev, tb], 2)

    qi = np.arange(BLOCK)[:, None]
    sj = np.arange(2 * BLOCK)[None, :]
    d = qi + BLOCK - sj
    inside = (d >= 0) & (d <= WINDOW)
    exists = (np.arange(nb)[:, None, None] > 0) | (sj[None] >= BLOCK)
    mask = (inside[None] & exists)[:, None, None]
    o = sink_attention(qb, band(k), band(v), rel_bias(rel_table, d), mask, sink)
    w = min(WINDOW, T)
    return o.reshape(B, T, ATT_W), k[:, -w:], v[:, -w:]


def swa_sample(q, k, v, k_cache, v_cache, rel_table, sink):
    S = q.shape[1]
    wc = k_cache.shape[1]
    kf = jnp.concatenate([k_cache.astype(k.dtype), k], 1)
    vf = jnp.concatenate([v_cache.astype(v.dtype), v], 1)
    d = (wc + np.arange(S))[:, None] - np.arange(wc + S)[None, :]
    mask = (d >= 0) & (d <= WINDOW)
    o = sink_attention(q, kf, vf, rel_bias(rel_table, d), mask, sink)
    return o, kf[:, -wc:], vf[:, -wc:]


def rwkv7_group(zc, zp, mu, w0, w2, a0, a2, g2, k_k, k_a, r_k, gn_w, gn_b, wkv0):
    f32 = jnp.float32
    B, T = zc.shape[:2]
    zm = (zc + (zp - zc) * mu).astype(f32)
    r, k, v, zw, za, zg = jnp.split(zm, [RW_W, 2 * RW_W, 3 * RW_W, 3 * RW_W + DECAY_LORA,
                                         3 * RW_W + DECAY_LORA + ICLR_LORA], -1)
    w = -jax.nn.softplus(-(w0 + jnp.tanh(zw) @ w2)) - 0.5
    decay = jnp.exp(-jnp.exp(w))
    a = jax.nn.sigmoid(a0 + za @ a2)
    g = jax.nn.sigmoid(zg) @ g2
    heads = lambda t: t.reshape(B, T, RW_HEADS, HEAD_DIM)
    kk = heads(k * k_k)
    kk = kk / jnp.maximum(jnp.sqrt(jnp.sum(kk * kk, -1, keepdims=True)), 1e-12)
    k = k * (1.0 + (a - 1.0) * k_a)
    r, k, v, decay, a = heads(r), heads(k), heads(v), heads(decay), heads(a)

    def step(S, inp):
        r_t, w_t, k_t, v_t, kk_t, a_t = inp
        Skk = jnp.einsum('bhvi,bhi->bhv', S, kk_t)
        S = (S * w_t[:, :, None, :] - Skk[..., None] * (kk_t * a_t)[:, :, None, :]
             + v_t[..., None] * k_t[:, :, None, :])
        return S, jnp.einsum('bhvk,bhk->bhv', S, r_t)

    xs = tuple(jnp.swapaxes(t, 0, 1) for t in (r, decay, k, v, kk, a))
    S_fin, o = lax.scan(step, wkv0.astype(f32), xs)
    o = jnp.swapaxes(o, 0, 1)
    mean = jnp.mean(o, -1, keepdims=True)
    var = jnp.mean(jnp.square(o - mean), -1, keepdims=True)
    o = ((o - mean) * lax.rsqrt(var + GN_EPS)).reshape(B, T, RW_W) * gn_w + gn_b
    bonus = jnp.sum(r * k * r_k.reshape(RW_HEADS, HEAD_DIM), -1, keepdims=True) * v
    o = (o + bonus.reshape(B, T, RW_W)) * g
    return o, S_fin


def decoder_layer(x, h_prev, wkv0, attn_fn, n1, w_in, w_out, mu, w0, w2, a0, a2, g2, k_k, k_a, r_k,
                  gn_w, gn_b, n2, w_gate, w_up, w_down):
    B, T = x.shape[:2]
    h = rmsnorm(x, n1)
    z = h @ w_in
    q, k, v = jnp.split(z[..., :ATT_IN_W], [ATT_W, ATT_W + KV_W], -1)
    zc = z[..., ATT_IN_W:]
    zp_first = (h_prev.astype(h.dtype) @ w_in[:, ATT_IN_W:]).astype(zc.dtype)
    zp = jnp.concatenate([zp_first[:, None], zc[:, :-1]], 1)
    att, k_state, v_state = attn_fn(q.reshape(B, T, N_KV_HEADS, GQA, HEAD_DIM),
                                    k.reshape(B, T, N_KV_HEADS, HEAD_DIM),
                                    v.reshape(B, T, N_KV_HEADS, HEAD_DIM))
    rw, wkv = rwkv7_group(zc, zp, mu, w0, w2, a0, a2, g2, k_k, k_a, r_k, gn_w, gn_b, wkv0)
    x = x + (jnp.concatenate([att, rw.astype(att.dtype)], -1) @ w_out).astype(x.dtype)
    hf = rmsnorm(x, n2)
    x = x + ((jax.nn.silu(hf @ w_gate) * (hf @ w_up)) @ w_down).astype(x.dtype)
    return x, h[:, -1], k_state, v_state, wkv


def setup_inputs(seed: int = 0) -> dict:
    key = jax.random.key(seed)
    ks = jax.random.split(key, 32)
    nrm = lambda i, shape, s: jax.random.normal(ks[i], shape, jnp.float32) * s
    wc = min(WINDOW, PAST_LEN)
    return {
        'x_prompt': nrm(0, (BATCH, SEQ, D_MODEL), 1.0),
        'x_sample': nrm(1, (DEC_BATCH, DEC_SEQ, D_MODEL), 1.0),
        'cache_k': nrm(2, (DEPTH, DEC_BATCH, wc, N_KV_HEADS, HEAD_DIM), 1.0),
        'cache_v': nrm(3, (DEPTH, DEC_BATCH, wc, N_KV_HEADS, HEAD_DIM), 1.0),
        'state_shift': nrm(4, (DEPTH, DEC_BATCH, D_MODEL), 1.0),
        'state_wkv': nrm(5, (DEPTH, DEC_BATCH, RW_HEADS, HEAD_DIM, HEAD_DIM), 0.5),
        'norm1': 1.0 + nrm(6, (DEPTH, D_MODEL), 0.02),
        'w_in': nrm(7, (DEPTH, D_MODEL, IN_W), D_MODEL ** -0.5),
        'w_out': nrm(8, (DEPTH, MIX_W, D_MODEL), MIX_W ** -0.5),
        'sink': nrm(9, (DEPTH, N_Q_HEADS), 0.5),
        'rel_table': nrm(10, (NUM_BUCKETS, N_Q_HEADS), 0.5),
        'mu': jax.random.uniform(ks[11], (DEPTH, RW_IN_W), jnp.float32),
        'w0': jax.random.uniform(ks[12], (DEPTH, RW_W), jnp.float32, -6.0, -1.0),
        'w2': nrm(13, (DEPTH, DECAY_LORA, RW_W), 0.1),
        'a0': nrm(14, (DEPTH, RW_W), 0.1),
        'a2': nrm(15, (DEPTH, ICLR_LORA, RW_W), 0.1),
        'g2': nrm(16, (DEPTH, GATE_LORA, RW_W), GATE_LORA ** -0.5),
        'k_k': 0.85 + nrm(17, (DEPTH, RW_W), 0.05),
        'k_a': 1.0 + nrm(18, (DEPTH, RW_W), 0.05),
        'r_k': nrm(19, (DEPTH, RW_W), 0.1),
        'gn_w': 1.0 + nrm(20, (DEPTH, RW_W), 0.02),
        'gn_b': nrm(21, (DEPTH, RW_W), 0.02),
        'norm2': 1.0 + nrm(22, (DEPTH, D_MODEL), 0.02),
        'w_gate': nrm(23, (DEPTH, D_MODEL, D_FF), D_MODEL ** -0.5),
        'w_up': nrm(24, (DEPTH, D_MODEL, D_FF), D_MODEL ** -0.5),
        'w_down': nrm(25, (DEPTH, D_FF, D_MODEL), D_FF ** -0.5),
        'norm_f': 1.0 + nrm(26, (D_MODEL,), 0.02),
    }


def reference(x_prompt, x_sample, cache_k, cache_v, state_shift, state_wkv, norm1, w_in, w_out, sink,
              rel_table, mu, w0, w2, a0, a2, g2, k_k, k_a, r_k, gn_w, gn_b, norm2, w_gate, w_up,
              w_down, norm_f):
    xp, xs = x_prompt, x_sample
    kp, vp, sp, wp, ksl, vsl, ssl, wsl = [], [], [], [], [], [], [], []
    for l in range(DEPTH):
        lw = (norm1[l], w_in[l], w_out[l], mu[l], w0[l], w2[l], a0[l], a2[l], g2[l], k_k[l], k_a[l],
              r_k[l], gn_w[l], gn_b[l], norm2[l], w_gate[l], w_up[l], w_down[l])
        attn_p = functools.partial(swa_prompt, rel_table=rel_table, sink=sink[l])
        wkv_zero = jnp.zeros((xp.shape[0], RW_HEADS, HEAD_DIM, HEAD_DIM), jnp.float32)
        xp, h_last, k_new, v_new, wkv = decoder_layer(xp, jnp.zeros_like(xp[:, 0]), wkv_zero, attn_p, *lw)
        kp.append(k_new); vp.append(v_new); sp.append(h_last); wp.append(wkv)
        attn_s = functools.partial(swa_sample, k_cache=cache_k[l], v_cache=cache_v[l],
                                   rel_table=rel_table, sink=sink[l])
        xs, h_last_s, k_buf, v_buf, wkv_s = decoder_layer(xs, state_shift[l], state_wkv[l], attn_s, *lw)
        ksl.append(k_buf); vsl.append(v_buf); ssl.append(h_last_s); wsl.append(wkv_s)
    y_prompt = rmsnorm(xp, norm_f)
    y_sample = rmsnorm(xs, norm_f)
    k_prompt, v_prompt = jnp.stack(kp), jnp.stack(vp)
    shift_prompt, wkv_prompt = jnp.stack(sp), jnp.stack(wp)
    k_sample, v_sample = jnp.stack(ksl), jnp.stack(vsl)
    shift_sample, wkv_sample = jnp.stack(ssl), jnp.stack(wsl)
    return (y_prompt, y_sample, k_prompt, v_prompt, shift_prompt, wkv_prompt,
            k_sample, v_sample, shift_sample, wkv_sample)
```

```python
import contextlib
import numpy as np
import concourse.bass as bass
import concourse.mybir as mybir
from concourse.bass_utils import run_bass_kernel_spmd

F32 = mybir.dt.float32
BF16 = mybir.dt.bfloat16
AF = mybir.ActivationFunctionType
ALU = mybir.AluOpType

NCORES = 8
TT = 512
C0 = 0.6065306597126334
NPV = 124
NSLOT = 4


class V:
    __slots__ = ("ap", "keys")

    def __init__(self, ap, keys):
        self.ap = ap
        if isinstance(keys, tuple):
            keys = [keys]
        self.keys = list(keys)

    def __getitem__(self, idx):
        return V(self.ap[idx], self.keys)

    def k(self, *suffix):
        return V(self.ap, [k + tuple(suffix) for k in self.keys])

    def re(self, s, **kw):
        return V(self.ap.rearrange(s, **kw), self.keys)

    def bc(self, dtype):
        return V(self.ap.bitcast(dtype), self.keys)

    def bcast(self, axis, n):
        a = self.ap.unsqueeze(axis)
        shp = list(a.shape)
        shp[axis] = n
        return V(a.to_broadcast(shp), self.keys)


def vjoin(ap, vs):
    ks = []
    for v in vs:
        ks += v.keys
    return V(ap, ks)


class Prog:
    ENGS = ("pe", "act", "dve", "pool", "sp")

    def __init__(self, nc):
        self.nc = nc
        self.ops = []
        self.lastw = {}
        self.readers = {}
        self.stack = contextlib.ExitStack()
        self.fin = []
        self.eng_free = {}

    COST = {"pe": 0.14, "act": 0.5, "dve": 0.6, "pool": 0.85, "sp": 0.1}

    @staticmethod
    def _n(v):
        n = 1
        for d in v.ap.shape[1:]:
            n *= d
        return n

    def _cost(self, eng, out):
        n = self._n(out)
        if eng == "pe":
            return 0.06 + n / 2000.0
        if eng == "act":
            return 0.25 + n * 0.00085
        if eng == "dve":
            return 0.12 + n * 0.0011
        return 0.2 + n * 0.0022

    def sb(self, name, shape, dtype):
        t = self.stack.enter_context(self.nc.sbuf_tensor(name, list(shape), dtype))
        return V(t[:], (name,))

    def ps(self, name, shape, dtype=F32):
        t = self.stack.enter_context(self.nc.psum_tensor(name, list(shape), dtype))
        return V(t[:], (name,))

    def add(self, eng, fn, reads=(), writes=(), slot=None, cost=None):
        i = len(self.ops)
        rk = []
        for r in reads:
            rk += r.keys
        wk = []
        for w in writes:
            wk += w.keys
        deps = set()
        for r in rk:
            if r in self.lastw:
                deps.add(self.lastw[r])
        for w in wk:
            if w in self.lastw:
                deps.add(self.lastw[w])
            rd = self.readers.get(w)
            if rd:
                deps.update(rd.values())
        self.ops.append(dict(eng=eng, fn=fn, deps=deps, slot=slot))
        t0 = self.eng_free.get(eng, 0.0)
        for d in deps:
            od = self.ops[d]
            t0 = max(t0, self.fin[d] + (0.05 if (od["eng"] == eng and od["slot"] is None) else 0.3))
        if slot is not None:
            self.fin.append(t0 + 2.0)
            self.eng_free[eng] = t0 + 0.1
        else:
            f = t0 + (cost if cost is not None else self.COST[eng])
            self.fin.append(f)
            self.eng_free[eng] = f
        wks = set(wk)
        for w in wk:
            self.lastw[w] = i
            self.readers[w] = {}
        who = eng if slot is None else ("dma", slot)
        for r in rk:
            if r not in wks:
                self.readers.setdefault(r, {})[who] = i
        return i

    def mm(self, out, lhsT, rhs, start=True, stop=True, skip=False):
        kw = dict(start=start, stop=stop)
        if skip:
            kw["skip_group_check"] = True
        return self.add("pe", lambda e: e.matmul(out.ap, lhsT.ap, rhs.ap, **kw),
                        [lhsT, rhs] + ([] if start else [out]), [out], cost=self._cost("pe", out))

    def tr(self, out, in_, ident):
        return self.add("pe", lambda e: e.transpose(out.ap, in_.ap, ident.ap), [in_, ident], [out])

    def act(self, out, in_, func, bias=None, scale=None):
        kw = {}
        rd = [in_]
        if bias is not None:
            kw["bias"] = bias.ap
            rd.append(bias)
        if scale is not None:
            if isinstance(scale, V):
                kw["scale"] = scale.ap
                rd.append(scale)
            else:
                kw["scale"] = scale
        return self.add("act", lambda e: e.activation(out.ap, in_.ap, func, **kw), rd, [out], cost=self._cost("act", out))

    def tt(self, out, a, b, op, eng="dve"):
        return self.add(eng, lambda e: e.tensor_tensor(out.ap, a.ap, b.ap, op), [a, b], [out], cost=self._cost(eng, out))

    def ts(self, out, a, s1, op0, s2=None, op1=None, eng="dve"):
        rd = [a]
        s1a = s1.ap if isinstance(s1, V) else s1
        s2a = s2.ap if isinstance(s2, V) else s2
        if isinstance(s1, V):
            rd.append(s1)
        if isinstance(s2, V):
            rd.append(s2)
        kw = {}
        if op1 is not None:
            kw["op1"] = op1
        return self.add(eng, lambda e: e.tensor_scalar(out.ap, a.ap, s1a, s2a, op0, **kw), rd, [out], cost=self._cost(eng, out))

    def stt(self, out, a, s, b, op0, op1, eng="dve"):
        rd = [a, b]
        sa = s.ap if isinstance(s, V) else s
        if isinstance(s, V):
            rd.append(s)
        return self.add("dve", lambda e: e.scalar_tensor_tensor(out.ap, a.ap, sa, b.ap, op0, op1), rd, [out], cost=self._cost("dve", out))

    def copy(self, out, in_, eng="dve"):
        if eng == "act":
            return self.add("act", lambda e: e.copy(out.ap, in_.ap), [in_], [out], cost=self._cost("act", out))
        return self.add(eng, lambda e: e.tensor_copy(out.ap, in_.ap), [in_], [out], cost=self._cost(eng, out))

    def recip(self, out, in_):
        return self.add("dve", lambda e: e.reciprocal(out.ap, in_.ap), [in_], [out])

    def memset(self, out, val, eng="pool"):
        return self.add(eng, lambda e: e.memset(out.ap, val), [], [out])

    def dma(self, out, in_, slot, eng="sp", **kw):
        return self.add(eng, lambda e: e.dma_start(out=out.ap, in_=in_.ap, **kw), [in_], [out], slot=slot)

    def emit(self):
        nc = self.nc
        ops = self.ops
        n = len(ops)
        has_dep = [False] * n
        for o in ops:
            for d in o["deps"]:
                has_dep[d] = True
        cnt = {e: 0 for e in self.ENGS}
        slotcnt = {}
        for i, o in enumerate(ops):
            if o["slot"] is not None:
                s = "slot:" + o["slot"]
                slotcnt[s] = slotcnt.get(s, 0) + 16
                o["tok"] = (s, slotcnt[s])
            elif has_dep[i]:
                cnt[o["eng"]] += 1
                o["tok"] = ("eng:" + o["eng"], cnt[o["eng"]])
            else:
                o["tok"] = None
        sems = {}
        for name in [("eng:" + e) for e in self.ENGS] + list(slotcnt.keys()):
            sems[name] = self.stack.enter_context(nc.semaphore(name.replace(":", "_")))
        per_eng = {e: [] for e in self.ENGS}
        for i, o in enumerate(ops):
            per_eng[o["eng"]].append(i)
        stats = {e: [0, 0] for e in self.ENGS}

        def run_engine(ename, e):
            waited = {}
            for i in per_eng[ename]:
                o = ops[i]
                need = {}
                for d in o["deps"]:
                    od = ops[d]
                    tok = od["tok"]
                    if tok is None:
                        continue
                    if od["slot"] is None and od["eng"] == ename and ename == "pe":
                        continue
                    s, v = tok
                    if waited.get(s, 0) >= v:
                        continue
                    if need.get(s, 0) < v:
                        need[s] = v
                for s, v in need.items():
                    e.wait_ge(sems[s], v)
                    waited[s] = v
                    stats[ename][1] += 1
                ins = o["fn"](e)
                stats[ename][0] += 1
                if o["tok"] is not None:
                    s, v = o["tok"]
                    ins.then_inc(sems[s], 16 if o["slot"] is not None else 1)
            if ename == "sp":
                for s, v in slotcnt.items():
                    if waited.get(s, 0) < v:
                        e.wait_ge(sems[s], v)

        with nc.Block() as block:
            @block.sync
            def _(e):
                run_engine("sp", e)

            @block.scalar
            def _(e):
                run_engine("act", e)

            @block.vector
            def _(e):
                run_engine("dve", e)

            @block.gpsimd
            def _(e):
                run_engine("pool", e)

            @block.tensor
            def _(e):
                run_engine("pe", e)
        self.stats = stats
        self.stack.close()


def build_program(n_ptiles=8, do_sample=True, nlayers=2, debug=False):
    nc = bass.Bass("TRN2", target_bir_lowering=False)
    P = Prog(nc)

    def din(name, shape):
        return nc.dram_tensor(name, list(shape), F32, kind="ExternalInput")

    def dout(name, shape):
        return nc.dram_tensor(name, list(shape), F32, kind="ExternalOutput")

    def DV(t, ap=None):
        return V(t.ap() if ap is None else ap, (t.name if hasattr(t, "name") else str(id(t)),))

    xT = din("xT", [2, 1024, 2048])
    xsT = din("xsT", [1024, 64])
    shT = din("shT", [2, 1024, 16])
    ckT = din("ckT", [2, 128, 4096])
    cvd = din("cvd", [2, 128, 4096])
    ckr = din("ckr", [2, 16, 128 * 128])
    cvr = din("cvr", [2, 16, 128 * 128])
    hbd = din("hbd", [2, 16, 128, 512])
    w_in = din("w_in", [2, 1024, 2688])
    w_out = din("w_out", [2, 1024, 1024])
    w_gu = din("w_gu", [2, 1024, 5632])
    w_dn = din("w_dn", [2, 8, 128, 2816])
    lora = din("lora", [2, 128, 1536])
    pvec = din("pvec", [128, NPV])
    rel = din("rel", [32, 8])
    sink = din("sink", [1, 16])
    cbf = din("cbf", [128, 1344])
    c32 = din("c32", [128, 256])
    ohv = din("ohv", [33, 512])

    yT = dout("yT", [2, 1024, 2048])
    ysT = dout("ysT", [1024, 64])
    kp = dout("kp", [2, 2, 128, 128])
    vp = dout("vp", [2, 2, 128, 128])
    shp = dout("shp", [2, 2, 8, 128])
    wkvp = dout("wkvp", [2, 2, 128, 512])
    ks = dout("ks", [2, 16, 128 * 128])
    vs = dout("vs", [2, 16, 128 * 128])
    shs = dout("shs", [2, 1024, 16])
    wkvs = dout("wkvs", [2, 16, 128, 512])
    gscr = nc.dram_tensor("gscr", [8, 512], F32, kind="Internal")
    if debug:
        dbg_mix = nc.dram_tensor("dbg_mix", [128, 8 * TT], BF16, kind="ExternalOutput")
        dbg_z = nc.dram_tensor("dbg_z", [128, 14 * (TT + 1)], BF16, kind="ExternalOutput")
        dbg_x1 = nc.dram_tensor("dbg_x1", [128, 8 * TT], F32, kind="ExternalOutput")
        dbg_x2 = nc.dram_tensor("dbg_x2", [128, 8 * TT], F32, kind="ExternalOutput")
        dbg_hf = nc.dram_tensor("dbg_hf", [128, 8 * TT], BF16, kind="ExternalOutput")
    if debug:
        dbg_mixs = nc.dram_tensor("dbg_mixs", [128, 8, 64], BF16, kind="ExternalOutput")
    dbgs = {"on": False}

    def dv(t, name):
        return V(t, (name,))

    CB = P.sb("CB", [128, 1344], BF16)
    P.dma(CB, dv(cbf.ap(), "cbf"), "cst", eng="pool")
    ident = CB[:, 0:128]
    ones = CB[:, 128:256]
    bones = CB[:, 256:384]
    MG = CB[:, 384:640]
    MLs = CB[:, 640:768]
    BD4 = CB[0:64, 768:832]
    C32 = P.sb("C32", [128, 256], F32)
    P.dma(C32, dv(c32.ap(), "c32"), "cst32")
    ident32 = C32[:, 0:128]
    JM = C32[:, 128:256]
    M64 = CB[:, 832:1088]
    M4 = CB[:, 1088:1344]
    PVt = P.sb("PVEC", [128, NPV + 8 + 28], F32)
    P.dma(PVt[:, 0:NPV], dv(pvec.ap(), "pvec"), "pv")
    for l in range(2):
        P.ts(PVt[:, NPV + 4 * l:NPV + 4 * l + 4], PVt[:, 58 * l + 42:58 * l + 46], -1.0, ALU.mult, 1.0, ALU.add)
        P.ts(PVt[:, NPV + 8 + 14 * l:NPV + 8 + 14 * l + 14], PVt[:, 58 * l + 16:58 * l + 30], -1.0, ALU.mult, 1.0, ALU.add)

    def pv(l, name, j=None):
        base = {"n1": 0, "n2": 8, "mu": 16, "w0": 30, "a0": 34, "kk": 38, "ka": 42, "rk": 46, "gw": 50, "gb": 54}
        if name == "nf":
            b0 = 116
        elif name == "omka":
            b0 = NPV + 4 * l
        elif name == "omu":
            b0 = NPV + 8 + 14 * l
        else:
            b0 = 58 * l + base[name]
        return PVt[:, b0 + j:b0 + j + 1]

    EPS6 = P.sb("EPS6", [128, 1], F32)
    P.memset(EPS6, 1e-6)
    GNE = P.sb("GNE", [128, 1], F32)
    P.memset(GNE, 64e-5)
    LR = P.sb("LR", [128, 2, 1536], BF16)
    P.dma(LR, dv(lora.ap().rearrange("l p n -> p l n"), "lora"), "lr", eng="pool")

    PSB = [P.ps("psb%d" % i, [128, 512]) for i in range(8)]

    def bank(i):
        return V(PSB[i].ap, [("psb%d" % i,)])

    def half(i, h):
        return V(PSB[i].ap[:, h * 256:(h + 1) * 256], [("psb%d" % i,)])

    rr = {"d": 0, "h": 0, "ev": 0}

    rr.update({"pp": 0, "p1": 0, "p2": 0})

    def pd():
        rr["d"] = (rr["d"] + 1) % 4
        return bank((0, 1, 6, 7)[rr["d"]])

    def pd_prep():
        return bank(0)

    def pb1():
        rr["p1"] = (rr["p1"] + 1) % 3
        return bank(5 + rr["p1"])

    def pb2():
        rr["p2"] ^= 1
        return bank(1 + rr["p2"])

    rr["pa"] = 0

    def pba():
        rr["pa"] ^= 1
        return bank(3 + rr["pa"])

    def ph():
        rr["h"] = (rr["h"] + 1) % 4
        return half(4 + rr["h"], 0)

    SCR = P.sb("SCR", [128, 22, TT], BF16)
    REL = P.sb("REL", [33, 8], F32)
    P.memset(REL[32:33, :], -30000.0)
    P.dma(REL[0:32, :], dv(rel.ap(), "rel"), "rel")
    OH = P.sb("OH", [33, 512], F32)
    P.dma(OH, dv(ohv.ap(), "ohv"), "oh")
    pg = pd()
    P.mm(pg[0:8, :], REL, OH)
    GS = P.sb("GS", [8, 512], F32)
    P.act(GS, pg[0:8, :], AF.Exp)
    P.dma(dv(gscr.ap(), "gscr"), GS, "gscr")
    EB = P.sb("EB", [128, 2, 8, 128], BF16)
    for ty in range(2):
        for h in range(8):
            i_ = ty * 8 + h
            st = V(SCR.ap[:, i_, :].bitcast(F32)[:, 0:128], [("SCR", i_)])
            src = V(bass.AP(gscr, h * 512 + ty * 256, [[1, 128], [1, 128]]), ("gscr",))
            P.dma(st, src, "ebf%d" % i_, eng="sp")
            pj = pd()
            P.mm(pj[:, 0:128], JM, st)
            P.copy(EB[:, ty, h, :], pj[:, 0:128], eng=("dve" if i_ % 2 else "act"))
    EBSC = P.sb("EBSC", [64, 2, 16, 4, 4], BF16)
    for j in range(2):
        P.tt(EBSC[:, j], EB[0:64, 0, 4 * j:4 * j + 4, 0:64].re("p n (b t) -> p b n t", t=4),
             V(BD4.ap.rearrange("p (b t) -> p b t", t=4).unsqueeze(2).to_broadcast([64, 16, 4, 4]), BD4.keys), ALU.mult)
    ES = P.sb("ES", [128, 16], F32)
    P.dma(ES, V(bass.AP(sink, 0, [[0, 128], [1, 16]]), ("sink",)), "es")
    P.act(ES, ES, AF.Exp)

    X = P.sb("X", [128, 8, TT], F32)
    HB = P.sb("HB", [128, 8, TT], BF16)
    WR = [P.sb("WR%d" % i, [128, 4096], BF16) for i in range(NSLOT)]
    QBD = P.sb("QBD", [128, 2, TT // 128, 2, 2, 128], BF16)
    P.memset(QBD, 0.0)
    QBP = P.sb("QBP", [128, 2, 16, 2, 2, 4], BF16)
    P.memset(QBP, 0.0)
    KD = [P.sb("KD%d_%d" % (l, j), [128, 128 + TT], BF16) for l in range(2) for j in range(2)]
    VT = P.sb("VT", [128, TT], BF16)
    VD = [P.sb("VD%d" % l, [128, 5, 2, 2, 64], BF16) for l in range(2)]
    VDS = P.sb("VDS", [64, 2, 2, 64], BF16)
    KO32 = P.sb("KO32", [128, 128], F32)
    VO32 = P.sb("VO32", [128, 128], F32)
    KVO = [P.sb("KVO%d" % i, [128, 128], F32) for i in range(2)]
    ZRW = P.sb("ZRW", [128, 14, TT + 1], BF16)
    ZC = [P.sb("ZC%d" % l, [128, 14], BF16) for l in range(2)]
    MIX0 = P.sb("MIX", [128, 8, TT], BF16)
    MIX = V(MIX0.ap, [MIX0.keys[0] + (c,) for c in range(8)])

    def MIXc(c0, c1):
        return V(MIX0.ap[:, c0:c1], [MIX0.keys[0] + (c,) for c in range(c0, c1)])
    H32 = [P.sb("H32_%d" % l, [128, 4, 128], F32) for l in range(2)]
    _qf = QBD.ap.rearrange("p j b c h q -> p (j b c h q)").bitcast(F32)
    HS32 = [V(_qf[:, 512 * i:512 * (i + 1)].rearrange("p (a v) -> p a v", a=4), QBD.keys) for i in range(4)]
    HBF4 = [P.sb("HBF4_%d" % i, [128, 4, 128], BF16) for i in range(2)]
    RD = P.sb("RD", [128, 512], F32)
    RSTD = RD
    HL32 = P.sb("HL32", [128, 8, 16], F32)
    YO = vjoin(SCR.ap[:, 0:16, :].rearrange("p a t -> p (a t)").bitcast(F32).rearrange("p (k t) -> p k t", k=8),
               [SCR.k(i) for i in range(16)])
    GC = 256
    SG = P.sb("SG", [128, 4, TT], F32)
    AA = P.sb("AA", [128, 4, TT], BF16)
    GATE = P.sb("GATE", [128, 4, TT], BF16)
    Fb = {i: P.sb("F%d" % i, [128, GC], F32) for i in (2, 4, 5, 6, 7, 8, 9, 10, 11, 12, 13, 14)}
    Bb = [P.sb("B%d" % i, [128, GC], BF16) for i in range(6)]
    ET = [Fb[i].bc(BF16) for i in (8, 9, 10, 11)]
    ATb = P.sb("ATb", [128, 4, 2, 64], BF16)
    RT4 = [P.sb("RT4_%d" % i, [128, 4, 2, 64], BF16) for i in range(3)]
    BONb = [P.sb("BON%d" % i, [128, GC], F32) for i in range(3)]
    PCt2 = [P.sb("PCt%d" % i, [128, 64], F32) for i in range(3)]
    BTb = P.sb("BTb", [128, 4, 2, 64], BF16)
    KTb = P.sb("KTb", [128, 4, 2, 64], BF16)
    BHb = P.sb("BHb", [128, 4, 2, 64], BF16)
    KHb = P.sb("KHb", [128, 4, 2, 64], BF16)
    VVb = P.sb("VVb", [128, 4, 2, 64], BF16)
    for t_ in (ATb, RT4[0], RT4[1], RT4[2], BTb, KTb, BHb, KHb, VVb):
        P.memset(t_, 0.0)
    def scrv(c0, n_):
        return vjoin(SCR.ap[:, c0:c0 + n_, :].rearrange("p a t -> p (a t)"), [SCR.k(i) for i in range(c0, c0 + n_)])
    TM4s = [[scrv(8 + c, 1).re("p (q t) -> p q t", q=4) for c in range(4)],
            [scrv(c, 1).re("p (q t) -> p q t", q=4) for c in range(4)]]
    TM4pairs = [[scrv(8, 2), scrv(10, 2)], [scrv(0, 2), scrv(2, 2)]]
    NT4 = [scrv(12, 1).re("p (c t) -> p c t", c=4), scrv(13, 1).re("p (c t) -> p c t", c=4)]
    NN4 = [scrv(14, 1).re("p (c t) -> p c t", c=4), scrv(15, 1).re("p (c t) -> p c t", c=4)]
    S4 = [scrv(16, 1).re("p (c t) -> p c t", c=4), scrv(17, 1).re("p (c t) -> p c t", c=4)]
    ETS = [scrv(18, 1), scrv(19, 1)]
    N4 = P.sb("N4", [128, 4, 128], BF16)
    AAK4 = P.sb("AAK4", [128, 4, 128], BF16)
    W14 = P.sb("W14", [128, 4, 128], BF16)
    MRB4s = [P.sb("MRB4", [128, 4, 128], BF16), scrv(4, 1).re("p (c t) -> p c t", c=4)]
    MRK4s = [P.sb("MRK4", [128, 4, 128], BF16), scrv(5, 1).re("p (c t) -> p c t", c=4)]
    APT4s = [P.sb("APT4", [128, 4, 128], BF16), scrv(6, 1).re("p (c t) -> p c t", c=4)]
    UP4s = [P.sb("UP4", [128, 4, 128], BF16), scrv(7, 1).re("p (c t) -> p c t", c=4)]
    TMPB = P.sb("TMPB", [128, 4, 128], BF16)
    U4 = P.sb("U4", [128, 4, 128], BF16)
    TMPH = P.sb("TMPH", [128, 4, 128], F32)
    MUs_b = V(CB.ap[:, 384:512].unsqueeze(1).to_broadcast([128, 4, 128]), CB.keys)
    MUi_b = V(CB.ap[:, 512:640].unsqueeze(1).to_broadcast([128, 4, 128]), CB.keys)
    MLs_b = V(CB.ap[:, 640:768].unsqueeze(1).to_broadcast([128, 4, 128]), CB.keys)
    ID_b = V(CB.ap[:, 0:128].unsqueeze(1).to_broadcast([128, 4, 128]), CB.keys)
    OT = P.sb("OT", [128, GC], F32)
    CPB = vjoin(SCR.ap[:, 16:20, :].rearrange("p a t -> p (a t)").bitcast(F32).rearrange("p (b f) -> p b f", b=8),
                [SCR.k(i) for i in range(16, 20)])

    wblocks = []

    def layer_blocks(l):
        r = []
        wv = w_in.ap()[l].rearrange("(kc p) n -> p kc n", p=128)
        for b in range(6):
            c0, c1 = 512 * b, min(512 * (b + 1), 2688)
            r.append((wv[:, :, c0:c1], "k8", c1 - c0))
        wv = w_out.ap()[l].rearrange("(kc p) n -> p kc n", p=128)
        for b in range(2):
            r.append((wv[:, :, 512 * b:512 * (b + 1)], "k8", 512))
        wv = w_gu.ap()[l].rearrange("(kc p) n -> p kc n", p=128)
        for b in range(11):
            r.append((wv[:, :, 512 * b:512 * (b + 1)], "k8", 512))
        for oc in range(8):
            r.append((w_dn.ap()[l, oc], "flat", 2816))
        return r

    ntiles = n_ptiles + (1 if do_sample else 0)
    for _t in range(ntiles):
        for l in range(nlayers):
            wblocks.extend(layer_blocks(l))
    wst = {"issued": 0, "next": 0}

    def wslot_view(i):
        src, kind, w = wblocks[i]
        s = WR[i % NSLOT]
        if kind == "k8":
            return s.re("p (kc n) -> p kc n", kc=8)[:, :, 0:w]
        return s[:, 0:w]

    nbl = 27 * nlayers
    wscr = nc.dram_tensor("wscr", [nbl, 128, 4096], BF16, kind="Internal")

    def wget():
        i = wst["next"]
        wst["next"] += 1
        while wst["issued"] < min(len(wblocks), i + NSLOT):
            j = wst["issued"]
            sl = j % NSLOT
            if j < nbl:
                P.dma(wslot_view(j), V(wblocks[j][0], ("wdram",)), "w%d" % sl, eng="pool")
                if ntiles > 1:
                    P.dma(V(wscr.ap()[j], ("wscr", j)), WR[sl], "wb%d" % sl, eng="sp")
            else:
                P.dma(WR[sl], V(wscr.ap()[j % nbl], ("wscr", j % nbl)), "wh%d" % sl, eng="sp")
            wst["issued"] += 1
        return wslot_view(i)

    def rmsnorm(xv, n, gname, l, outv, f32_cols=None):
        sq = SCR[:, 0:8, 0:n]
        for kc in range(8):
            P.act(SCR[:, kc, 0:n].k(kc), xv[:, kc, :], AF.Square)
        pn = pd()
        for kc in range(8):
            P.mm(pn[:, 0:n], ones, SCR[:, kc, 0:n].k(kc), start=(kc == 0), stop=(kc == 7))
        P.act(RSTD[:, 0:n], pn[:, 0:n], AF.Ln, bias=EPS6, scale=1.0 / 1024)
        P.act(RSTD[:, 0:n], RSTD[:, 0:n], AF.Exp, scale=-0.5)
        for kc in range(8):
            P.stt(outv[:, kc, :], xv[:, kc, :], pv(l, gname, kc), RSTD[:, 0:n], ALU.mult, ALU.mult,
                  eng=("dve" if kc % 2 == 0 else "pool"))
        if f32_cols is not None:
            cols, dst = f32_cols
            for ci, c in enumerate(cols):
                P.stt(dst[:, :, ci:ci + 1], xv[:, :, c:c + 1], 1.0, RSTD[:, c:c + 1].bcast(1, 8), ALU.mult, ALU.mult)
            for kc in range(8):
                P.ts(dst[:, kc, 0:len(cols)], dst[:, kc, 0:len(cols)], pv(l, gname, kc), ALU.mult)

    def dense8(lhs_fn, rhs_v, n, alloc=None):
        ps = (alloc or pd)()
        for kc in range(8):
            P.mm(ps[:, 0:n], lhs_fn(kc), rhs_v[:, kc, 0:n], start=(kc == 0), stop=(kc == 7))
        return ps

    def attention_block(l, j, qcols, ncur, kcur, kprev, vcur, vprev, mixcols, ebc, ebp, sample=False):
        nq = ncur
        W4 = 4 * nq
        etc = ET[rr["ev"] % 2]
        etp = ET[2 + rr["ev"] % 2]
        rr["ev"] += 1
        pa = bank(2)
        P.mm(pa[0:ncur, 0:W4], kcur, QBD[:, j, qcols].re("p c h q -> p (c h q)"))
        P.act(etc[0:ncur, 0:W4], pa[0:ncur, 0:W4], AF.Exp, scale=0.125)
        P.tt(etc[0:ncur, 0:W4].re("p (n q) -> p n q", n=4), etc[0:ncur, 0:W4].re("p (n q) -> p n q", n=4), ebc, ALU.mult, eng="pool")
        po = bank(4)
        pdn = bank(5)
        if kprev is not None:
            pb = bank(3)
            P.mm(pb[:, 0:W4], kprev, QBD[:, j, qcols].re("p c h q -> p (c h q)"))
            P.act(etp[:, 0:W4], pb[:, 0:W4], AF.Exp, scale=0.125)
            P.tt(etp[:, 0:W4].re("p (n q) -> p n q", n=4), etp[:, 0:W4].re("p (n q) -> p n q", n=4), ebp, ALU.mult, eng="pool")
            P.mm(po[:, 0:W4], vprev, etp[:, 0:W4], start=True, stop=False)
            P.mm(po[:, 0:W4], vcur, etc[0:ncur, 0:W4], start=False, stop=True)
            P.mm(pdn[:, 0:W4], ones, etp[:, 0:W4], start=True, stop=False)
            P.mm(pdn[:, 0:W4], ones[0:ncur, :], etc[0:ncur, 0:W4], start=False, stop=True)
        else:
            P.mm(po[:, 0:W4], vcur, etc[0:ncur, 0:W4])
            P.mm(pdn[:, 0:W4], ones[0:ncur, :], etc[0:ncur, 0:W4])
        finish_attn(l, j, po, pdn, nq, mixcols)

    def finish_attn(l, j, po, pdn, nq, mixcols):
        W4 = 4 * nq
        P.tt(RD[:, 0:W4].re("p (n q) -> p n q", n=4), pdn[:, 0:W4].re("p (n q) -> p n q", n=4),
             ES[:, 8 * l + 4 * j:8 * l + 4 * j + 4].bcast(2, nq), ALU.add)
        P.act(RD[:, 0:W4], RD[:, 0:W4], AF.Ln)
        P.act(RD[:, 0:W4], RD[:, 0:W4], AF.Exp, scale=-1.0)
        for hh in range(2):
            ps_ = slice(hh * 64, (hh + 1) * 64)
            src = po[ps_, 0:W4].re("p (c h q) -> p c h q", c=2, h=2)[:, :, hh, :]
            rdv = RD[ps_, 0:W4].re("p (c h q) -> p c h q", c=2, h=2)[:, :, hh, :]
            P.tt(MIX[ps_, 2 * j:2 * j + 2, mixcols], src, rdv, ALU.mult)

    def pbank():
        rr["h"] = (rr["h"] + 1) % 6
        return bank(2 + rr["h"])

    def c4(b_):
        return b_.re("p (c t) -> p c t", c=4)

    def F4(i, m):
        return Fb[i][:, 0:4 * m].re("p (a m) -> p a m", a=4)

    def B4(i, m):
        return Bb[i][:, 0:4 * m].re("p (a m) -> p a m", a=4)

    def pv4(l, name, m):
        base = {"mur": 16, "muk": 20, "muv": 24, "kk": 38, "ka": 42, "rk": 46, "gw": 50, "gb": 54}
        b0 = (NPV + 4 * l) if name == "omka" else 58 * l + base[name]
        return V(PVt.ap[:, b0:b0 + 4].unsqueeze(2).to_broadcast([128, 4, m]), PVt.keys)

    def run_streams(*gens, bg=None):
        gens = [g for g in gens if g is not None]
        nfg = len(gens)
        if bg is not None and not bg["done"]:
            gens.append(bg["gen"])
        ng = len(gens)
        done = [False] * ng
        hold = [False] * ng
        started = [False] * ng
        clock = [0.0] * ng
        if ng > nfg:
            clock[nfg] = bg["clock"]
        nread = 0
        while not all(done[:nfg]):
            cand = [i for i in range(ng) if not done[i] and not (hold[i] and nread > 0)]
            assert cand, "stream scheduler stuck"
            i = min(cand, key=lambda j: (clock[j], j))
            hold[i] = False
            n0 = len(P.ops)
            try:
                v = next(gens[i])
            except StopIteration:
                done[i] = True
                if i >= nfg:
                    bg["done"] = True
                if started[i]:
                    nread -= 1
                    started[i] = False
                continue
            if len(P.ops) > n0:
                clock[i] = max(P.fin[n0:])
                if i >= nfg:
                    bg["clock"] = clock[i]
            if v == "GSTART":
                nread += 1
                started[i] = True
            elif v == "GDONE":
                nread -= 1
                started[i] = False
            elif v == "BD":
                hold[i] = True

    def rwkv_gates_gen(l, zc_fn, n, o=0, alloc=None):
        alloc = alloc or pd
        zwa = zc_fn(12)
        th = Bb[0][:, 0:n]
        zab = zwa
        P.act(th, zwa, AF.Tanh)
        for j in range(4):
            p1 = alloc()
            P.mm(p1[:, 0:n], LR[:, l, j * 128:(j + 1) * 128], th)
            P.act(SG.k(o // 256)[:, j, o:o + n], p1[:, 0:n], AF.Sigmoid, bias=pv(l, "w0", j))
            p2 = alloc()
            P.mm(p2[:, 0:n], LR[:, l, 512 + j * 128:512 + (j + 1) * 128], zab)
            P.act(AA.k(o // 256)[:, j, o:o + n], p2[:, 0:n], AF.Sigmoid, bias=pv(l, "a0", j))
            yield
        zg = zc_fn(13)
        sgz = Bb[1][:, 0:n]
        P.act(sgz, zg, AF.Sigmoid)
        for j in range(4):
            p1 = alloc()
            P.mm(p1[:, 0:n], LR[:, l, 1024 + j * 128:1024 + (j + 1) * 128], sgz)
            P.copy(GATE.k(o // 256)[:, j, o:o + n], p1[:, 0:n], eng="act")
            yield

    def rwkv_gates(l, zc_fn, n, o=0):
        for _ in rwkv_gates_gen(l, zc_fn, n, o):
            pass

    def elem_gen(l, z4, sg4, aa4, m, CL, bs, msk):
        G = m // CL

        def g3(v):
            return v.re("p a (g t) -> p (a g) t", t=CL)
        R, Kraw, Vv, K = z4("r"), z4("k"), z4("v"), F4(2, m)
        KAP, T2, BETA = F4(4, m), F4(5, m), F4(6, m)
        BON = BONb[bs][:, 0:4 * m].re("p (a m) -> p a m", a=4)
        CS, CSX, CSH, ENEG, EPOS = F4(8, m), F4(9, m), F4(10, m), F4(11, m), F4(12, m)
        sqk = B4(2, m)
        P.tt(KAP, Kraw, pv4(l, "kk", m), ALU.mult)
        P.act(sqk, KAP, AF.Square)
        yield
        p1 = pd_prep()
        P.mm(p1[:, 0:4 * m], bones, sqk.re("p a m -> p (a m)"))
        P.ts(T2.re("p a m -> p (a m)"), p1[:, 0:4 * m], 1e-24, ALU.max)
        yield
        P.act(T2, T2, AF.Ln)
        P.act(T2, T2, AF.Exp, scale=-0.5)
        yield
        P.tt(KAP, KAP, T2, ALU.mult)
        P.tt(T2, aa4, pv4(l, "ka", m), ALU.mult, eng="pool")
        yield
        P.tt(T2, T2, pv4(l, "omka", m), ALU.add, eng="pool")
        P.tt(K, Kraw, T2, ALU.mult)
        yield
        P.stt(BETA, KAP, -1.0, aa4, ALU.mult, ALU.mult)
        rk = B4(3, m)
        P.tt(T2, R, pv4(l, "rk", m), ALU.mult, eng="pool")
        yield
        P.tt(rk, T2, K, ALU.mult)
        p2 = pd_prep()
        P.mm(p2[:, 0:4 * m], bones, rk.re("p a m -> p (a m)"))
        yield
        P.tt(BON, p2[:, 0:4 * m].re("p (a m) -> p a m", a=4), Vv, ALU.mult)
        SGc = F4(7, m)
        P.copy(SGc, sg4, eng="act")
        yield
        csf, sgf, mkf = CS.re("p a m -> p (a m)"), SGc.re("p a m -> p (a m)"), msk[:, 0:4 * m]
        P.add("dve", lambda e, o=csf, m_=mkf, s_=sgf: e.tensor_tensor_scan(o.ap, m_.ap, s_.ap, 0.0, ALU.mult, ALU.add),
              [mkf, sgf], [csf])
        yield
        P.tt(CSX, CS, SGc, ALU.subtract, eng="pool")
        last = g3(CS)[:, :, CL - 1:CL].re("p g o -> p (g o)")
        P.tt(g3(CSH), last.bcast(2, CL), g3(CS), ALU.subtract, eng="pool")
        yield
        P.act(CSX, CSX, AF.Exp, scale=-C0)
        P.act(ENEG, CS, AF.Exp, scale=C0)
        yield
        P.act(EPOS, CS, AF.Exp, scale=-C0)
        P.act(CSH, CSH, AF.Exp, scale=-C0)
        P.act(PCt2[bs][:, 0:4 * G], last, AF.Exp, scale=-C0)
        yield

    def bd_gen(m, CL, cols, rs, z4):
        R, K, Vv = z4("r"), F4(2, m), z4("v")
        KAP, BETA = F4(4, m), F4(6, m)
        CSX, CSH, ENEG, EPOS = F4(9, m), F4(10, m), F4(11, m), F4(12, m)
        yield "BD"
        for hh in range(2):
            ps_ = slice(hh * 64, (hh + 1) * 64)
            eng = "dve" if hh == 0 else "pool"

            def sv(x):
                return x[ps_, :, cols]
            P.tt(ATb[ps_, :, hh, 0:CL], sv(KAP), sv(CSX), ALU.mult, eng=eng)
            P.tt(RT4[rs][ps_, :, hh, 0:CL], sv(R), sv(EPOS), ALU.mult, eng=eng)
            yield
            P.tt(BTb[ps_, :, hh, 0:CL], sv(BETA), sv(ENEG), ALU.mult, eng=eng)
            P.tt(KTb[ps_, :, hh, 0:CL], sv(K), sv(ENEG), ALU.mult, eng=eng)
            yield
            P.tt(BHb[ps_, :, hh, 0:CL], sv(BETA), sv(CSH), ALU.mult, eng=eng)
            P.tt(KHb[ps_, :, hh, 0:CL], sv(K), sv(CSH), ALU.mult, eng=eng)
            yield
            P.copy(VVb[ps_, :, hh, 0:CL], sv(Vv), eng=eng)
        yield

    def ph1_gen(k, rs):
        yield "GSTART"
        atv = [ATb[:, p].re("p h t -> p (h t)") for p in range(4)]
        rtv = [RT4[rs][:, p].re("p h t -> p (h t)") for p in range(4)]
        btv = [BTb[:, p].re("p h t -> p (h t)") for p in range(4)]
        ktv = [KTb[:, p].re("p h t -> p (h t)") for p in range(4)]
        for half_ in range(2):
            ptr = pb1()
            ptb = ptr.bc(BF16)
            for c_ in range(2):
                p = 2 * half_ + c_
                for qi, src in enumerate((atv[p], BHb[:, p].re("p h t -> p (h t)"), KHb[:, p].re("p h t -> p (h t)"),
                                          VVb[:, p].re("p h t -> p (h t)"))):
                    P.tr(ptb[:, (c_ * 4 + qi) * 128:(c_ * 4 + qi + 1) * 128], src, ident)
                yield
            P.copy(TM4pairs[k][half_], ptb, eng="act")
        def gprod(lf, rf):
            b_ = pb1()
            for p in range(4):
                P.mm(b_[:, p * 128:(p + 1) * 128], lf[p], rf[p])
            return b_
        bE = gprod(atv, btv)
        P.tt(NT4[0], c4(bE), MLs_b, ALU.mult)
        yield
        bA = gprod(btv, atv)
        P.tt(N4, c4(bA), MUs_b, ALU.mult)
        P.tt(S4[0], N4, ID_b, ALU.add, eng="pool")
        yield
        bC = gprod(ktv, atv)
        P.tt(AAK4, c4(bC), MUs_b, ALU.mult)
        yield
        bB = gprod(btv, rtv)
        P.copy(TMPB, c4(bB), eng="act")
        P.tt(MRB4s[k], TMPB, MUi_b, ALU.mult, eng="pool")
        yield
        bD = gprod(ktv, rtv)
        yield "GDONE"
        P.copy(W14, c4(bD), eng="act")
        P.tt(MRK4s[k], W14, MUi_b, ALU.mult, eng="pool")
        yield
        Nk, NTk, Sk = N4, NT4[0], S4[0]
        for lvl in range(5):
            NTn = NT4[(lvl + 1) % 2]
            q1 = pb1()
            for p in range(4):
                P.mm(q1[:, p * 128:(p + 1) * 128], Nk[:, p], NTk[:, p])
            P.copy(NTn, c4(q1), eng="act")
            yield
            if lvl < 4:
                Nn = NN4[lvl % 2]
                q2 = pb1()
                for p in range(4):
                    P.mm(q2[:, p * 128:(p + 1) * 128], NTk[:, p], Nk[:, p])
                P.copy(Nn, c4(q2), eng="act")
                yield
            Sn = S4[(lvl + 1) % 2]
            q3 = pb1()
            for p in range(4):
                P.mm(q3[:, p * 128:(p + 1) * 128], NTn[:, p], Sk[:, p])
            P.tt(Sn, c4(q3), Sk, ALU.add)
            yield
            NTk, Sk = NTn, Sn
            if lvl < 4:
                Nk = Nn
        q4 = pb1()
        q5 = pb1()
        for p in range(4):
            P.mm(q5[:, p * 128:(p + 1) * 128], AAK4[:, p], TM4s[k][p][:, 3])
        P.copy(W14, c4(q5), eng="act")
        yield
        for p in range(4):
            P.mm(q4[:, p * 128:(p + 1) * 128], TM4s[k][p][:, 0], Sk[:, p])
        P.copy(APT4s[k], c4(q4), eng="act")
        yield
        q6 = pb1()
        for p in range(4):
            P.mm(q6[:, p * 128:(p + 1) * 128], Sk[:, p], W14[:, p])
        P.copy(UP4s[k], c4(q6), eng="act")
        yield
    def ph2_gen(l, CL, k, rs, pc4, H32v, hb_old, hb_new, ot_dst, gn=None):
        rtv = [RT4[rs][:, p].re("p h t -> p (h t)") for p in range(4)]
        P.tt(TMPH, H32v, pc4.bcast(2, 128), ALU.mult, eng="pool")
        q7 = pb2()
        for p in range(4):
            P.mm(q7[:, p * 128:(p + 1) * 128], APT4s[k][:, p], hb_old[:, p])
        P.tt(U4, c4(q7), UP4s[k], ALU.add)
        yield
        q9 = pb2()
        for p in range(4):
            P.mm(q9[:, p * 128:(p + 1) * 128], TM4s[k][p][:, 2], TM4s[k][p][:, 3], start=True, stop=False)
            P.mm(q9[:, p * 128:(p + 1) * 128], TM4s[k][p][:, 1], U4[:, p], start=False, stop=True)
        yield
        P.tt(hb_new, TMPH, c4(q9), ALU.add)
        P.tt(H32v, TMPH, c4(q9), ALU.add)
        yield
        q8 = pb2()
        for p in range(4):
            cs_ = slice(p * 128, (p + 1) * 128)
            P.mm(q8[:, cs_], hb_old[:, p], rtv[p], start=True, stop=False)
            P.mm(q8[:, cs_], U4[:, p], MRB4s[k][:, p], start=False, stop=False)
            P.mm(q8[:, cs_], TM4s[k][p][:, 3], MRK4s[k][:, p], start=False, stop=True)
        yield
        for hh in range(2):
            ps_ = slice(hh * 64, (hh + 1) * 64)
            P.copy(ot_dst[ps_], c4(q8)[ps_, :, hh * 64:hh * 64 + CL], eng="act")
        yield
        if gn is not None:
            for _ in gn():
                yield

    def gn_gen(l, m, bon4, gate4, mixv):
        otv = OT[:, 0:4 * m]
        ob = Bb[4][:, 0:4 * m]
        P.copy(ob, otv, eng="act")
        p1 = pb2()
        P.mm(p1[:, 0:4 * m], bones, ob)
        yield
        CEN = Fb[13][:, 0:4 * m]
        P.stt(CEN, p1[:, 0:4 * m], -1.0 / 64, otv, ALU.mult, ALU.add)
        sq = Bb[5][:, 0:4 * m]
        P.act(sq, CEN, AF.Square)
        yield
        p2 = pb2()
        P.mm(p2[:, 0:4 * m], bones, sq)
        RS = Fb[14][:, 0:4 * m]
        P.act(RS, p2[:, 0:4 * m], AF.Ln, bias=GNE, scale=1.0 / 64)
        yield
        P.act(RS, RS, AF.Exp, scale=-0.5)
        P.tt(CEN, CEN, RS, ALU.mult)
        yield
        C3 = CEN.re("p (a m) -> p a m", a=4)
        P.tt(C3, C3, pv4(l, "gw", m), ALU.mult, eng="pool")
        P.tt(C3, C3, pv4(l, "gb", m), ALU.add, eng="pool")
        yield
        P.tt(C3, C3, bon4, ALU.add, eng="pool")
        P.tt(mixv, C3, gate4, ALU.mult)
        yield

    def out_ffn(l, n):
        for oc in range(8):
            if oc % 4 == 0:
                w = wget()
            o = (oc % 4) * 128
            ps = dense8(lambda kc: w[:, kc, o:o + 128], MIX, n)
            P.tt(X[:, oc, 0:n], ps[:, 0:n], X[:, oc, 0:n], ALU.add)
        if debug and dbgs["on"]:
            P.dma(dv(dbg_x1.ap(), "dbg_x1"), X.re("p a t -> p (a t)"), "dbgx1")
        rmsnorm(X[:, :, 0:n], n, "n2", l, HB[:, :, 0:n])
        if debug and dbgs["on"]:
            P.dma(dv(dbg_hf.ap(), "dbg_hf"), HB.re("p a t -> p (a t)"), "dbghf")
        for b in range(11):
            w = wget()
            for jj in range(2):
                j = 2 * b + jj
                pg_ = dense8(lambda kc: w[:, kc, jj * 256:jj * 256 + 128], HB, n)
                pu_ = dense8(lambda kc: w[:, kc, jj * 256 + 128:jj * 256 + 256], HB, n)
                sg_ = RD[:, 0:n]
                P.act(sg_, pg_[:, 0:n], AF.Silu)
                P.tt(SCR[:, j, 0:n].k(j), pu_[:, 0:n], sg_, ALU.mult)
        for oc in range(8):
            w = wget().re("p (j m) -> p j m", j=22)
            ps = pd()
            for j in range(22):
                P.mm(ps[:, 0:n], w[:, j, :], SCR[:, j, 0:n].k(j), start=(j == 0), stop=(j == 21))
            P.tt(X[:, oc, 0:n], ps[:, 0:n], X[:, oc, 0:n], ALU.add)

    def in_proj_gen(l, n, next_, evac, blks, alloc=None):
        for blk in blks:
            w = wget()
            for off in range(4):
                c = 4 * blk + off
                if c >= 21:
                    break
                role = c + 7 if c < 14 else c - 14
                ncol = n if role < 7 else next_
                ps = dense8(lambda kc: w[:, kc, off * 128:(off + 1) * 128], HB, ncol, alloc=alloc)
                evac(role, ps)
                yield

    def in_proj(l, n, next_, evac, blks=range(6)):
        for _ in in_proj_gen(l, n, next_, evac, blks):
            pass

    okv = {"i": 0}

    def kv_out(src32, ncols, dst_ap, name, alloc=None):
        pt = (alloc or pd)()
        P.tr(pt[0:ncols, 0:128], src32[:, 0:ncols], ident32)
        st = KVO[okv["i"] % 2]
        P.copy(st[0:ncols, :], pt[0:ncols, 0:128], eng="act")
        P.dma(V(dst_ap, (name,)), st[0:ncols, :], "kvo%d" % (okv["i"] % 2))
        okv["i"] += 1

    def sample_tile():
        n = 64
        for t_ in (ATb, RT4[0], RT4[1], RT4[2], BTb, KTb, BHb, KHb, VVb):
            P.memset(t_, 0.0)
        P.dma(X[:, :, 0:n], dv(xsT.ap().rearrange("(kc p) t -> p kc t", p=128), "xsT"), "x")
        for l in range(nlayers):
            rmsnorm(X[:, :, 0:n], n, "n1", l, HB[:, :, 0:n], f32_cols=([4 * b + 3 for b in range(16)], HL32))
            P.dma(dv(shs.ap()[l].rearrange("(kc p) b -> p kc b", p=128), "shs"), HL32, "hl")
            P.dma(HB[:, :, 64:80], dv(shT.ap()[l].rearrange("(kc p) b -> p kc b", p=128), "shT"), "sh", eng="pool")
            CK = vjoin(SCR.ap[:, 0:8, :].rearrange("p a t -> p (a t)"), [SCR.k(i) for i in range(8)])
            CVv = vjoin(SCR.ap[:, 8:16, :].rearrange("p a t -> p (a t)"), [SCR.k(i) for i in range(8, 16)])
            P.dma(CK, dv(ckT.ap()[l], "ckT"), "ck", eng="pool")
            P.dma(CVv, dv(cvd.ap()[l], "cvd"), "cv", eng="pool")
            CK4 = CK.re("p (b j s) -> p b j s", b=16, j=2)
            CV4 = CVv.re("p (b j s) -> p b j s", b=16, j=2)
            for src_, dst_, nm_ in ((ckr, ks, "ks"), (cvr, vs, "vs")):
                sv = src_.ap()[l].rearrange("b (s f) -> s b f", f=128)[4:128]
                dvv = dst_.ap()[l].rearrange("b (s f) -> s b f", f=128)[0:124]
                for hb_ in range(2):
                    P.dma(CPB[0:124], dv(sv[:, 8 * hb_:8 * hb_ + 8, :], nm_ + "src"), "cpb")
                    P.dma(dv(dvv[:, 8 * hb_:8 * hb_ + 8, :], nm_), CPB[0:124], "cpb")

            def evac(c, ps):
                if c < 4:
                    for hh in range(2):
                        ps_ = slice(hh * 64, (hh + 1) * 64)
                        P.copy(QBP[ps_, c // 2, :, c % 2, hh, :], ps[ps_, 0:n].re("p (b t) -> p b t", t=4), eng="act")
                elif c < 6:
                    j = c - 4
                    P.copy(KD[2 * l + j][:, 128:128 + n], ps[:, 0:n], eng="dve")
                    P.copy(KO32[j * 64:(j + 1) * 64, 0:n].re("p (t b) -> p b t", t=4),
                           ps[j * 64:(j + 1) * 64, 0:n].re("p (b t) -> p b t", t=4), eng="dve")
                elif c == 6:
                    P.copy(VT[:, 0:n], ps[:, 0:n], eng="dve")
                    P.copy(VO32[:, 0:n].re("p (t b) -> p b t", t=4), ps[:, 0:n].re("p (b t) -> p b t", t=4), eng="dve")
                else:
                    rc = c - 7
                    P.act(ZRW[:, rc, 0:64], ps[:, 0:64], AF.Copy, scale=pv(l, "omu", rc))
                    zv = ZRW[:, rc, 0:64].re("p (b t) -> p b t", t=4)
                    pv_ = ps[:, 0:64].re("p (b t) -> p b t", t=4)
                    P.stt(zv[:, :, 1:4], pv_[:, :, 0:3], pv(l, "mu", rc), zv[:, :, 1:4], ALU.mult, ALU.add)
                    P.stt(zv[:, :, 0:1], ps[:, 64:80].re("p (b o) -> p b o", o=1), pv(l, "mu", rc), zv[:, :, 0:1], ALU.mult, ALU.add)
            in_proj(l, n, 80, evac)
            for src32, dst, nm in ((KO32, ks, "ks"), (VO32, vs, "vs")):
                pt = pd()
                P.tr(pt[0:64, 0:128], src32[:, 0:64], ident32)
                st = KVO[okv["i"] % 2]
                P.copy(st[0:64, :], pt[0:64, 0:128], eng="act")
                d3 = dst.ap()[l].rearrange("b (s f) -> b s f", f=128)
                for t_ in range(4):
                    P.dma(V(d3[:, 124 + t_, :], (nm,)), st[16 * t_:16 * t_ + 16, :], "kvo%d" % (okv["i"] % 2))
                okv["i"] += 1
            pt = ph()
            ptb = pt.bc(BF16)
            P.tr(ptb[0:64, 0:128], VT[:, 0:64], ident)
            for j in range(2):
                P.copy(VDS[:, j], ptb[0:64, j * 64:(j + 1) * 64].bcast(1, 2), eng="dve")
            for j in range(2):
                etp, etc = ET[2], ET[0]
                pa = bank(3)
                for b in range(16):
                    P.mm(pa[:, b * 16:b * 16 + 16], CK4[:, b, j, :], QBP[:, j, b].re("p c h t -> p (c h t)"), skip=True)
                P.act(etp[:, 0:256], pa[:, 0:256], AF.Exp, scale=0.125)
                ebp = V(EB.ap[:, 1, 4 * j:4 * j + 4, 0:4].unsqueeze(1).to_broadcast([128, 16, 4, 4]), EB.keys)
                e4 = etp[:, 0:256].re("p (b n t) -> p b n t", b=16, n=4)
                P.tt(e4, e4, ebp, ALU.mult, eng="pool")
                pb = bank(2)
                P.mm(pb[0:64, 0:256], KD[2 * l + j][:, 128:192], QBP[:, j].re("p b c h t -> p (b c h t)"))
                P.act(etc[0:64, 0:256], pb[0:64, 0:256], AF.Exp, scale=0.125)
                e3 = etc[0:64, 0:256].re("p (b n t) -> p b n t", b=16, n=4)
                P.tt(e3, e3, EBSC[:, j], ALU.mult, eng="pool")
                po = bank(4)
                pdn = bank(5)
                vds = VDS[:, j].re("p a d -> p (a d)")
                P.mm(po[:, 0:256], vds[0:32], etc[0:32, 0:256], start=True, stop=False, skip=True)
                for b in range(16):
                    P.mm(po[:, b * 16:b * 16 + 16], CV4[:, b, j, :], etp[:, b * 16:b * 16 + 16], start=False, stop=False, skip=True)
                P.mm(po[:, 0:256], vds[32:64], etc[32:64, 0:256], start=False, stop=True, skip=True)
                P.mm(pdn[:, 0:256], ones[0:64, :], etc[0:64, 0:256], start=True, stop=False)
                P.mm(pdn[:, 0:256], ones, etp[:, 0:256], start=False, stop=True)
                P.tt(RD[:, 0:256].re("p (b n t) -> p b n t", b=16, n=4), pdn[:, 0:256].re("p (b n t) -> p b n t", b=16, n=4),
                     V(ES.ap[:, 8 * l + 4 * j:8 * l + 4 * j + 4].unsqueeze(1).unsqueeze(3).to_broadcast([128, 16, 4, 4]), ES.keys), ALU.add)
                P.act(RD[:, 0:256], RD[:, 0:256], AF.Ln)
                P.act(RD[:, 0:256], RD[:, 0:256], AF.Exp, scale=-1.0)
                for hh in range(2):
                    ps_ = slice(hh * 64, (hh + 1) * 64)
                    src = po[ps_, 0:256].re("p (b c h t) -> p c h b t", b=16, c=2, h=2)[:, :, hh]
                    rdv = RD[ps_, 0:256].re("p (b c h t) -> p c h b t", b=16, c=2, h=2)[:, :, hh]
                    P.tt(MIX[ps_, 2 * j:2 * j + 2, 0:64].re("p c (b t) -> p c b t", t=4), src, rdv, ALU.mult)
            cs64 = slice(0, 64)
            rwkv_gates(l, lambda rc: ZRW[:, rc, cs64], 64)
            zoff = {"r": 0, "k": 4, "v": 8}
            for _ in elem_gen(l, lambda kind: ZRW[:, zoff[kind]:zoff[kind] + 4, 0:64],
                              SG.k(0)[:, :, 0:64], AA.k(0)[:, :, 0:64], 64, 4, 0, M4):
                pass
            pc_all = PCt2[0].re("p (a b) -> p a b", a=4)
            ot4 = OT[:, 0:256].re("p (a m) -> p a m", a=4)

            def load_state(b):
                P.dma(HS32[b % 4], dv(hbd.ap()[l, b].rearrange("p (a v) -> p a v", a=4), "hbd"), "hs%d" % (b % 4))

            def ph2S(b):
                hb_old, hb_new = HBF4[b % 2], HBF4[1 - b % 2]
                P.copy(hb_old, HS32[b % 4], eng="dve")
                for v_ in ph2_gen(l, 4, b % 2, b % 3, pc_all[:, :, b], HS32[b % 4], hb_old, hb_new,
                                  ot4[:, :, 4 * b:4 * b + 4]):
                    yield v_
                P.dma(dv(wkvs.ap()[l, b].rearrange("p (a v) -> p a v", a=4), "wkvs"), HS32[b % 4], "hs%d" % (b % 4))
                if b + 3 < 16:
                    load_state(b + 3)
                yield
            for b in range(3):
                load_state(b)
            zsel = lambda kind: ZRW[:, zoff[kind]:zoff[kind] + 4, 0:64]
            for t in range(16 + 2):
                run_streams(ph2S(t - 2) if 0 <= t - 2 < 16 else None,
                            ph1_gen((t - 1) % 2, (t - 1) % 3) if 0 <= t - 1 < 16 else None,
                            bd_gen(64, 4, slice(4 * t, 4 * t + 4), t % 3, zsel) if t < 16 else None)
            bon4 = BONb[0][:, 0:256].re("p (a m) -> p a m", a=4)
            for _ in gn_gen(l, 64, bon4, GATE.k(0)[:, :, 0:64], MIX[:, 4:8, 0:64]):
                pass
            if debug and l == 0:
                P.dma(dv(dbg_mixs.ap(), "dbg_mixs"), MIX[:, :, 0:64], "dbgms")
            out_ffn(l, n)
        rmsnorm(X[:, :, 0:n], n, "nf", 0, YO[:, :, 0:n])
        P.dma(dv(ysT.ap().rearrange("(kc p) t -> p kc t", p=128), "ysT"), YO[:, :, 0:n], "yo")

    def prompt_tile(s, ti):
        n = TT
        last = (ti == 2048 // TT - 1)
        cols = slice(ti * TT, (ti + 1) * TT)
        P.dma(X, dv(xT.ap()[s].rearrange("(kc p) t -> p kc t", p=128)[:, :, cols], "xT"), "x")
        for l in range(nlayers):
            if last:
                rmsnorm(X, n, "n1", l, HB, f32_cols=([n - 1], HL32))
                P.dma(dv(shp.ap()[l, s].rearrange("kc p -> p kc"), "shp"), HL32[:, :, 0:1].re("p k o -> p (k o)"), "hl",
                      allow_slow_non_contiguous=True)
            else:
                rmsnorm(X, n, "n1", l, HB)
            if ti == 0:
                P.memset(ZC[l], 0.0)
                P.memset(H32[l], 0.0)

            def evac(c, ps):
                if c < 4:
                    for hh in range(2):
                        ps_ = slice(hh * 64, (hh + 1) * 64)
                        P.copy(QBD[ps_, c // 2, :, c % 2, hh, :], ps[ps_, 0:n].re("p (b q) -> p b q", q=128), eng="act")
                elif c < 6:
                    j = c - 4
                    P.copy(KD[2 * l + j][:, 128:128 + n], ps[:, 0:n], eng="dve" if last else "act")
                    if last:
                        P.copy(KO32[j * 64:(j + 1) * 64, :], ps[j * 64:(j + 1) * 64, n - 128:n], eng="dve")
                elif c == 6:
                    P.copy(VT[:, 0:n], ps[:, 0:n], eng="dve" if last else "act")
                    if last:
                        P.copy(VO32, ps[:, n - 128:n], eng="dve")
                else:
                    rc = c - 7
                    P.act(ZRW[:, rc, 0:n], ps[:, 0:n], AF.Copy, scale=pv(l, "omu", rc))
                    P.stt(ZRW[:, rc, 1:n], ps[:, 0:n - 1], pv(l, "mu", rc), ZRW[:, rc, 1:n], ALU.mult, ALU.add)
                    P.stt(ZRW[:, rc, 0:1], ZC[l][:, rc:rc + 1], pv(l, "mu", rc), ZRW[:, rc, 0:1], ALU.mult, ALU.add)
                    P.copy(ZC[l][:, rc:rc + 1], ps[:, n - 1:n], eng="dve")
            in_proj(l, n, n, evac, blks=range(4))

            def att_gen():
                for v_ in in_proj_gen(l, n, n, evac, (4, 5), alloc=pba):
                    yield v_
                if last:
                    kv_out(KO32, 128, kp.ap()[l, s], "kp", alloc=pba)
                    yield
                    kv_out(VO32, 128, vp.ap()[l, s], "vp", alloc=pba)
                    yield
                for i in range(4):
                    gb = ti * 4 + i
                    pt = pba()
                    ptb = pt.bc(BF16)
                    P.tr(ptb[:, 0:128], VT[:, i * 128:(i + 1) * 128], ident)
                    for j in range(2):
                        P.copy(VD[l][:, gb % 5, j], ptb[:, j * 64:(j + 1) * 64].bcast(1, 2), eng="dve")
                    yield
                un = 0
                for i in range(4):
                    gb = ti * 4 + i
                    bc = slice(i * 128, (i + 1) * 128)
                    for j in range(2):
                        kd = KD[2 * l + j]
                        kcur = kd[:, 128 + i * 128:128 + (i + 1) * 128]
                        kprev = kd[:, i * 128:(i + 1) * 128]
                        vcur = VD[l][:, gb % 5, j].re("p a d -> p (a d)")
                        vprev = VD[l][:, (gb - 1) % 5, j].re("p a d -> p (a d)")
                        for c2 in range(2):
                            h0 = 4 * j + 2 * c2
                            E = ETS[un % 2]
                            rd = RD[:, (un % 2) * 256:(un % 2) * 256 + 256]
                            un += 1
                            qv = QBD[:, j, i, c2].re("p h q -> p (h q)")
                            W = 512 if gb > 0 else 256
                            pa = pba()
                            P.mm(pa[:, 0:256], kcur, qv)
                            if gb > 0:
                                P.mm(pa[:, 256:512], kprev, qv)
                            P.act(E[:, 0:W], pa[:, 0:W], AF.Exp, scale=0.125)
                            yield
                            P.tt(E[:, 0:256].re("p (h q) -> p h q", h=2), E[:, 0:256].re("p (h q) -> p h q", h=2),
                                 EB[:, 0, h0:h0 + 2, :], ALU.mult, eng="pool")
                            if gb > 0:
                                P.tt(E[:, 256:512].re("p (h q) -> p h q", h=2), E[:, 256:512].re("p (h q) -> p h q", h=2),
                                     EB[:, 1, h0:h0 + 2, :], ALU.mult, eng="pool")
                            yield
                            po = pba()
                            if gb > 0:
                                P.mm(po[:, 0:256], vprev, E[:, 256:512], start=True, stop=False)
                                P.mm(po[:, 0:256], vcur, E[:, 0:256], start=False, stop=True)
                                P.mm(po[:, 256:512], ones, E[:, 256:512], start=True, stop=False)
                                P.mm(po[:, 256:512], ones, E[:, 0:256], start=False, stop=True)
                            else:
                                P.mm(po[:, 0:256], vcur, E[:, 0:256])
                                P.mm(po[:, 256:512], ones, E[:, 0:256])
                            P.tt(rd.re("p (h q) -> p h q", h=2), po[:, 256:512].re("p (h q) -> p h q", h=2),
                                 ES[:, 8 * l + h0:8 * l + h0 + 2].bcast(2, 128), ALU.add)
                            yield
                            P.act(rd, rd, AF.Ln)
                            P.act(rd, rd, AF.Exp, scale=-1.0)
                            yield
                            for hh in range(2):
                                ps_ = slice(hh * 64, (hh + 1) * 64)
                                P.tt(MIXc(2 * j + c2, 2 * j + c2 + 1)[ps_, 0, bc], po[ps_, hh * 128:(hh + 1) * 128],
                                     rd[ps_, hh * 128:(hh + 1) * 128], ALU.mult)
                            yield
                for j in range(2):
                    kd = KD[2 * l + j]
                    P.copy(kd[:, 0:128], kd[:, n:n + 128], eng="pool")
                yield
            att = {"gen": att_gen(), "clock": 0.0, "done": False}
            P.copy(HBF4[0], H32[l], eng="dve")
            zoff = {"r": 0, "k": 4, "v": 8}
            for g in range(2):
                run_streams(rwkv_gates_gen(l, lambda rc, g=g: ZRW[:, rc, 256 * g:256 * g + 256], 256, o=256 * g), bg=att)

            def zs(c):
                return lambda kind: ZRW[:, zoff[kind]:zoff[kind] + 4, 64 * c:64 * c + 64]

            def prepS(c):
                cc = slice(64 * c, 64 * c + 64)
                for v_ in elem_gen(l, zs(c), SG.k(c // 4)[:, :, cc], AA.k(c // 4)[:, :, cc], 64, 64, c % 3, M64):
                    yield v_
                for v_ in bd_gen(64, 64, slice(0, 64), c % 3, zs(c)):
                    yield v_

            def ph2S(c):
                cc = slice(64 * c, 64 * c + 64)
                bon4 = BONb[c % 3][:, 0:256].re("p (a m) -> p a m", a=4)
                gn = lambda: gn_gen(l, 64, bon4, GATE.k(c // 4)[:, :, cc], MIXc(4, 8)[:, :, cc])
                return ph2_gen(l, 64, c % 2, c % 3, PCt2[c % 3][:, 0:4], H32[l], HBF4[c % 2], HBF4[1 - c % 2],
                               OT[:, 0:256].re("p (a m) -> p a m", a=4), gn=gn)
            NCH = n // 64
            for t in range(NCH + 2):
                run_streams(ph2S(t - 2) if 0 <= t - 2 < NCH else None,
                            ph1_gen((t - 1) % 2, (t - 1) % 3) if 0 <= t - 1 < NCH else None,
                            prepS(t) if t < NCH else None, bg=att)
            for _ in att["gen"]:
                pass
            if last:
                P.dma(dv(wkvp.ap()[l, s].rearrange("p (a v) -> p a v", a=4), "wkvp"), H32[l], "wkvp")
            if debug and s == 0 and ti == 0 and l == 0:
                P.dma(dv(dbg_mix.ap(), "dbg_mix"), MIX.re("p a t -> p (a t)"), "dbgm")
                P.dma(dv(dbg_z.ap(), "dbg_z"), ZRW.re("p a t -> p (a t)"), "dbgz")
                dbgs["on"] = True
            out_ffn(l, n)
            if debug and dbgs["on"]:
                P.dma(dv(dbg_x2.ap(), "dbg_x2"), X.re("p a t -> p (a t)"), "dbgx2")
                dbgs["on"] = False
        rmsnorm(X, n, "nf", 0, YO)
        P.dma(dv(yT.ap()[s].rearrange("(kc p) t -> p kc t", p=128)[:, :, cols], "yT"), YO, "yo")

    for t in range(n_ptiles):
        prompt_tile(t // (2048 // TT), t % (2048 // TT))
    if do_sample:
        sample_tile()
    P.emit()
    return nc, P


def _t5_bucket(d):
    n = np.maximum(d, 0)
    large = 16 + (np.log(np.maximum(n, 1) / 16) / np.log(128 / 16) * 16).astype(np.int32)
    large = np.minimum(large, 31)
    return np.where(n < 16, n, large).astype(np.int32)


def _constants():
    i = np.arange(128)
    hs, ss = i // 64, i % 64
    same = hs[:, None] == hs[None, :]
    ident = np.eye(128, dtype=np.float32)
    ones = np.ones((128, 128), np.float32)
    bones = same.astype(np.float32)
    MUs = (same & (ss[:, None] < ss[None, :])).astype(np.float32)
    MUi = (same & (ss[:, None] <= ss[None, :])).astype(np.float32)
    MLs = (same & (ss[:, None] > ss[None, :])).astype(np.float32)
    bd4 = np.zeros((128, 64), np.float32)
    j = np.arange(64)
    bd4[:64] = (j[:, None] // 4 == j[None, :] // 4).astype(np.float32)
    m64 = np.ones(256, np.float32)
    m64[::64] = 0
    m4 = np.ones(256, np.float32)
    m4[::4] = 0
    cbf = np.concatenate([ident, ones, bones, MUs, MUi, MLs, bd4, np.tile(m64[None], (128, 1)), np.tile(m4[None], (128, 1))], 1)
    c32 = np.concatenate([ident, ident[:, ::-1]], 1).astype(np.float32)
    ohv = np.zeros((33, 512), np.float32)
    for ty in range(2):
        for ii in range(256):
            if ty == 0:
                d = 127 - ii
                valid = (0 <= d <= 127)
            else:
                d = 255 - ii
                valid = (1 <= d <= 128)
            col = ty * 256 + (254 - ii if ii < 255 else 255)
            if valid:
                ohv[_t5_bucket(np.array(d))[()], col] = 1.0
            else:
                ohv[32, col] = 1.0
    return cbf, c32, ohv


_CACHE = {}


def kernel(x_prompt, x_sample, cache_k, cache_v, state_shift, state_wkv, norm1, w_in, w_out, sink,
           rel_table, mu, w0, w2, a0, a2, g2, k_k, k_a, r_k, gn_w, gn_b, norm2, w_gate, w_up,
           w_down, norm_f):
    f = lambda a: np.ascontiguousarray(np.asarray(a, dtype=np.float32))
    x_prompt, x_sample, cache_k, cache_v, state_shift, state_wkv = map(f, (x_prompt, x_sample, cache_k, cache_v, state_shift, state_wkv))
    norm1, w_in, w_out, sink, rel_table, mu, w0, w2, a0, a2, g2 = map(f, (norm1, w_in, w_out, sink, rel_table, mu, w0, w2, a0, a2, g2))
    k_k, k_a, r_k, gn_w, gn_b, norm2, w_gate, w_up, w_down, norm_f = map(f, (k_k, k_a, r_k, gn_w, gn_b, norm2, w_gate, w_up, w_down, norm_f))
    nc = _CACHE.get("nc_override")
    if nc is None:
        if "nc" not in _CACHE:
            _CACHE["nc"] = build_program()[0]
        nc = _CACHE["nc"]
    cbf, c32, ohv = _constants()
    q_, k_, v_, rw_ = w_in[:, :, 0:512], w_in[:, :, 512:640], w_in[:, :, 640:768], w_in[:, :, 768:]
    w_in_aug = np.concatenate([rw_, q_, k_[:, :, 0:64], k_[:, :, 0:64], k_[:, :, 64:128], k_[:, :, 64:128], v_], 2)
    w_gu = np.stack([w_gate.reshape(2, 1024, 22, 128), w_up.reshape(2, 1024, 22, 128)], 3).reshape(2, 1024, 5632)
    w_dn = w_down.reshape(2, 22, 128, 8, 128).transpose(0, 3, 2, 1, 4).reshape(2, 8, 128, 2816)
    lora = np.zeros((2, 128, 1536), np.float32)
    lora[:, 0:64, 0:512] = w2
    lora[:, 64:128, 512:1024] = a2
    lora[:, :, 1024:1536] = g2
    pvec = np.zeros((128, NPV), np.float32)

    def cols(vec, nchunk):
        return vec.reshape(nchunk, 128).T

    for l in range(2):
        b0 = 58 * l
        pvec[:, b0:b0 + 8] = cols(norm1[l], 8)
        pvec[:, b0 + 8:b0 + 16] = cols(norm2[l], 8)
        pvec[:, b0 + 16:b0 + 30] = cols(mu[l], 14)
        for nm, off in ((w0, 30), (a0, 34), (k_k, 38), (k_a, 42), (r_k, 46), (gn_w, 50), (gn_b, 54)):
            pvec[:, b0 + off:b0 + off + 4] = cols(nm[l], 4)
    pvec[:, 116:124] = cols(norm_f, 8)
    shared = dict(w_in=np.ascontiguousarray(w_in_aug), w_out=w_out, w_gu=np.ascontiguousarray(w_gu),
                  w_dn=np.ascontiguousarray(w_dn), lora=lora, pvec=pvec, rel=rel_table,
                  sink=np.ascontiguousarray(sink.reshape(1, 16)), cbf=cbf, c32=c32, ohv=ohv)
    in_maps = []
    for c in range(NCORES):
        xs = x_sample[16 * c:16 * c + 16]
        ck = cache_k[:, 16 * c:16 * c + 16]
        cv = cache_v[:, 16 * c:16 * c + 16]
        ckT = np.repeat(ck.transpose(0, 1, 3, 4, 2)[:, :, :, None], 2, 3)
        ckT = ckT.transpose(0, 3, 4, 1, 2, 5).reshape(2, 128, 4096)
        cvdd = np.repeat(cv[:, :, :, :, None], 2, 4)
        cvdd = cvdd.transpose(0, 2, 1, 3, 4, 5).reshape(2, 128, 4096)
        sw = state_wkv[:, 16 * c:16 * c + 16]
        hb = np.zeros((2, 16, 4, 2, 64, 2, 64), np.float32)
        swp = sw.reshape(2, 16, 4, 2, 64, 64)
        for hh in range(2):
            hb[:, :, :, hh, :, hh, :] = swp[:, :, :, hh].transpose(0, 1, 2, 4, 3)
        hb = hb.transpose(0, 1, 3, 4, 2, 5, 6).reshape(2, 16, 128, 512)
        m = dict(shared)
        m.update(xT=np.ascontiguousarray(x_prompt[2 * c:2 * c + 2].transpose(0, 2, 1)),
                 xsT=np.ascontiguousarray(xs.reshape(64, 1024).T),
                 shT=np.ascontiguousarray(state_shift[:, 16 * c:16 * c + 16].transpose(0, 2, 1)),
                 ckT=np.ascontiguousarray(ckT), cvd=np.ascontiguousarray(cvdd),
                 ckr=np.ascontiguousarray(ck.reshape(2, 16, 128 * 128)), cvr=np.ascontiguousarray(cv.reshape(2, 16, 128 * 128)),
                 hbd=np.ascontiguousarray(hb))
        in_maps.append(m)
    res = run_bass_kernel_spmd(nc, in_maps, core_ids=list(range(NCORES)))
    R = res.results
    y_prompt = np.concatenate([np.asarray(r["yT"]).transpose(0, 2, 1) for r in R], 0)
    y_sample = np.concatenate([np.asarray(r["ysT"]).T.reshape(16, 4, 1024) for r in R], 0)
    k_prompt = np.concatenate([np.asarray(r["kp"]).reshape(2, 2, 128, 2, 64) for r in R], 1)
    v_prompt = np.concatenate([np.asarray(r["vp"]).reshape(2, 2, 128, 2, 64) for r in R], 1)
    shift_prompt = np.concatenate([np.asarray(r["shp"]).reshape(2, 2, 1024) for r in R], 1)

    def unbd(a, nb):
        a = np.asarray(a).reshape(2, nb, 2, 64, 4, 2, 64)
        out = np.zeros((2, nb, 4, 2, 64, 64), np.float32)
        for hh in range(2):
            out[:, :, :, hh] = a[:, :, hh, :, :, hh, :].transpose(0, 1, 3, 4, 2)
        return out.reshape(2, nb, 8, 64, 64)
    wkv_prompt = np.concatenate([unbd(r["wkvp"], 2) for r in R], 1)
    k_sample = np.concatenate([np.asarray(r["ks"]).reshape(2, 16, 128, 2, 64) for r in R], 1)
    v_sample = np.concatenate([np.asarray(r["vs"]).reshape(2, 16, 128, 2, 64) for r in R], 1)
    shift_sample = np.concatenate([np.asarray(r["shs"]).transpose(0, 2, 1) for r in R], 1)
    wkv_sample = np.concatenate([unbd(r["wkvs"], 16) for r in R], 1)
    outs = (y_prompt, y_sample, k_prompt, v_prompt, shift_prompt, wkv_prompt, k_sample, v_sample, shift_sample, wkv_sample)
    return tuple(np.ascontiguousarray(o, dtype=np.float32) for o in outs)
```

```python
import contextlib
import numpy as np
import concourse.bass as bass
import concourse.mybir as mybir
from concourse.bass_utils import run_bass_kernel_spmd

F32 = mybir.dt.float32
BF16 = mybir.dt.bfloat16
AF = mybir.ActivationFunctionType
ALU = mybir.AluOpType

NCORES = 8
TT = 512
C0 = 0.6065306597126334
NPV = 124
NSLOT = 4


class V:
    __slots__ = ("ap", "keys")

    def __init__(self, ap, keys):
        self.ap = ap
        if isinstance(keys, tuple):
            keys = [keys]
        self.keys = list(keys)

    def __getitem__(self, idx):
        return V(self.ap[idx], self.keys)

    def k(self, *suffix):
        return V(self.ap, [k + tuple(suffix) for k in self.keys])

    def re(self, s, **kw):
        return V(self.ap.rearrange(s, **kw), self.keys)

    def bc(self, dtype):
        return V(self.ap.bitcast(dtype), self.keys)

    def bcast(self, axis, n):
        a = self.ap.unsqueeze(axis)
        shp = list(a.shape)
        shp[axis] = n
        return V(a.to_broadcast(shp), self.keys)


def vjoin(ap, vs):
    ks = []
    for v in vs:
        ks += v.keys
    return V(ap, ks)


class Prog:
    ENGS = ("pe", "act", "dve", "pool", "sp")

    def __init__(self, nc):
        self.nc = nc
        self.ops = []
        self.lastw = {}
        self.readers = {}
        self.stack = contextlib.ExitStack()
        self.fin = []
        self.eng_free = {}

    COST = {"pe": 0.14, "act": 0.5, "dve": 0.6, "pool": 0.85, "sp": 0.1}

    @staticmethod
    def _n(v):
        n = 1
        for d in v.ap.shape[1:]:
            n *= d
        return n

    def _cost(self, eng, out):
        n = self._n(out)
        if eng == "pe":
            return 0.06 + n / 2000.0
        if eng == "act":
            return 0.25 + n * 0.00085
        if eng == "dve":
            return 0.12 + n * 0.0011
        return 0.2 + n * 0.0022

    def sb(self, name, shape, dtype):
        t = self.stack.enter_context(self.nc.sbuf_tensor(name, list(shape), dtype))
        return V(t[:], (name,))

    def ps(self, name, shape, dtype=F32):
        t = self.stack.enter_context(self.nc.psum_tensor(name, list(shape), dtype))
        return V(t[:], (name,))

    def add(self, eng, fn, reads=(), writes=(), slot=None, cost=None):
        i = len(self.ops)
        rk = []
        for r in reads:
            rk += r.keys
        wk = []
        for w in writes:
            wk += w.keys
        deps = set()
        for r in rk:
            if r in self.lastw:
                deps.add(self.lastw[r])
        for w in wk:
            if w in self.lastw:
                deps.add(self.lastw[w])
            rd = self.readers.get(w)
            if rd:
                deps.update(rd.values())
        self.ops.append(dict(eng=eng, fn=fn, deps=deps, slot=slot))
        t0 = self.eng_free.get(eng, 0.0)
        for d in deps:
            od = self.ops[d]
            t0 = max(t0, self.fin[d] + (0.05 if (od["eng"] == eng and od["slot"] is None) else 0.3))
        if slot is not None:
            self.fin.append(t0 + 2.0)
            self.eng_free[eng] = t0 + 0.1
        else:
            f = t0 + (cost if cost is not None else self.COST[eng])
            self.fin.append(f)
            self.eng_free[eng] = f
        wks = set(wk)
        for w in wk:
            self.lastw[w] = i
            self.readers[w] = {}
        who = eng if slot is None else ("dma", slot)
        for r in rk:
            if r not in wks:
                self.readers.setdefault(r, {})[who] = i
        return i

    def mm(self, out, lhsT, rhs, start=True, stop=True, skip=False):
        kw = dict(start=start, stop=stop)
        if skip:
            kw["skip_group_check"] = True
        return self.add("pe", lambda e: e.matmul(out.ap, lhsT.ap, rhs.ap, **kw),
                        [lhsT, rhs] + ([] if start else [out]), [out], cost=self._cost("pe", out))

    def tr(self, out, in_, ident):
        return self.add("pe", lambda e: e.transpose(out.ap, in_.ap, ident.ap), [in_, ident], [out])

    def act(self, out, in_, func, bias=None, scale=None):
        kw = {}
        rd = [in_]
        if bias is not None:
            kw["bias"] = bias.ap
            rd.append(bias)
        if scale is not None:
            if isinstance(scale, V):
                kw["scale"] = scale.ap
                rd.append(scale)
            else:
                kw["scale"] = scale
        return self.add("act", lambda e: e.activation(out.ap, in_.ap, func, **kw), rd, [out], cost=self._cost("act", out))

    def tt(self, out, a, b, op, eng="dve"):
        return self.add(eng, lambda e: e.tensor_tensor(out.ap, a.ap, b.ap, op), [a, b], [out], cost=self._cost(eng, out))

    def ts(self, out, a, s1, op0, s2=None, op1=None, eng="dve"):
        rd = [a]
        s1a = s1.ap if isinstance(s1, V) else s1
        s2a = s2.ap if isinstance(s2, V) else s2
        if isinstance(s1, V):
            rd.append(s1)
        if isinstance(s2, V):
            rd.append(s2)
        kw = {}
        if op1 is not None:
            kw["op1"] = op1
        return self.add(eng, lambda e: e.tensor_scalar(out.ap, a.ap, s1a, s2a, op0, **kw), rd, [out], cost=self._cost(eng, out))

    def stt(self, out, a, s, b, op0, op1, eng="dve"):
        rd = [a, b]
        sa = s.ap if isinstance(s, V) else s
        if isinstance(s, V):
            rd.append(s)
        return self.add("dve", lambda e: e.scalar_tensor_tensor(out.ap, a.ap, sa, b.ap, op0, op1), rd, [out], cost=self._cost("dve", out))

    def copy(self, out, in_, eng="dve"):
        if eng == "act":
            return self.add("act", lambda e: e.copy(out.ap, in_.ap), [in_], [out], cost=self._cost("act", out))
        return self.add(eng, lambda e: e.tensor_copy(out.ap, in_.ap), [in_], [out], cost=self._cost(eng, out))

    def recip(self, out, in_):
        return self.add("dve", lambda e: e.reciprocal(out.ap, in_.ap), [in_], [out])

    def memset(self, out, val, eng="pool"):
        return self.add(eng, lambda e: e.memset(out.ap, val), [], [out])

    def dma(self, out, in_, slot, eng="sp", **kw):
        return self.add(eng, lambda e: e.dma_start(out=out.ap, in_=in_.ap, **kw), [in_], [out], slot=slot)

    def emit(self):
        nc = self.nc
        ops = self.ops
        n = len(ops)
        has_dep = [False] * n
        for o in ops:
            for d in o["deps"]:
                has_dep[d] = True
        cnt = {e: 0 for e in self.ENGS}
        slotcnt = {}
        for i, o in enumerate(ops):
            if o["slot"] is not None:
                s = "slot:" + o["slot"]
                slotcnt[s] = slotcnt.get(s, 0) + 16
                o["tok"] = (s, slotcnt[s])
            elif has_dep[i]:
                cnt[o["eng"]] += 1
                o["tok"] = ("eng:" + o["eng"], cnt[o["eng"]])
            else:
                o["tok"] = None
        sems = {}
        for name in [("eng:" + e) for e in self.ENGS] + list(slotcnt.keys()):
            sems[name] = self.stack.enter_context(nc.semaphore(name.replace(":", "_")))
        per_eng = {e: [] for e in self.ENGS}
        for i, o in enumerate(ops):
            per_eng[o["eng"]].append(i)
        stats = {e: [0, 0] for e in self.ENGS}

        def run_engine(ename, e):
            waited = {}
            for i in per_eng[ename]:
                o = ops[i]
                need = {}
                for d in o["deps"]:
                    od = ops[d]
                    tok = od["tok"]
                    if tok is None:
                        continue
                    if od["slot"] is None and od["eng"] == ename and ename == "pe":
                        continue
                    s, v = tok
                    if waited.get(s, 0) >= v:
                        continue
                    if need.get(s, 0) < v:
                        need[s] = v
                for s, v in need.items():
                    e.wait_ge(sems[s], v)
                    waited[s] = v
                    stats[ename][1] += 1
                ins = o["fn"](e)
                stats[ename][0] += 1
                if o["tok"] is not None:
                    s, v = o["tok"]
                    ins.then_inc(sems[s], 16 if o["slot"] is not None else 1)
            if ename == "sp":
                for s, v in slotcnt.items():
                    if waited.get(s, 0) < v:
                        e.wait_ge(sems[s], v)

        with nc.Block() as block:
            @block.sync
            def _(e):
                run_engine("sp", e)

            @block.scalar
            def _(e):
                run_engine("act", e)

            @block.vector
            def _(e):
                run_engine("dve", e)

            @block.gpsimd
            def _(e):
                run_engine("pool", e)

            @block.tensor
            def _(e):
                run_engine("pe", e)
        self.stats = stats
        self.stack.close()


def build_program(n_ptiles=8, do_sample=True, nlayers=2, debug=False):
    nc = bass.Bass("TRN2", target_bir_lowering=False)
    P = Prog(nc)

    def din(name, shape):
        return nc.dram_tensor(name, list(shape), F32, kind="ExternalInput")

    def dout(name, shape):
        return nc.dram_tensor(name, list(shape), F32, kind="ExternalOutput")

    def DV(t, ap=None):
        return V(t.ap() if ap is None else ap, (t.name if hasattr(t, "name") else str(id(t)),))

    xT = din("xT", [2, 1024, 2048])
    xsT = din("xsT", [1024, 64])
    shT = din("shT", [2, 1024, 16])
    ckT = din("ckT", [2, 128, 4096])
    cvd = din("cvd", [2, 128, 4096])
    ckr = din("ckr", [2, 16, 128 * 128])
    cvr = din("cvr", [2, 16, 128 * 128])
    hbd = din("hbd", [2, 16, 128, 512])
    w_in = din("w_in", [2, 1024, 2688])
    w_out = din("w_out", [2, 1024, 1024])
    w_gu = din("w_gu", [2, 1024, 5632])
    w_dn = din("w_dn", [2, 8, 128, 2816])
    lora = din("lora", [2, 128, 1536])
    pvec = din("pvec", [128, NPV])
    rel = din("rel", [32, 8])
    sink = din("sink", [1, 16])
    cbf = din("cbf", [128, 1344])
    c32 = din("c32", [128, 256])
    ohv = din("ohv", [33, 512])

    yT = dout("yT", [2, 1024, 2048])
    ysT = dout("ysT", [1024, 64])
    kp = dout("kp", [2, 2, 128, 128])
    vp = dout("vp", [2, 2, 128, 128])
    shp = dout("shp", [2, 2, 8, 128])
    wkvp = dout("wkvp", [2, 2, 128, 512])
    ks = dout("ks", [2, 16, 128 * 128])
    vs = dout("vs", [2, 16, 128 * 128])
    shs = dout("shs", [2, 1024, 16])
    wkvs = dout("wkvs", [2, 16, 128, 512])
    gscr = nc.dram_tensor("gscr", [8, 512], F32, kind="Internal")
    if debug:
        dbg_mix = nc.dram_tensor("dbg_mix", [128, 8 * TT], BF16, kind="ExternalOutput")
        dbg_z = nc.dram_tensor("dbg_z", [128, 14 * (TT + 1)], BF16, kind="ExternalOutput")
        dbg_x1 = nc.dram_tensor("dbg_x1", [128, 8 * TT], F32, kind="ExternalOutput")
        dbg_x2 = nc.dram_tensor("dbg_x2", [128, 8 * TT], F32, kind="ExternalOutput")
        dbg_hf = nc.dram_tensor("dbg_hf", [128, 8 * TT], BF16, kind="ExternalOutput")
    if debug:
        dbg_mixs = nc.dram_tensor("dbg_mixs", [128, 8, 64], BF16, kind="ExternalOutput")
    dbgs = {"on": False}

    def dv(t, name):
        return V(t, (name,))

    CB = P.sb("CB", [128, 1344], BF16)
    P.dma(CB, dv(cbf.ap(), "cbf"), "cst", eng="pool")
    ident = CB[:, 0:128]
    ones = CB[:, 128:256]
    bones = CB[:, 256:384]
    MG = CB[:, 384:640]
    MLs = CB[:, 640:768]
    BD4 = CB[0:64, 768:832]
    C32 = P.sb("C32", [128, 256], F32)
    P.dma(C32, dv(c32.ap(), "c32"), "cst32")
    ident32 = C32[:, 0:128]
    JM = C32[:, 128:256]
    M64 = CB[:, 832:1088]
    M4 = CB[:, 1088:1344]
    PVt = P.sb("PVEC", [128, NPV + 8 + 28], F32)
    P.dma(PVt[:, 0:NPV], dv(pvec.ap(), "pvec"), "pv")
    for l in range(2):
        P.ts(PVt[:, NPV + 4 * l:NPV + 4 * l + 4], PVt[:, 58 * l + 42:58 * l + 46], -1.0, ALU.mult, 1.0, ALU.add)
        P.ts(PVt[:, NPV + 8 + 14 * l:NPV + 8 + 14 * l + 14], PVt[:, 58 * l + 16:58 * l + 30], -1.0, ALU.mult, 1.0, ALU.add)

    def pv(l, name, j=None):
        base = {"n1": 0, "n2": 8, "mu": 16, "w0": 30, "a0": 34, "kk": 38, "ka": 42, "rk": 46, "gw": 50, "gb": 54}
        if name == "nf":
            b0 = 116
        elif name == "omka":
            b0 = NPV + 4 * l
        elif name == "omu":
            b0 = NPV + 8 + 14 * l
        else:
            b0 = 58 * l + base[name]
        return PVt[:, b0 + j:b0 + j + 1]

    EPS6 = P.sb("EPS6", [128, 1], F32)
    P.memset(EPS6, 1e-6)
    GNE = P.sb("GNE", [128, 1], F32)
    P.memset(GNE, 64e-5)
    LR = P.sb("LR", [128, 2, 1536], BF16)
    P.dma(LR, dv(lora.ap().rearrange("l p n -> p l n"), "lora"), "lr", eng="pool")

    PSB = [P.ps("psb%d" % i, [128, 512]) for i in range(8)]

    def bank(i):
        return V(PSB[i].ap, [("psb%d" % i,)])

    def half(i, h):
        return V(PSB[i].ap[:, h * 256:(h + 1) * 256], [("psb%d" % i,)])

    rr = {"d": 0, "h": 0, "ev": 0}

    rr.update({"pp": 0, "p1": 0, "p2": 0})

    def pd():
        rr["d"] = (rr["d"] + 1) % 4
        return bank((0, 1, 6, 7)[rr["d"]])

    def pd_prep():
        return bank(0)

    def pb1():
        rr["p1"] = (rr["p1"] + 1) % 3
        return bank(5 + rr["p1"])

    def pb2():
        rr["p2"] ^= 1
        return bank(1 + rr["p2"])

    rr["pa"] = 0

    def pba():
        rr["pa"] ^= 1
        return bank(3 + rr["pa"])

    def ph():
        rr["h"] = (rr["h"] + 1) % 4
        return half(4 + rr["h"], 0)

    SCR = P.sb("SCR", [128, 22, TT], BF16)
    REL = P.sb("REL", [33, 8], F32)
    P.memset(REL[32:33, :], -30000.0)
    P.dma(REL[0:32, :], dv(rel.ap(), "rel"), "rel")
    OH = P.sb("OH", [33, 512], F32)
    P.dma(OH, dv(ohv.ap(), "ohv"), "oh")
    pg = pd()
    P.mm(pg[0:8, :], REL, OH)
    GS = P.sb("GS", [8, 512], F32)
    P.act(GS, pg[0:8, :], AF.Exp)
    P.dma(dv(gscr.ap(), "gscr"), GS, "gscr")
    EB = P.sb("EB", [128, 2, 8, 128], BF16)
    for ty in range(2):
        for h in range(8):
            i_ = ty * 8 + h
            st = V(SCR.ap[:, i_, :].bitcast(F32)[:, 0:128], [("SCR", i_)])
            src = V(bass.AP(gscr, h * 512 + ty * 256, [[1, 128], [1, 128]]), ("gscr",))
            P.dma(st, src, "ebf%d" % i_, eng="sp")
            pj = pd()
            P.mm(pj[:, 0:128], JM, st)
            P.copy(EB[:, ty, h, :], pj[:, 0:128], eng=("dve" if i_ % 2 else "act"))
    EBSC = P.sb("EBSC", [64, 2, 16, 4, 4], BF16)
    for j in range(2):
        P.tt(EBSC[:, j], EB[0:64, 0, 4 * j:4 * j + 4, 0:64].re("p n (b t) -> p b n t", t=4),
             V(BD4.ap.rearrange("p (b t) -> p b t", t=4).unsqueeze(2).to_broadcast([64, 16, 4, 4]), BD4.keys), ALU.mult)
    ES = P.sb("ES", [128, 16], F32)
    P.dma(ES, V(bass.AP(sink, 0, [[0, 128], [1, 16]]), ("sink",)), "es")
    P.act(ES, ES, AF.Exp)

    X = P.sb("X", [128, 8, TT], F32)
    HB = P.sb("HB", [128, 8, TT], BF16)
    WR = [P.sb("WR%d" % i, [128, 4096], BF16) for i in range(NSLOT)]
    QBD = P.sb("QBD", [128, 2, TT // 128, 2, 2, 128], BF16)
    P.memset(QBD, 0.0)
    QBP = P.sb("QBP", [128, 2, 16, 2, 2, 4], BF16)
    P.memset(QBP, 0.0)
    KD = [P.sb("KD%d_%d" % (l, j), [128, 128 + TT], BF16) for l in range(2) for j in range(2)]
    VT = P.sb("VT", [128, TT], BF16)
    VD = [P.sb("VD%d" % l, [128, 5, 2, 2, 64], BF16) for l in range(2)]
    VDS = P.sb("VDS", [64, 2, 2, 64], BF16)
    KO32 = P.sb("KO32", [128, 128], F32)
    VO32 = P.sb("VO32", [128, 128], F32)
    KVO = [P.sb("KVO%d" % i, [128, 128], F32) for i in range(2)]
    ZRW = P.sb("ZRW", [128, 14, TT + 1], BF16)
    ZC = [P.sb("ZC%d" % l, [128, 14], BF16) for l in range(2)]
    MIX0 = P.sb("MIX", [128, 8, TT], BF16)
    MIX = V(MIX0.ap, [MIX0.keys[0] + (c,) for c in range(8)])

    def MIXc(c0, c1):
        return V(MIX0.ap[:, c0:c1], [MIX0.keys[0] + (c,) for c in range(c0, c1)])
    H32 = [P.sb("H32_%d" % l, [128, 4, 128], F32) for l in range(2)]
    _qf = QBD.ap.rearrange("p j b c h q -> p (j b c h q)").bitcast(F32)
    HS32 = [V(_qf[:, 512 * i:512 * (i + 1)].rearrange("p (a v) -> p a v", a=4), QBD.keys) for i in range(4)]
    HBF4 = [P.sb("HBF4_%d" % i, [128, 4, 128], BF16) for i in range(2)]
    RD = P.sb("RD", [128, 512], F32)
    RSTD = RD
    HL32 = P.sb("HL32", [128, 8, 16], F32)
    YO = vjoin(SCR.ap[:, 0:16, :].rearrange("p a t -> p (a t)").bitcast(F32).rearrange("p (k t) -> p k t", k=8),
               [SCR.k(i) for i in range(16)])
    GC = 256
    SG = P.sb("SG", [128, 4, TT], F32)
    AA = P.sb("AA", [128, 4, TT], BF16)
    GATE = P.sb("GATE", [128, 4, TT], BF16)
    Fb = {i: P.sb("F%d" % i, [128, GC], F32) for i in (2, 4, 5, 6, 7, 8, 9, 10, 11, 12, 13, 14)}
    Bb = [P.sb("B%d" % i, [128, GC], BF16) for i in range(6)]
    ET = [Fb[i].bc(BF16) for i in (8, 9, 10, 11)]
    ATb = P.sb("ATb", [128, 4, 2, 64], BF16)
    RT4 = [P.sb("RT4_%d" % i, [128, 4, 2, 64], BF16) for i in range(3)]
    BONb = [P.sb("BON%d" % i, [128, GC], F32) for i in range(3)]
    PCt2 = [P.sb("PCt%d" % i, [128, 64], F32) for i in range(3)]
    BTb = P.sb("BTb", [128, 4, 2, 64], BF16)
    KTb = P.sb("KTb", [128, 4, 2, 64], BF16)
    BHb = P.sb("BHb", [128, 4, 2, 64], BF16)
    KHb = P.sb("KHb", [128, 4, 2, 64], BF16)
    VVb = P.sb("VVb", [128, 4, 2, 64], BF16)
    for t_ in (ATb, RT4[0], RT4[1], RT4[2], BTb, KTb, BHb, KHb, VVb):
        P.memset(t_, 0.0)
    def scrv(c0, n_):
        return vjoin(SCR.ap[:, c0:c0 + n_, :].rearrange("p a t -> p (a t)"), [SCR.k(i) for i in range(c0, c0 + n_)])
    TM4s = [[scrv(8 + c, 1).re("p (q t) -> p q t", q=4) for c in range(4)],
            [scrv(c, 1).re("p (q t) -> p q t", q=4) for c in range(4)]]
    TM4pairs = [[scrv(8, 2), scrv(10, 2)], [scrv(0, 2), scrv(2, 2)]]
    NT4 = [scrv(12, 1).re("p (c t) -> p c t", c=4), scrv(13, 1).re("p (c t) -> p c t", c=4)]
    NN4 = [scrv(14, 1).re("p (c t) -> p c t", c=4), scrv(15, 1).re("p (c t) -> p c t", c=4)]
    S4 = [scrv(16, 1).re("p (c t) -> p c t", c=4), scrv(17, 1).re("p (c t) -> p c t", c=4)]
    ETS = [scrv(18, 1), scrv(19, 1)]
    N4 = P.sb("N4", [128, 4, 128], BF16)
    AAK4 = P.sb("AAK4", [128, 4, 128], BF16)
    W14 = P.sb("W14", [128, 4, 128], BF16)
    MRB4s = [P.sb("MRB4", [128, 4, 128], BF16), scrv(4, 1).re("p (c t) -> p c t", c=4)]
    MRK4s = [P.sb("MRK4", [128, 4, 128], BF16), scrv(5, 1).re("p (c t) -> p c t", c=4)]
    APT4s = [P.sb("APT4", [128, 4, 128], BF16), scrv(6, 1).re("p (c t) -> p c t", c=4)]
    UP4s = [P.sb("UP4", [128, 4, 128], BF16), scrv(7, 1).re("p (c t) -> p c t", c=4)]
    TMPB = P.sb("TMPB", [128, 4, 128], BF16)
    U4 = P.sb("U4", [128, 4, 128], BF16)
    TMPH = P.sb("TMPH", [128, 4, 128], F32)
    MUs_b = V(CB.ap[:, 384:512].unsqueeze(1).to_broadcast([128, 4, 128]), CB.keys)
    MUi_b = V(CB.ap[:, 512:640].unsqueeze(1).to_broadcast([128, 4, 128]), CB.keys)
    MLs_b = V(CB.ap[:, 640:768].unsqueeze(1).to_broadcast([128, 4, 128]), CB.keys)
    ID_b = V(CB.ap[:, 0:128].unsqueeze(1).to_broadcast([128, 4, 128]), CB.keys)
    OT = P.sb("OT", [128, GC], F32)
    CPB = vjoin(SCR.ap[:, 16:20, :].rearrange("p a t -> p (a t)").bitcast(F32).rearrange("p (b f) -> p b f", b=8),
                [SCR.k(i) for i in range(16, 20)])

    wblocks = []

    def layer_blocks(l):
        r = []
        wv = w_in.ap()[l].rearrange("(kc p) n -> p kc n", p=128)
        for b in range(6):
            c0, c1 = 512 * b, min(512 * (b + 1), 2688)
            r.append((wv[:, :, c0:c1], "k8", c1 - c0))
        wv = w_out.ap()[l].rearrange("(kc p) n -> p kc n", p=128)
        for b in range(2):
            r.append((wv[:, :, 512 * b:512 * (b + 1)], "k8", 512))
        wv = w_gu.ap()[l].rearrange("(kc p) n -> p kc n", p=128)
        for b in range(11):
            r.append((wv[:, :, 512 * b:512 * (b + 1)], "k8", 512))
        for oc in range(8):
            r.append((w_dn.ap()[l, oc], "flat", 2816))
        return r

    ntiles = n_ptiles + (1 if do_sample else 0)
    for _t in range(ntiles):
        for l in range(nlayers):
            wblocks.extend(layer_blocks(l))
    wst = {"issued": 0, "next": 0}

    def wslot_view(i):
        src, kind, w = wblocks[i]
        s = WR[i % NSLOT]
        if kind == "k8":
            return s.re("p (kc n) -> p kc n", kc=8)[:, :, 0:w]
        return s[:, 0:w]

    nbl = 27 * nlayers
    wscr = nc.dram_tensor("wscr", [nbl, 128, 4096], BF16, kind="Internal")

    def wget():
        i = wst["next"]
        wst["next"] += 1
        while wst["issued"] < min(len(wblocks), i + NSLOT):
            j = wst["issued"]
            sl = j % NSLOT
            if j < nbl:
                P.dma(wslot_view(j), V(wblocks[j][0], ("wdram",)), "w%d" % sl, eng="pool")
                if ntiles > 1:
                    P.dma(V(wscr.ap()[j], ("wscr", j)), WR[sl], "wb%d" % sl, eng="sp")
            else:
                P.dma(WR[sl], V(wscr.ap()[j % nbl], ("wscr", j % nbl)), "wh%d" % sl, eng="sp")
            wst["issued"] += 1
        return wslot_view(i)

    def rmsnorm(xv, n, gname, l, outv, f32_cols=None):
        sq = SCR[:, 0:8, 0:n]
        for kc in range(8):
            P.act(SCR[:, kc, 0:n].k(kc), xv[:, kc, :], AF.Square)
        pn = pd()
        for kc in range(8):
            P.mm(pn[:, 0:n], ones, SCR[:, kc, 0:n].k(kc), start=(kc == 0), stop=(kc == 7))
        P.act(RSTD[:, 0:n], pn[:, 0:n], AF.Ln, bias=EPS6, scale=1.0 / 1024)
        P.act(RSTD[:, 0:n], RSTD[:, 0:n], AF.Exp, scale=-0.5)
        for kc in range(8):
            P.stt(outv[:, kc, :], xv[:, kc, :], pv(l, gname, kc), RSTD[:, 0:n], ALU.mult, ALU.mult,
                  eng=("dve" if kc % 2 == 0 else "pool"))
        if f32_cols is not None:
            cols, dst = f32_cols
            for ci, c in enumerate(cols):
                P.stt(dst[:, :, ci:ci + 1], xv[:, :, c:c + 1], 1.0, RSTD[:, c:c + 1].bcast(1, 8), ALU.mult, ALU.mult)
            for kc in range(8):
                P.ts(dst[:, kc, 0:len(cols)], dst[:, kc, 0:len(cols)], pv(l, gname, kc), ALU.mult)

    def dense8(lhs_fn, rhs_v, n, alloc=None):
        ps = (alloc or pd)()
        for kc in range(8):
            P.mm(ps[:, 0:n], lhs_fn(kc), rhs_v[:, kc, 0:n], start=(kc == 0), stop=(kc == 7))
        return ps

    def attention_block(l, j, qcols, ncur, kcur, kprev, vcur, vprev, mixcols, ebc, ebp, sample=False):
        nq = ncur
        W4 = 4 * nq
        etc = ET[rr["ev"] % 2]
        etp = ET[2 + rr["ev"] % 2]
        rr["ev"] += 1
        pa = bank(2)
        P.mm(pa[0:ncur, 0:W4], kcur, QBD[:, j, qcols].re("p c h q -> p (c h q)"))
        P.act(etc[0:ncur, 0:W4], pa[0:ncur, 0:W4], AF.Exp, scale=0.125)
        P.tt(etc[0:ncur, 0:W4].re("p (n q) -> p n q", n=4), etc[0:ncur, 0:W4].re("p (n q) -> p n q", n=4), ebc, ALU.mult, eng="pool")
        po = bank(4)
        pdn = bank(5)
        if kprev is not None:
            pb = bank(3)
            P.mm(pb[:, 0:W4], kprev, QBD[:, j, qcols].re("p c h q -> p (c h q)"))
            P.act(etp[:, 0:W4], pb[:, 0:W4], AF.Exp, scale=0.125)
            P.tt(etp[:, 0:W4].re("p (n q) -> p n q", n=4), etp[:, 0:W4].re("p (n q) -> p n q", n=4), ebp, ALU.mult, eng="pool")
            P.mm(po[:, 0:W4], vprev, etp[:, 0:W4], start=True, stop=False)
            P.mm(po[:, 0:W4], vcur, etc[0:ncur, 0:W4], start=False, stop=True)
            P.mm(pdn[:, 0:W4], ones, etp[:, 0:W4], start=True, stop=False)
            P.mm(pdn[:, 0:W4], ones[0:ncur, :], etc[0:ncur, 0:W4], start=False, stop=True)
        else:
            P.mm(po[:, 0:W4], vcur, etc[0:ncur, 0:W4])
            P.mm(pdn[:, 0:W4], ones[0:ncur, :], etc[0:ncur, 0:W4])
        finish_attn(l, j, po, pdn, nq, mixcols)

    def finish_attn(l, j, po, pdn, nq, mixcols):
        W4 = 4 * nq
        P.tt(RD[:, 0:W4].re("p (n q) -> p n q", n=4), pdn[:, 0:W4].re("p (n q) -> p n q", n=4),
             ES[:, 8 * l + 4 * j:8 * l + 4 * j + 4].bcast(2, nq), ALU.add)
        P.act(RD[:, 0:W4], RD[:, 0:W4], AF.Ln)
        P.act(RD[:, 0:W4], RD[:, 0:W4], AF.Exp, scale=-1.0)
        for hh in range(2):
            ps_ = slice(hh * 64, (hh + 1) * 64)
            src = po[ps_, 0:W4].re("p (c h q) -> p c h q", c=2, h=2)[:, :, hh, :]
            rdv = RD[ps_, 0:W4].re("p (c h q) -> p c h q", c=2, h=2)[:, :, hh, :]
            P.tt(MIX[ps_, 2 * j:2 * j + 2, mixcols], src, rdv, ALU.mult)

    def pbank():
        rr["h"] = (rr["h"] + 1) % 6
        return bank(2 + rr["h"])

    def c4(b_):
        return b_.re("p (c t) -> p c t", c=4)

    def F4(i, m):
        return Fb[i][:, 0:4 * m].re("p (a m) -> p a m", a=4)

    def B4(i, m):
        return Bb[i][:, 0:4 * m].re("p (a m) -> p a m", a=4)

    def pv4(l, name, m):
        base = {"mur": 16, "muk": 20, "muv": 24, "kk": 38, "ka": 42, "rk": 46, "gw": 50, "gb": 54}
        b0 = (NPV + 4 * l) if name == "omka" else 58 * l + base[name]
        return V(PVt.ap[:, b0:b0 + 4].unsqueeze(2).to_broadcast([128, 4, m]), PVt.keys)

    def run_streams(*gens, bg=None):
        gens = [g for g in gens if g is not None]
        nfg = len(gens)
        if bg is not None and not bg["done"]:
            gens.append(bg["gen"])
        ng = len(gens)
        done = [False] * ng
        hold = [False] * ng
        started = [False] * ng
        clock = [0.0] * ng
        if ng > nfg:
            clock[nfg] = bg["clock"]
        nread = 0
        while not all(done[:nfg]):
            cand = [i for i in range(ng) if not done[i] and not (hold[i] and nread > 0)]
            assert cand, "stream scheduler stuck"
            i = min(cand, key=lambda j: (clock[j], j))
            hold[i] = False
            n0 = len(P.ops)
            try:
                v = next(gens[i])
            except StopIteration:
                done[i] = True
                if i >= nfg:
                    bg["done"] = True
                if started[i]:
                    nread -= 1
                    started[i] = False
                continue
            if len(P.ops) > n0:
                clock[i] = max(P.fin[n0:])
                if i >= nfg:
                    bg["clock"] = clock[i]
            if v == "GSTART":
                nread += 1
                started[i] = True
            elif v == "GDONE":
                nread -= 1
                started[i] = False
            elif v == "BD":
                hold[i] = True

    def rwkv_gates(l, zc_fn, n, o=0):
        zwa = zc_fn(12)
        th = Bb[0][:, 0:n]
        zab = zwa
        P.act(th, zwa, AF.Tanh)
        for j in range(4):
            p1 = pd()
            P.mm(p1[:, 0:n], LR[:, l, j * 128:(j + 1) * 128], th)
            P.act(SG[:, j, o:o + n], p1[:, 0:n], AF.Sigmoid, bias=pv(l, "w0", j))
            p2 = pd()
            P.mm(p2[:, 0:n], LR[:, l, 512 + j * 128:512 + (j + 1) * 128], zab)
            P.act(AA[:, j, o:o + n], p2[:, 0:n], AF.Sigmoid, bias=pv(l, "a0", j))
        zg = zc_fn(13)
        sgz = Bb[1][:, 0:n]
        P.act(sgz, zg, AF.Sigmoid)
        for j in range(4):
            p1 = pd()
            P.mm(p1[:, 0:n], LR[:, l, 1024 + j * 128:1024 + (j + 1) * 128], sgz)
            P.copy(GATE[:, j, o:o + n], p1[:, 0:n], eng="act")

    def elem_gen(l, z4, sg4, aa4, m, CL, bs, msk):
        G = m // CL

        def g3(v):
            return v.re("p a (g t) -> p (a g) t", t=CL)
        R, Kraw, Vv, K = z4("r"), z4("k"), z4("v"), F4(2, m)
        KAP, T2, BETA = F4(4, m), F4(5, m), F4(6, m)
        BON = BONb[bs][:, 0:4 * m].re("p (a m) -> p a m", a=4)
        CS, CSX, CSH, ENEG, EPOS = F4(8, m), F4(9, m), F4(10, m), F4(11, m), F4(12, m)
        sqk = B4(2, m)
        P.tt(KAP, Kraw, pv4(l, "kk", m), ALU.mult)
        P.act(sqk, KAP, AF.Square)
        yield
        p1 = pd_prep()
        P.mm(p1[:, 0:4 * m], bones, sqk.re("p a m -> p (a m)"))
        P.ts(T2.re("p a m -> p (a m)"), p1[:, 0:4 * m], 1e-24, ALU.max)
        yield
        P.act(T2, T2, AF.Ln)
        P.act(T2, T2, AF.Exp, scale=-0.5)
        yield
        P.tt(KAP, KAP, T2, ALU.mult)
        P.tt(T2, aa4, pv4(l, "ka", m), ALU.mult, eng="pool")
        yield
        P.tt(T2, T2, pv4(l, "omka", m), ALU.add, eng="pool")
        P.tt(K, Kraw, T2, ALU.mult)
        yield
        P.stt(BETA, KAP, -1.0, aa4, ALU.mult, ALU.mult)
        rk = B4(3, m)
        P.tt(T2, R, pv4(l, "rk", m), ALU.mult, eng="pool")
        yield
        P.tt(rk, T2, K, ALU.mult)
        p2 = pd_prep()
        P.mm(p2[:, 0:4 * m], bones, rk.re("p a m -> p (a m)"))
        yield
        P.tt(BON, p2[:, 0:4 * m].re("p (a m) -> p a m", a=4), Vv, ALU.mult)
        SGc = F4(7, m)
        P.copy(SGc, sg4, eng="act")
        yield
        csf, sgf, mkf = CS.re("p a m -> p (a m)"), SGc.re("p a m -> p (a m)"), msk[:, 0:4 * m]
        P.add("dve", lambda e, o=csf, m_=mkf, s_=sgf: e.tensor_tensor_scan(o.ap, m_.ap, s_.ap, 0.0, ALU.mult, ALU.add),
              [mkf, sgf], [csf])
        yield
        P.tt(CSX, CS, SGc, ALU.subtract, eng="pool")
        last = g3(CS)[:, :, CL - 1:CL].re("p g o -> p (g o)")
        P.tt(g3(CSH), last.bcast(2, CL), g3(CS), ALU.subtract, eng="pool")
        yield
        P.act(CSX, CSX, AF.Exp, scale=-C0)
        P.act(ENEG, CS, AF.Exp, scale=C0)
        yield
        P.act(EPOS, CS, AF.Exp, scale=-C0)
        P.act(CSH, CSH, AF.Exp, scale=-C0)
        P.act(PCt2[bs][:, 0:4 * G], last, AF.Exp, scale=-C0)
        yield

    def bd_gen(m, CL, cols, rs, z4):
        R, K, Vv = z4("r"), F4(2, m), z4("v")
        KAP, BETA = F4(4, m), F4(6, m)
        CSX, CSH, ENEG, EPOS = F4(9, m), F4(10, m), F4(11, m), F4(12, m)
        yield "BD"
        for hh in range(2):
            ps_ = slice(hh * 64, (hh + 1) * 64)
            eng = "dve" if hh == 0 else "pool"

            def sv(x):
                return x[ps_, :, cols]
            P.tt(ATb[ps_, :, hh, 0:CL], sv(KAP), sv(CSX), ALU.mult, eng=eng)
            P.tt(RT4[rs][ps_, :, hh, 0:CL], sv(R), sv(EPOS), ALU.mult, eng=eng)
            yield
            P.tt(BTb[ps_, :, hh, 0:CL], sv(BETA), sv(ENEG), ALU.mult, eng=eng)
            P.tt(KTb[ps_, :, hh, 0:CL], sv(K), sv(ENEG), ALU.mult, eng=eng)
            yield
            P.tt(BHb[ps_, :, hh, 0:CL], sv(BETA), sv(CSH), ALU.mult, eng=eng)
            P.tt(KHb[ps_, :, hh, 0:CL], sv(K), sv(CSH), ALU.mult, eng=eng)
            yield
            P.copy(VVb[ps_, :, hh, 0:CL], sv(Vv), eng=eng)
        yield

    def ph1_gen(k, rs):
        yield "GSTART"
        atv = [ATb[:, p].re("p h t -> p (h t)") for p in range(4)]
        rtv = [RT4[rs][:, p].re("p h t -> p (h t)") for p in range(4)]
        btv = [BTb[:, p].re("p h t -> p (h t)") for p in range(4)]
        ktv = [KTb[:, p].re("p h t -> p (h t)") for p in range(4)]
        for half_ in range(2):
            ptr = pb1()
            ptb = ptr.bc(BF16)
            for c_ in range(2):
                p = 2 * half_ + c_
                for qi, src in enumerate((atv[p], BHb[:, p].re("p h t -> p (h t)"), KHb[:, p].re("p h t -> p (h t)"),
                                          VVb[:, p].re("p h t -> p (h t)"))):
                    P.tr(ptb[:, (c_ * 4 + qi) * 128:(c_ * 4 + qi + 1) * 128], src, ident)
                yield
            P.copy(TM4pairs[k][half_], ptb, eng="act")
        def gprod(lf, rf):
            b_ = pb1()
            for p in range(4):
                P.mm(b_[:, p * 128:(p + 1) * 128], lf[p], rf[p])
            return b_
        bE = gprod(atv, btv)
        P.tt(NT4[0], c4(bE), MLs_b, ALU.mult)
        yield
        bA = gprod(btv, atv)
        P.tt(N4, c4(bA), MUs_b, ALU.mult)
        P.tt(S4[0], N4, ID_b, ALU.add, eng="pool")
        yield
        bC = gprod(ktv, atv)
        P.tt(AAK4, c4(bC), MUs_b, ALU.mult)
        yield
        bB = gprod(btv, rtv)
        P.tt(MRB4s[k], c4(bB), MUi_b, ALU.mult)
        yield
        bD = gprod(ktv, rtv)
        yield "GDONE"
        P.tt(MRK4s[k], c4(bD), MUi_b, ALU.mult)
        yield
        Nk, NTk, Sk = N4, NT4[0], S4[0]
        for lvl in range(5):
            NTn = NT4[(lvl + 1) % 2]
            q1 = pb1()
            for p in range(4):
                P.mm(q1[:, p * 128:(p + 1) * 128], Nk[:, p], NTk[:, p])
            P.copy(NTn, c4(q1), eng="act")
            yield
            if lvl < 4:
                Nn = NN4[lvl % 2]
                q2 = pb1()
                for p in range(4):
                    P.mm(q2[:, p * 128:(p + 1) * 128], NTk[:, p], Nk[:, p])
                P.copy(Nn, c4(q2), eng="act")
                yield
            Sn = S4[(lvl + 1) % 2]
            q3 = pb1()
            for p in range(4):
                P.mm(q3[:, p * 128:(p + 1) * 128], NTn[:, p], Sk[:, p])
            P.tt(Sn, c4(q3), Sk, ALU.add)
            yield
            NTk, Sk = NTn, Sn
            if lvl < 4:
                Nk = Nn
        q4 = pb1()
        q5 = pb1()
        for p in range(4):
            P.mm(q5[:, p * 128:(p + 1) * 128], AAK4[:, p], TM4s[k][p][:, 3])
        P.copy(W14, c4(q5), eng="act")
        yield
        for p in range(4):
            P.mm(q4[:, p * 128:(p + 1) * 128], TM4s[k][p][:, 0], Sk[:, p])
        P.copy(APT4s[k], c4(q4), eng="act")
        yield
        q6 = pb1()
        for p in range(4):
            P.mm(q6[:, p * 128:(p + 1) * 128], Sk[:, p], W14[:, p])
        P.copy(UP4s[k], c4(q6), eng="act")
        yield
    def ph2_gen(l, CL, k, rs, pc4, H32v, hb_old, hb_new, ot_dst, gn=None):
        rtv = [RT4[rs][:, p].re("p h t -> p (h t)") for p in range(4)]
        P.tt(TMPH, H32v, pc4.bcast(2, 128), ALU.mult, eng="pool")
        q7 = pb2()
        for p in range(4):
            P.mm(q7[:, p * 128:(p + 1) * 128], APT4s[k][:, p], hb_old[:, p])
        P.tt(U4, c4(q7), UP4s[k], ALU.add)
        yield
        q9 = pb2()
        for p in range(4):
            P.mm(q9[:, p * 128:(p + 1) * 128], TM4s[k][p][:, 2], TM4s[k][p][:, 3], start=True, stop=False)
            P.mm(q9[:, p * 128:(p + 1) * 128], TM4s[k][p][:, 1], U4[:, p], start=False, stop=True)
        yield
        P.tt(hb_new, TMPH, c4(q9), ALU.add)
        P.tt(H32v, TMPH, c4(q9), ALU.add)
        yield
        q8 = pb2()
        for p in range(4):
            cs_ = slice(p * 128, (p + 1) * 128)
            P.mm(q8[:, cs_], hb_old[:, p], rtv[p], start=True, stop=False)
            P.mm(q8[:, cs_], U4[:, p], MRB4s[k][:, p], start=False, stop=False)
            P.mm(q8[:, cs_], TM4s[k][p][:, 3], MRK4s[k][:, p], start=False, stop=True)
        yield
        for hh in range(2):
            ps_ = slice(hh * 64, (hh + 1) * 64)
            P.copy(ot_dst[ps_], c4(q8)[ps_, :, hh * 64:hh * 64 + CL], eng="act")
        yield
        if gn is not None:
            for _ in gn():
                yield

    def gn_gen(l, m, bon4, gate4, mixv):
        otv = OT[:, 0:4 * m]
        ob = Bb[4][:, 0:4 * m]
        P.copy(ob, otv, eng="act")
        p1 = pb2()
        P.mm(p1[:, 0:4 * m], bones, ob)
        yield
        CEN = Fb[13][:, 0:4 * m]
        P.stt(CEN, p1[:, 0:4 * m], -1.0 / 64, otv, ALU.mult, ALU.add)
        sq = Bb[5][:, 0:4 * m]
        P.act(sq, CEN, AF.Square)
        yield
        p2 = pb2()
        P.mm(p2[:, 0:4 * m], bones, sq)
        RS = Fb[14][:, 0:4 * m]
        P.act(RS, p2[:, 0:4 * m], AF.Ln, bias=GNE, scale=1.0 / 64)
        yield
        P.act(RS, RS, AF.Exp, scale=-0.5)
        P.tt(CEN, CEN, RS, ALU.mult)
        yield
        C3 = CEN.re("p (a m) -> p a m", a=4)
        P.tt(C3, C3, pv4(l, "gw", m), ALU.mult, eng="pool")
        P.tt(C3, C3, pv4(l, "gb", m), ALU.add, eng="pool")
        yield
        P.tt(C3, C3, bon4, ALU.add, eng="pool")
        P.tt(mixv, C3, gate4, ALU.mult)
        yield

    def out_ffn(l, n):
        for oc in range(8):
            if oc % 4 == 0:
                w = wget()
            o = (oc % 4) * 128
            ps = dense8(lambda kc: w[:, kc, o:o + 128], MIX, n)
            P.tt(X[:, oc, 0:n], ps[:, 0:n], X[:, oc, 0:n], ALU.add)
        if debug and dbgs["on"]:
            P.dma(dv(dbg_x1.ap(), "dbg_x1"), X.re("p a t -> p (a t)"), "dbgx1")
        rmsnorm(X[:, :, 0:n], n, "n2", l, HB[:, :, 0:n])
        if debug and dbgs["on"]:
            P.dma(dv(dbg_hf.ap(), "dbg_hf"), HB.re("p a t -> p (a t)"), "dbghf")
        for b in range(11):
            w = wget()
            for jj in range(2):
                j = 2 * b + jj
                pg_ = dense8(lambda kc: w[:, kc, jj * 256:jj * 256 + 128], HB, n)
                pu_ = dense8(lambda kc: w[:, kc, jj * 256 + 128:jj * 256 + 256], HB, n)
                sg_ = RD[:, 0:n]
                P.act(sg_, pg_[:, 0:n], AF.Silu)
                P.tt(SCR[:, j, 0:n].k(j), pu_[:, 0:n], sg_, ALU.mult)
        for oc in range(8):
            w = wget().re("p (j m) -> p j m", j=22)
            ps = pd()
            for j in range(22):
                P.mm(ps[:, 0:n], w[:, j, :], SCR[:, j, 0:n].k(j), start=(j == 0), stop=(j == 21))
            P.tt(X[:, oc, 0:n], ps[:, 0:n], X[:, oc, 0:n], ALU.add)

    def in_proj_gen(l, n, next_, evac, blks, alloc=None):
        for blk in blks:
            w = wget()
            for off in range(4):
                c = 4 * blk + off
                if c >= 21:
                    break
                role = c + 7 if c < 14 else c - 14
                ncol = n if role < 7 else next_
                ps = dense8(lambda kc: w[:, kc, off * 128:(off + 1) * 128], HB, ncol, alloc=alloc)
                evac(role, ps)
                yield

    def in_proj(l, n, next_, evac, blks=range(6)):
        for _ in in_proj_gen(l, n, next_, evac, blks):
            pass

    okv = {"i": 0}

    def kv_out(src32, ncols, dst_ap, name, alloc=None):
        pt = (alloc or pd)()
        P.tr(pt[0:ncols, 0:128], src32[:, 0:ncols], ident32)
        st = KVO[okv["i"] % 2]
        P.copy(st[0:ncols, :], pt[0:ncols, 0:128], eng="act")
        P.dma(V(dst_ap, (name,)), st[0:ncols, :], "kvo%d" % (okv["i"] % 2))
        okv["i"] += 1

    def sample_tile():
        n = 64
        for t_ in (ATb, RT4[0], RT4[1], RT4[2], BTb, KTb, BHb, KHb, VVb):
            P.memset(t_, 0.0)
        P.dma(X[:, :, 0:n], dv(xsT.ap().rearrange("(kc p) t -> p kc t", p=128), "xsT"), "x")
        for l in range(nlayers):
            rmsnorm(X[:, :, 0:n], n, "n1", l, HB[:, :, 0:n], f32_cols=([4 * b + 3 for b in range(16)], HL32))
            P.dma(dv(shs.ap()[l].rearrange("(kc p) b -> p kc b", p=128), "shs"), HL32, "hl")
            P.dma(HB[:, :, 64:80], dv(shT.ap()[l].rearrange("(kc p) b -> p kc b", p=128), "shT"), "sh", eng="pool")
            CK = vjoin(SCR.ap[:, 0:8, :].rearrange("p a t -> p (a t)"), [SCR.k(i) for i in range(8)])
            CVv = vjoin(SCR.ap[:, 8:16, :].rearrange("p a t -> p (a t)"), [SCR.k(i) for i in range(8, 16)])
            P.dma(CK, dv(ckT.ap()[l], "ckT"), "ck", eng="pool")
            P.dma(CVv, dv(cvd.ap()[l], "cvd"), "cv", eng="pool")
            CK4 = CK.re("p (b j s) -> p b j s", b=16, j=2)
            CV4 = CVv.re("p (b j s) -> p b j s", b=16, j=2)
            for src_, dst_, nm_ in ((ckr, ks, "ks"), (cvr, vs, "vs")):
                sv = src_.ap()[l].rearrange("b (s f) -> s b f", f=128)[4:128]
                dvv = dst_.ap()[l].rearrange("b (s f) -> s b f", f=128)[0:124]
                for hb_ in range(2):
                    P.dma(CPB[0:124], dv(sv[:, 8 * hb_:8 * hb_ + 8, :], nm_ + "src"), "cpb")
                    P.dma(dv(dvv[:, 8 * hb_:8 * hb_ + 8, :], nm_), CPB[0:124], "cpb")

            def evac(c, ps):
                if c < 4:
                    for hh in range(2):
                        ps_ = slice(hh * 64, (hh + 1) * 64)
                        P.copy(QBP[ps_, c // 2, :, c % 2, hh, :], ps[ps_, 0:n].re("p (b t) -> p b t", t=4), eng="act")
                elif c < 6:
                    j = c - 4
                    P.copy(KD[2 * l + j][:, 128:128 + n], ps[:, 0:n], eng="dve")
                    P.copy(KO32[j * 64:(j + 1) * 64, 0:n].re("p (t b) -> p b t", t=4),
                           ps[j * 64:(j + 1) * 64, 0:n].re("p (b t) -> p b t", t=4), eng="dve")
                elif c == 6:
                    P.copy(VT[:, 0:n], ps[:, 0:n], eng="dve")
                    P.copy(VO32[:, 0:n].re("p (t b) -> p b t", t=4), ps[:, 0:n].re("p (b t) -> p b t", t=4), eng="dve")
                else:
                    rc = c - 7
                    P.act(ZRW[:, rc, 0:64], ps[:, 0:64], AF.Copy, scale=pv(l, "omu", rc))
                    zv = ZRW[:, rc, 0:64].re("p (b t) -> p b t", t=4)
                    pv_ = ps[:, 0:64].re("p (b t) -> p b t", t=4)
                    P.stt(zv[:, :, 1:4], pv_[:, :, 0:3], pv(l, "mu", rc), zv[:, :, 1:4], ALU.mult, ALU.add)
                    P.stt(zv[:, :, 0:1], ps[:, 64:80].re("p (b o) -> p b o", o=1), pv(l, "mu", rc), zv[:, :, 0:1], ALU.mult, ALU.add)
            in_proj(l, n, 80, evac)
            for src32, dst, nm in ((KO32, ks, "ks"), (VO32, vs, "vs")):
                pt = pd()
                P.tr(pt[0:64, 0:128], src32[:, 0:64], ident32)
                st = KVO[okv["i"] % 2]
                P.copy(st[0:64, :], pt[0:64, 0:128], eng="act")
                d3 = dst.ap()[l].rearrange("b (s f) -> b s f", f=128)
                for t_ in range(4):
                    P.dma(V(d3[:, 124 + t_, :], (nm,)), st[16 * t_:16 * t_ + 16, :], "kvo%d" % (okv["i"] % 2))
                okv["i"] += 1
            pt = ph()
            ptb = pt.bc(BF16)
            P.tr(ptb[0:64, 0:128], VT[:, 0:64], ident)
            for j in range(2):
                P.copy(VDS[:, j], ptb[0:64, j * 64:(j + 1) * 64].bcast(1, 2), eng="dve")
            for j in range(2):
                etp, etc = ET[2], ET[0]
                pa = bank(3)
                for b in range(16):
                    P.mm(pa[:, b * 16:b * 16 + 16], CK4[:, b, j, :], QBP[:, j, b].re("p c h t -> p (c h t)"), skip=True)
                P.act(etp[:, 0:256], pa[:, 0:256], AF.Exp, scale=0.125)
                ebp = V(EB.ap[:, 1, 4 * j:4 * j + 4, 0:4].unsqueeze(1).to_broadcast([128, 16, 4, 4]), EB.keys)
                e4 = etp[:, 0:256].re("p (b n t) -> p b n t", b=16, n=4)
                P.tt(e4, e4, ebp, ALU.mult, eng="pool")
                pb = bank(2)
                P.mm(pb[0:64, 0:256], KD[2 * l + j][:, 128:192], QBP[:, j].re("p b c h t -> p (b c h t)"))
                P.act(etc[0:64, 0:256], pb[0:64, 0:256], AF.Exp, scale=0.125)
                e3 = etc[0:64, 0:256].re("p (b n t) -> p b n t", b=16, n=4)
                P.tt(e3, e3, EBSC[:, j], ALU.mult, eng="pool")
                po = bank(4)
                pdn = bank(5)
                vds = VDS[:, j].re("p a d -> p (a d)")
                P.mm(po[:, 0:256], vds[0:32], etc[0:32, 0:256], start=True, stop=False, skip=True)
                for b in range(16):
                    P.mm(po[:, b * 16:b * 16 + 16], CV4[:, b, j, :], etp[:, b * 16:b * 16 + 16], start=False, stop=False, skip=True)
                P.mm(po[:, 0:256], vds[32:64], etc[32:64, 0:256], start=False, stop=True, skip=True)
                P.mm(pdn[:, 0:256], ones[0:64, :], etc[0:64, 0:256], start=True, stop=False)
                P.mm(pdn[:, 0:256], ones, etp[:, 0:256], start=False, stop=True)
                P.tt(RD[:, 0:256].re("p (b n t) -> p b n t", b=16, n=4), pdn[:, 0:256].re("p (b n t) -> p b n t", b=16, n=4),
                     V(ES.ap[:, 8 * l + 4 * j:8 * l + 4 * j + 4].unsqueeze(1).unsqueeze(3).to_broadcast([128, 16, 4, 4]), ES.keys), ALU.add)
                P.act(RD[:, 0:256], RD[:, 0:256], AF.Ln)
                P.act(RD[:, 0:256], RD[:, 0:256], AF.Exp, scale=-1.0)
                for hh in range(2):
                    ps_ = slice(hh * 64, (hh + 1) * 64)
                    src = po[ps_, 0:256].re("p (b c h t) -> p c h b t", b=16, c=2, h=2)[:, :, hh]
                    rdv = RD[ps_, 0:256].re("p (b c h t) -> p c h b t", b=16, c=2, h=2)[:, :, hh]
                    P.tt(MIX[ps_, 2 * j:2 * j + 2, 0:64].re("p c (b t) -> p c b t", t=4), src, rdv, ALU.mult)
            cs64 = slice(0, 64)
            rwkv_gates(l, lambda rc: ZRW[:, rc, cs64], 64)
            zoff = {"r": 0, "k": 4, "v": 8}
            for _ in elem_gen(l, lambda kind: ZRW[:, zoff[kind]:zoff[kind] + 4, 0:64],
                              SG[:, :, 0:64], AA[:, :, 0:64], 64, 4, 0, M4):
                pass
            pc_all = PCt2[0].re("p (a b) -> p a b", a=4)
            ot4 = OT[:, 0:256].re("p (a m) -> p a m", a=4)

            def load_state(b):
                P.dma(HS32[b % 4], dv(hbd.ap()[l, b].rearrange("p (a v) -> p a v", a=4), "hbd"), "hs%d" % (b % 4))

            def ph2S(b):
                hb_old, hb_new = HBF4[b % 2], HBF4[1 - b % 2]
                P.copy(hb_old, HS32[b % 4], eng="dve")
                for v_ in ph2_gen(l, 4, b % 2, b % 3, pc_all[:, :, b], HS32[b % 4], hb_old, hb_new,
                                  ot4[:, :, 4 * b:4 * b + 4]):
                    yield v_
                P.dma(dv(wkvs.ap()[l, b].rearrange("p (a v) -> p a v", a=4), "wkvs"), HS32[b % 4], "hs%d" % (b % 4))
                if b + 3 < 16:
                    load_state(b + 3)
                yield
            for b in range(3):
                load_state(b)
            zsel = lambda kind: ZRW[:, zoff[kind]:zoff[kind] + 4, 0:64]
            for t in range(16 + 2):
                run_streams(ph2S(t - 2) if 0 <= t - 2 < 16 else None,
                            ph1_gen((t - 1) % 2, (t - 1) % 3) if 0 <= t - 1 < 16 else None,
                            bd_gen(64, 4, slice(4 * t, 4 * t + 4), t % 3, zsel) if t < 16 else None)
            bon4 = BONb[0][:, 0:256].re("p (a m) -> p a m", a=4)
            for _ in gn_gen(l, 64, bon4, GATE[:, :, 0:64], MIX[:, 4:8, 0:64]):
                pass
            if debug and l == 0:
                P.dma(dv(dbg_mixs.ap(), "dbg_mixs"), MIX[:, :, 0:64], "dbgms")
            out_ffn(l, n)
        rmsnorm(X[:, :, 0:n], n, "nf", 0, YO[:, :, 0:n])
        P.dma(dv(ysT.ap().rearrange("(kc p) t -> p kc t", p=128), "ysT"), YO[:, :, 0:n], "yo")

    def prompt_tile(s, ti):
        n = TT
        last = (ti == 2048 // TT - 1)
        cols = slice(ti * TT, (ti + 1) * TT)
        P.dma(X, dv(xT.ap()[s].rearrange("(kc p) t -> p kc t", p=128)[:, :, cols], "xT"), "x")
        for l in range(nlayers):
            if last:
                rmsnorm(X, n, "n1", l, HB, f32_cols=([n - 1], HL32))
                P.dma(dv(shp.ap()[l, s].rearrange("kc p -> p kc"), "shp"), HL32[:, :, 0:1].re("p k o -> p (k o)"), "hl",
                      allow_slow_non_contiguous=True)
            else:
                rmsnorm(X, n, "n1", l, HB)
            if ti == 0:
                P.memset(ZC[l], 0.0)
                P.memset(H32[l], 0.0)

            def evac(c, ps):
                if c < 4:
                    for hh in range(2):
                        ps_ = slice(hh * 64, (hh + 1) * 64)
                        P.copy(QBD[ps_, c // 2, :, c % 2, hh, :], ps[ps_, 0:n].re("p (b q) -> p b q", q=128), eng="act")
                elif c < 6:
                    j = c - 4
                    P.copy(KD[2 * l + j][:, 128:128 + n], ps[:, 0:n], eng="dve" if last else "act")
                    if last:
                        P.copy(KO32[j * 64:(j + 1) * 64, :], ps[j * 64:(j + 1) * 64, n - 128:n], eng="dve")
                elif c == 6:
                    P.copy(VT[:, 0:n], ps[:, 0:n], eng="dve" if last else "act")
                    if last:
                        P.copy(VO32, ps[:, n - 128:n], eng="dve")
                else:
                    rc = c - 7
                    P.act(ZRW[:, rc, 0:n], ps[:, 0:n], AF.Copy, scale=pv(l, "omu", rc))
                    P.stt(ZRW[:, rc, 1:n], ps[:, 0:n - 1], pv(l, "mu", rc), ZRW[:, rc, 1:n], ALU.mult, ALU.add)
                    P.stt(ZRW[:, rc, 0:1], ZC[l][:, rc:rc + 1], pv(l, "mu", rc), ZRW[:, rc, 0:1], ALU.mult, ALU.add)
                    P.copy(ZC[l][:, rc:rc + 1], ps[:, n - 1:n], eng="dve")
            in_proj(l, n, n, evac, blks=range(4))

            def att_gen():
                for v_ in in_proj_gen(l, n, n, evac, (4, 5), alloc=pba):
                    yield v_
                if last:
                    kv_out(KO32, 128, kp.ap()[l, s], "kp", alloc=pba)
                    yield
                    kv_out(VO32, 128, vp.ap()[l, s], "vp", alloc=pba)
                    yield
                for i in range(4):
                    gb = ti * 4 + i
                    pt = pba()
                    ptb = pt.bc(BF16)
                    P.tr(ptb[:, 0:128], VT[:, i * 128:(i + 1) * 128], ident)
                    for j in range(2):
                        P.copy(VD[l][:, gb % 5, j], ptb[:, j * 64:(j + 1) * 64].bcast(1, 2), eng="dve")
                    yield
                un = 0
                for i in range(4):
                    gb = ti * 4 + i
                    bc = slice(i * 128, (i + 1) * 128)
                    for j in range(2):
                        kd = KD[2 * l + j]
                        kcur = kd[:, 128 + i * 128:128 + (i + 1) * 128]
                        kprev = kd[:, i * 128:(i + 1) * 128]
                        vcur = VD[l][:, gb % 5, j].re("p a d -> p (a d)")
                        vprev = VD[l][:, (gb - 1) % 5, j].re("p a d -> p (a d)")
                        for c2 in range(2):
                            h0 = 4 * j + 2 * c2
                            E = ETS[un % 2]
                            rd = RD[:, (un % 2) * 256:(un % 2) * 256 + 256]
                            un += 1
                            qv = QBD[:, j, i, c2].re("p h q -> p (h q)")
                            W = 512 if gb > 0 else 256
                            pa = pba()
                            P.mm(pa[:, 0:256], kcur, qv)
                            if gb > 0:
                                P.mm(pa[:, 256:512], kprev, qv)
                            P.act(E[:, 0:W], pa[:, 0:W], AF.Exp, scale=0.125)
                            yield
                            P.tt(E[:, 0:256].re("p (h q) -> p h q", h=2), E[:, 0:256].re("p (h q) -> p h q", h=2),
                                 EB[:, 0, h0:h0 + 2, :], ALU.mult, eng="pool")
                            if gb > 0:
                                P.tt(E[:, 256:512].re("p (h q) -> p h q", h=2), E[:, 256:512].re("p (h q) -> p h q", h=2),
                                     EB[:, 1, h0:h0 + 2, :], ALU.mult, eng="pool")
                            yield
                            po = pba()
                            if gb > 0:
                                P.mm(po[:, 0:256], vprev, E[:, 256:512], start=True, stop=False)
                                P.mm(po[:, 0:256], vcur, E[:, 0:256], start=False, stop=True)
                                P.mm(po[:, 256:512], ones, E[:, 256:512], start=True, stop=False)
                                P.mm(po[:, 256:512], ones, E[:, 0:256], start=False, stop=True)
                            else:
                                P.mm(po[:, 0:256], vcur, E[:, 0:256])
                                P.mm(po[:, 256:512], ones, E[:, 0:256])
                            P.tt(rd.re("p (h q) -> p h q", h=2), po[:, 256:512].re("p (h q) -> p h q", h=2),
                                 ES[:, 8 * l + h0:8 * l + h0 + 2].bcast(2, 128), ALU.add)
                            yield
                            P.act(rd, rd, AF.Ln)
                            P.act(rd, rd, AF.Exp, scale=-1.0)
                            yield
                            for hh in range(2):
                                ps_ = slice(hh * 64, (hh + 1) * 64)
                                P.tt(MIXc(2 * j + c2, 2 * j + c2 + 1)[ps_, 0, bc], po[ps_, hh * 128:(hh + 1) * 128],
                                     rd[ps_, hh * 128:(hh + 1) * 128], ALU.mult)
                            yield
                for j in range(2):
                    kd = KD[2 * l + j]
                    P.copy(kd[:, 0:128], kd[:, n:n + 128], eng="pool")
                yield
            att = {"gen": att_gen(), "clock": 0.0, "done": False}
            P.copy(HBF4[0], H32[l], eng="dve")
            zoff = {"r": 0, "k": 4, "v": 8}
            for g in range(2):
                rwkv_gates(l, lambda rc: ZRW[:, rc, 256 * g:256 * g + 256], 256, o=256 * g)

            def zs(c):
                return lambda kind: ZRW[:, zoff[kind]:zoff[kind] + 4, 64 * c:64 * c + 64]

            def prepS(c):
                cc = slice(64 * c, 64 * c + 64)
                for v_ in elem_gen(l, zs(c), SG[:, :, cc], AA[:, :, cc], 64, 64, c % 3, M64):
                    yield v_
                for v_ in bd_gen(64, 64, slice(0, 64), c % 3, zs(c)):
                    yield v_

            def ph2S(c):
                cc = slice(64 * c, 64 * c + 64)
                bon4 = BONb[c % 3][:, 0:256].re("p (a m) -> p a m", a=4)
                gn = lambda: gn_gen(l, 64, bon4, GATE[:, :, cc], MIXc(4, 8)[:, :, cc])
                return ph2_gen(l, 64, c % 2, c % 3, PCt2[c % 3][:, 0:4], H32[l], HBF4[c % 2], HBF4[1 - c % 2],
                               OT[:, 0:256].re("p (a m) -> p a m", a=4), gn=gn)
            NCH = n // 64
            for t in range(NCH + 2):
                run_streams(ph2S(t - 2) if 0 <= t - 2 < NCH else None,
                            ph1_gen((t - 1) % 2, (t - 1) % 3) if 0 <= t - 1 < NCH else None,
                            prepS(t) if t < NCH else None, bg=att)
            for _ in att["gen"]:
                pass
            if last:
                P.dma(dv(wkvp.ap()[l, s].rearrange("p (a v) -> p a v", a=4), "wkvp"), H32[l], "wkvp")
            if debug and s == 0 and ti == 0 and l == 0:
                P.dma(dv(dbg_mix.ap(), "dbg_mix"), MIX.re("p a t -> p (a t)"), "dbgm")
                P.dma(dv(dbg_z.ap(), "dbg_z"), ZRW.re("p a t -> p (a t)"), "dbgz")
                dbgs["on"] = True
            out_ffn(l, n)
            if debug and dbgs["on"]:
                P.dma(dv(dbg_x2.ap(), "dbg_x2"), X.re("p a t -> p (a t)"), "dbgx2")
                dbgs["on"] = False
        rmsnorm(X, n, "nf", 0, YO)
        P.dma(dv(yT.ap()[s].rearrange("(kc p) t -> p kc t", p=128)[:, :, cols], "yT"), YO, "yo")

    for t in range(n_ptiles):
        prompt_tile(t // (2048 // TT), t % (2048 // TT))
    if do_sample:
        sample_tile()
    P.emit()
    return nc, P


def _t5_bucket(d):
    n = np.maximum(d, 0)
    large = 16 + (np.log(np.maximum(n, 1) / 16) / np.log(128 / 16) * 16).astype(np.int32)
    large = np.minimum(large, 31)
    return np.where(n < 16, n, large).astype(np.int32)


def _constants():
    i = np.arange(128)
    hs, ss = i // 64, i % 64
    same = hs[:, None] == hs[None, :]
    ident = np.eye(128, dtype=np.float32)
    ones = np.ones((128, 128), np.float32)
    bones = same.astype(np.float32)
    MUs = (same & (ss[:, None] < ss[None, :])).astype(np.float32)
    MUi = (same & (ss[:, None] <= ss[None, :])).astype(np.float32)
    MLs = (same & (ss[:, None] > ss[None, :])).astype(np.float32)
    bd4 = np.zeros((128, 64), np.float32)
    j = np.arange(64)
    bd4[:64] = (j[:, None] // 4 == j[None, :] // 4).astype(np.float32)
    m64 = np.ones(256, np.float32)
    m64[::64] = 0
    m4 = np.ones(256, np.float32)
    m4[::4] = 0
    cbf = np.concatenate([ident, ones, bones, MUs, MUi, MLs, bd4, np.tile(m64[None], (128, 1)), np.tile(m4[None], (128, 1))], 1)
    c32 = np.concatenate([ident, ident[:, ::-1]], 1).astype(np.float32)
    ohv = np.zeros((33, 512), np.float32)
    for ty in range(2):
        for ii in range(256):
            if ty == 0:
                d = 127 - ii
                valid = (0 <= d <= 127)
            else:
                d = 255 - ii
                valid = (1 <= d <= 128)
            col = ty * 256 + (254 - ii if ii < 255 else 255)
            if valid:
                ohv[_t5_bucket(np.array(d))[()], col] = 1.0
            else:
                ohv[32, col] = 1.0
    return cbf, c32, ohv


_CACHE = {}


def kernel(x_prompt, x_sample, cache_k, cache_v, state_shift, state_wkv, norm1, w_in, w_out, sink,
           rel_table, mu, w0, w2, a0, a2, g2, k_k, k_a, r_k, gn_w, gn_b, norm2, w_gate, w_up,
           w_down, norm_f):
    f = lambda a: np.ascontiguousarray(np.asarray(a, dtype=np.float32))
    x_prompt, x_sample, cache_k, cache_v, state_shift, state_wkv = map(f, (x_prompt, x_sample, cache_k, cache_v, state_shift, state_wkv))
    norm1, w_in, w_out, sink, rel_table, mu, w0, w2, a0, a2, g2 = map(f, (norm1, w_in, w_out, sink, rel_table, mu, w0, w2, a0, a2, g2))
    k_k, k_a, r_k, gn_w, gn_b, norm2, w_gate, w_up, w_down, norm_f = map(f, (k_k, k_a, r_k, gn_w, gn_b, norm2, w_gate, w_up, w_down, norm_f))
    nc = _CACHE.get("nc_override")
    if nc is None:
        if "nc" not in _CACHE:
            _CACHE["nc"] = build_program()[0]
        nc = _CACHE["nc"]
    cbf, c32, ohv = _constants()
    q_, k_, v_, rw_ = w_in[:, :, 0:512], w_in[:, :, 512:640], w_in[:, :, 640:768], w_in[:, :, 768:]
    w_in_aug = np.concatenate([rw_, q_, k_[:, :, 0:64], k_[:, :, 0:64], k_[:, :, 64:128], k_[:, :, 64:128], v_], 2)
    w_gu = np.stack([w_gate.reshape(2, 1024, 22, 128), w_up.reshape(2, 1024, 22, 128)], 3).reshape(2, 1024, 5632)
    w_dn = w_down.reshape(2, 22, 128, 8, 128).transpose(0, 3, 2, 1, 4).reshape(2, 8, 128, 2816)
    lora = np.zeros((2, 128, 1536), np.float32)
    lora[:, 0:64, 0:512] = w2
    lora[:, 64:128, 512:1024] = a2
    lora[:, :, 1024:1536] = g2
    pvec = np.zeros((128, NPV), np.float32)

    def cols(vec, nchunk):
        return vec.reshape(nchunk, 128).T

    for l in range(2):
        b0 = 58 * l
        pvec[:, b0:b0 + 8] = cols(norm1[l], 8)
        pvec[:, b0 + 8:b0 + 16] = cols(norm2[l], 8)
        pvec[:, b0 + 16:b0 + 30] = cols(mu[l], 14)
        for nm, off in ((w0, 30), (a0, 34), (k_k, 38), (k_a, 42), (r_k, 46), (gn_w, 50), (gn_b, 54)):
            pvec[:, b0 + off:b0 + off + 4] = cols(nm[l], 4)
    pvec[:, 116:124] = cols(norm_f, 8)
    shared = dict(w_in=np.ascontiguousarray(w_in_aug), w_out=w_out, w_gu=np.ascontiguousarray(w_gu),
                  w_dn=np.ascontiguousarray(w_dn), lora=lora, pvec=pvec, rel=rel_table,
                  sink=np.ascontiguousarray(sink.reshape(1, 16)), cbf=cbf, c32=c32, ohv=ohv)
    in_maps = []
    for c in range(NCORES):
        xs = x_sample[16 * c:16 * c + 16]
        ck = cache_k[:, 16 * c:16 * c + 16]
        cv = cache_v[:, 16 * c:16 * c + 16]
        ckT = np.repeat(ck.transpose(0, 1, 3, 4, 2)[:, :, :, None], 2, 3)
        ckT = ckT.transpose(0, 3, 4, 1, 2, 5).reshape(2, 128, 4096)
        cvdd = np.repeat(cv[:, :, :, :, None], 2, 4)
        cvdd = cvdd.transpose(0, 2, 1, 3, 4, 5).reshape(2, 128, 4096)
        sw = state_wkv[:, 16 * c:16 * c + 16]
        hb = np.zeros((2, 16, 4, 2, 64, 2, 64), np.float32)
        swp = sw.reshape(2, 16, 4, 2, 64, 64)
        for hh in range(2):
            hb[:, :, :, hh, :, hh, :] = swp[:, :, :, hh].transpose(0, 1, 2, 4, 3)
        hb = hb.transpose(0, 1, 3, 4, 2, 5, 6).reshape(2, 16, 128, 512)
        m = dict(shared)
        m.update(xT=np.ascontiguousarray(x_prompt[2 * c:2 * c + 2].transpose(0, 2, 1)),
                 xsT=np.ascontiguousarray(xs.reshape(64, 1024).T),
                 shT=np.ascontiguousarray(state_shift[:, 16 * c:16 * c + 16].transpose(0, 2, 1)),
                 ckT=np.ascontiguousarray(ckT), cvd=np.ascontiguousarray(cvdd),
                 ckr=np.ascontiguousarray(ck.reshape(2, 16, 128 * 128)), cvr=np.ascontiguousarray(cv.reshape(2, 16, 128 * 128)),
                 hbd=np.ascontiguousarray(hb))
        in_maps.append(m)
    res = run_bass_kernel_spmd(nc, in_maps, core_ids=list(range(NCORES)))
    R = res.results
    y_prompt = np.concatenate([np.asarray(r["yT"]).transpose(0, 2, 1) for r in R], 0)
    y_sample = np.concatenate([np.asarray(r["ysT"]).T.reshape(16, 4, 1024) for r in R], 0)
    k_prompt = np.concatenate([np.asarray(r["kp"]).reshape(2, 2, 128, 2, 64) for r in R], 1)
    v_prompt = np.concatenate([np.asarray(r["vp"]).reshape(2, 2, 128, 2, 64) for r in R], 1)
    shift_prompt = np.concatenate([np.asarray(r["shp"]).reshape(2, 2, 1024) for r in R], 1)

    def unbd(a, nb):
        a = np.asarray(a).reshape(2, nb, 2, 64, 4, 2, 64)
        out = np.zeros((2, nb, 4, 2, 64, 64), np.float32)
        for hh in range(2):
            out[:, :, :, hh] = a[:, :, hh, :, :, hh, :].transpose(0, 1, 3, 4, 2)
        return out.reshape(2, nb, 8, 64, 64)
    wkv_prompt = np.concatenate([unbd(r["wkvp"], 2) for r in R], 1)
    k_sample = np.concatenate([np.asarray(r["ks"]).reshape(2, 16, 128, 2, 64) for r in R], 1)
    v_sample = np.concatenate([np.asarray(r["vs"]).reshape(2, 16, 128, 2, 64) for r in R], 1)
    shift_sample = np.concatenate([np.asarray(r["shs"]).transpose(0, 2, 1) for r in R], 1)
    wkv_sample = np.concatenate([unbd(r["wkvs"], 16) for r in R], 1)
    outs = (y_prompt, y_sample, k_prompt, v_prompt, shift_prompt, wkv_prompt, k_sample, v_sample, shift_sample, wkv_sample)
    return tuple(np.ascontiguousarray(o, dtype=np.float32) for o in outs)
```

```python
import contextlib
import numpy as np
import concourse.bass as bass
import concourse.mybir as mybir
from concourse.bass_utils import run_bass_kernel_spmd

F32 = mybir.dt.float32
BF16 = mybir.dt.bfloat16
AF = mybir.ActivationFunctionType
ALU = mybir.AluOpType

NCORES = 8
TT = 512
C0 = 0.6065306597126334
NPV = 124
NSLOT = 4


class V:
    __slots__ = ("ap", "keys")

    def __init__(self, ap, keys):
        self.ap = ap
        if isinstance(keys, tuple):
            keys = [keys]
        self.keys = list(keys)

    def __getitem__(self, idx):
        return V(self.ap[idx], self.keys)

    def k(self, *suffix):
        return V(self.ap, [k + tuple(suffix) for k in self.keys])

    def re(self, s, **kw):
        return V(self.ap.rearrange(s, **kw), self.keys)

    def bc(self, dtype):
        return V(self.ap.bitcast(dtype), self.keys)

    def bcast(self, axis, n):
        a = self.ap.unsqueeze(axis)
        shp = list(a.shape)
        shp[axis] = n
        return V(a.to_broadcast(shp), self.keys)


def vjoin(ap, vs):
    ks = []
    for v in vs:
        ks += v.keys
    return V(ap, ks)


class Prog:
    ENGS = ("pe", "act", "dve", "pool", "sp")

    def __init__(self, nc):
        self.nc = nc
        self.ops = []
        self.lastw = {}
        self.readers = {}
        self.stack = contextlib.ExitStack()
        self.fin = []
        self.eng_free = {}

    COST = {"pe": 0.14, "act": 0.5, "dve": 0.6, "pool": 0.85, "sp": 0.1}

    @staticmethod
    def _n(v):
        n = 1
        for d in v.ap.shape[1:]:
            n *= d
        return n

    def _cost(self, eng, out):
        n = self._n(out)
        if eng == "pe":
            return 0.06 + n / 2000.0
        if eng == "act":
            return 0.25 + n * 0.00085
        if eng == "dve":
            return 0.12 + n * 0.0011
        return 0.2 + n * 0.0022

    def sb(self, name, shape, dtype):
        t = self.stack.enter_context(self.nc.sbuf_tensor(name, list(shape), dtype))
        return V(t[:], (name,))

    def ps(self, name, shape, dtype=F32):
        t = self.stack.enter_context(self.nc.psum_tensor(name, list(shape), dtype))
        return V(t[:], (name,))

    def add(self, eng, fn, reads=(), writes=(), slot=None, cost=None):
        i = len(self.ops)
        rk = []
        for r in reads:
            rk += r.keys
        wk = []
        for w in writes:
            wk += w.keys
        deps = set()
        for r in rk:
            if r in self.lastw:
                deps.add(self.lastw[r])
        for w in wk:
            if w in self.lastw:
                deps.add(self.lastw[w])
            rd = self.readers.get(w)
            if rd:
                deps.update(rd.values())
        self.ops.append(dict(eng=eng, fn=fn, deps=deps, slot=slot))
        t0 = self.eng_free.get(eng, 0.0)
        for d in deps:
            od = self.ops[d]
            t0 = max(t0, self.fin[d] + (0.05 if (od["eng"] == eng and od["slot"] is None) else 0.3))
        if slot is not None:
            self.fin.append(t0 + 2.0)
            self.eng_free[eng] = t0 + 0.1
        else:
            f = t0 + (cost if cost is not None else self.COST[eng])
            self.fin.append(f)
            self.eng_free[eng] = f
        wks = set(wk)
        for w in wk:
            self.lastw[w] = i
            self.readers[w] = {}
        who = eng if slot is None else ("dma", slot)
        for r in rk:
            if r not in wks:
                self.readers.setdefault(r, {})[who] = i
        return i

    def mm(self, out, lhsT, rhs, start=True, stop=True, skip=False):
        kw = dict(start=start, stop=stop)
        if skip:
            kw["skip_group_check"] = True
        return self.add("pe", lambda e: e.matmul(out.ap, lhsT.ap, rhs.ap, **kw),
                        [lhsT, rhs] + ([] if start else [out]), [out], cost=self._cost("pe", out))

    def tr(self, out, in_, ident):
        return self.add("pe", lambda e: e.transpose(out.ap, in_.ap, ident.ap), [in_, ident], [out])

    def act(self, out, in_, func, bias=None, scale=None):
        kw = {}
        rd = [in_]
        if bias is not None:
            kw["bias"] = bias.ap
            rd.append(bias)
        if scale is not None:
            if isinstance(scale, V):
                kw["scale"] = scale.ap
                rd.append(scale)
            else:
                kw["scale"] = scale
        return self.add("act", lambda e: e.activation(out.ap, in_.ap, func, **kw), rd, [out], cost=self._cost("act", out))

    def tt(self, out, a, b, op, eng="dve"):
        return self.add(eng, lambda e: e.tensor_tensor(out.ap, a.ap, b.ap, op), [a, b], [out], cost=self._cost(eng, out))

    def ts(self, out, a, s1, op0, s2=None, op1=None, eng="dve"):
        rd = [a]
        s1a = s1.ap if isinstance(s1, V) else s1
        s2a = s2.ap if isinstance(s2, V) else s2
        if isinstance(s1, V):
            rd.append(s1)
        if isinstance(s2, V):
            rd.append(s2)
        kw = {}
        if op1 is not None:
            kw["op1"] = op1
        return self.add(eng, lambda e: e.tensor_scalar(out.ap, a.ap, s1a, s2a, op0, **kw), rd, [out], cost=self._cost(eng, out))

    def stt(self, out, a, s, b, op0, op1, eng="dve"):
        rd = [a, b]
        sa = s.ap if isinstance(s, V) else s
        if isinstance(s, V):
            rd.append(s)
        return self.add("dve", lambda e: e.scalar_tensor_tensor(out.ap, a.ap, sa, b.ap, op0, op1), rd, [out], cost=self._cost("dve", out))

    def copy(self, out, in_, eng="dve"):
        if eng == "act":
            return self.add("act", lambda e: e.copy(out.ap, in_.ap), [in_], [out], cost=self._cost("act", out))
        return self.add(eng, lambda e: e.tensor_copy(out.ap, in_.ap), [in_], [out], cost=self._cost(eng, out))

    def recip(self, out, in_):
        return self.add("dve", lambda e: e.reciprocal(out.ap, in_.ap), [in_], [out])

    def memset(self, out, val, eng="pool"):
        return self.add(eng, lambda e: e.memset(out.ap, val), [], [out])

    def dma(self, out, in_, slot, eng="sp", **kw):
        return self.add(eng, lambda e: e.dma_start(out=out.ap, in_=in_.ap, **kw), [in_], [out], slot=slot)

    def emit(self):
        nc = self.nc
        ops = self.ops
        n = len(ops)
        has_dep = [False] * n
        for o in ops:
            for d in o["deps"]:
                has_dep[d] = True
        cnt = {e: 0 for e in self.ENGS}
        slotcnt = {}
        for i, o in enumerate(ops):
            if o["slot"] is not None:
                s = "slot:" + o["slot"]
                slotcnt[s] = slotcnt.get(s, 0) + 16
                o["tok"] = (s, slotcnt[s])
            elif has_dep[i]:
                cnt[o["eng"]] += 1
                o["tok"] = ("eng:" + o["eng"], cnt[o["eng"]])
            else:
                o["tok"] = None
        sems = {}
        for name in [("eng:" + e) for e in self.ENGS] + list(slotcnt.keys()):
            sems[name] = self.stack.enter_context(nc.semaphore(name.replace(":", "_")))
        per_eng = {e: [] for e in self.ENGS}
        for i, o in enumerate(ops):
            per_eng[o["eng"]].append(i)
        stats = {e: [0, 0] for e in self.ENGS}

        def run_engine(ename, e):
            waited = {}
            for i in per_eng[ename]:
                o = ops[i]
                need = {}
                for d in o["deps"]:
                    od = ops[d]
                    tok = od["tok"]
                    if tok is None:
                        continue
                    if od["slot"] is None and od["eng"] == ename and ename == "pe":
                        continue
                    s, v = tok
                    if waited.get(s, 0) >= v:
                        continue
                    if need.get(s, 0) < v:
                        need[s] = v
                for s, v in need.items():
                    e.wait_ge(sems[s], v)
                    waited[s] = v
                    stats[ename][1] += 1
                ins = o["fn"](e)
                stats[ename][0] += 1
                if o["tok"] is not None:
                    s, v = o["tok"]
                    ins.then_inc(sems[s], 16 if o["slot"] is not None else 1)
            if ename == "sp":
                for s, v in slotcnt.items():
                    if waited.get(s, 0) < v:
                        e.wait_ge(sems[s], v)

        with nc.Block() as block:
            @block.sync
            def _(e):
                run_engine("sp", e)

            @block.scalar
            def _(e):
                run_engine("act", e)

            @block.vector
            def _(e):
                run_engine("dve", e)

            @block.gpsimd
            def _(e):
                run_engine("pool", e)

            @block.tensor
            def _(e):
                run_engine("pe", e)
        self.stats = stats
        self.stack.close()


def build_program(n_ptiles=8, do_sample=True, nlayers=2, debug=False):
    nc = bass.Bass("TRN2", target_bir_lowering=False)
    P = Prog(nc)

    def din(name, shape):
        return nc.dram_tensor(name, list(shape), F32, kind="ExternalInput")

    def dout(name, shape):
        return nc.dram_tensor(name, list(shape), F32, kind="ExternalOutput")

    def DV(t, ap=None):
        return V(t.ap() if ap is None else ap, (t.name if hasattr(t, "name") else str(id(t)),))

    xT = din("xT", [2, 1024, 2048])
    xsT = din("xsT", [1024, 64])
    shT = din("shT", [2, 1024, 16])
    ckT = din("ckT", [2, 128, 4096])
    cvd = din("cvd", [2, 128, 4096])
    ckr = din("ckr", [2, 16, 128 * 128])
    cvr = din("cvr", [2, 16, 128 * 128])
    hbd = din("hbd", [2, 16, 128, 512])
    w_in = din("w_in", [2, 1024, 2688])
    w_out = din("w_out", [2, 1024, 1024])
    w_gu = din("w_gu", [2, 1024, 5632])
    w_dn = din("w_dn", [2, 8, 128, 2816])
    lora = din("lora", [2, 128, 1536])
    pvec = din("pvec", [128, NPV])
    rel = din("rel", [32, 8])
    sink = din("sink", [1, 16])
    cbf = din("cbf", [128, 1344])
    c32 = din("c32", [128, 256])
    ohv = din("ohv", [33, 512])

    yT = dout("yT", [2, 1024, 2048])
    ysT = dout("ysT", [1024, 64])
    kp = dout("kp", [2, 2, 128, 128])
    vp = dout("vp", [2, 2, 128, 128])
    shp = dout("shp", [2, 2, 8, 128])
    wkvp = dout("wkvp", [2, 2, 128, 512])
    ks = dout("ks", [2, 16, 128 * 128])
    vs = dout("vs", [2, 16, 128 * 128])
    shs = dout("shs", [2, 1024, 16])
    wkvs = dout("wkvs", [2, 16, 128, 512])
    gscr = nc.dram_tensor("gscr", [8, 512], F32, kind="Internal")
    if debug:
        dbg_mix = nc.dram_tensor("dbg_mix", [128, 8 * TT], BF16, kind="ExternalOutput")
        dbg_z = nc.dram_tensor("dbg_z", [128, 14 * (TT + 1)], BF16, kind="ExternalOutput")
        dbg_x1 = nc.dram_tensor("dbg_x1", [128, 8 * TT], F32, kind="ExternalOutput")
        dbg_x2 = nc.dram_tensor("dbg_x2", [128, 8 * TT], F32, kind="ExternalOutput")
        dbg_hf = nc.dram_tensor("dbg_hf", [128, 8 * TT], BF16, kind="ExternalOutput")
    if debug:
        dbg_mixs = nc.dram_tensor("dbg_mixs", [128, 8, 64], BF16, kind="ExternalOutput")
    dbgs = {"on": False}

    def dv(t, name):
        return V(t, (name,))

    CB = P.sb("CB", [128, 1344], BF16)
    P.dma(CB, dv(cbf.ap(), "cbf"), "cst", eng="pool")
    ident = CB[:, 0:128]
    ones = CB[:, 128:256]
    bones = CB[:, 256:384]
    MG = CB[:, 384:640]
    MLs = CB[:, 640:768]
    BD4 = CB[0:64, 768:832]
    C32 = P.sb("C32", [128, 256], F32)
    P.dma(C32, dv(c32.ap(), "c32"), "cst32")
    ident32 = C32[:, 0:128]
    JM = C32[:, 128:256]
    M64 = CB[:, 832:1088]
    M4 = CB[:, 1088:1344]
    PVt = P.sb("PVEC", [128, NPV + 8 + 28], F32)
    P.dma(PVt[:, 0:NPV], dv(pvec.ap(), "pvec"), "pv")
    for l in range(2):
        P.ts(PVt[:, NPV + 4 * l:NPV + 4 * l + 4], PVt[:, 58 * l + 42:58 * l + 46], -1.0, ALU.mult, 1.0, ALU.add)
        P.ts(PVt[:, NPV + 8 + 14 * l:NPV + 8 + 14 * l + 14], PVt[:, 58 * l + 16:58 * l + 30], -1.0, ALU.mult, 1.0, ALU.add)

    def pv(l, name, j=None):
        base = {"n1": 0, "n2": 8, "mu": 16, "w0": 30, "a0": 34, "kk": 38, "ka": 42, "rk": 46, "gw": 50, "gb": 54}
        if name == "nf":
            b0 = 116
        elif name == "omka":
            b0 = NPV + 4 * l
        elif name == "omu":
            b0 = NPV + 8 + 14 * l
        else:
            b0 = 58 * l + base[name]
        return PVt[:, b0 + j:b0 + j + 1]

    EPS6 = P.sb("EPS6", [128, 1], F32)
    P.memset(EPS6, 1e-6)
    GNE = P.sb("GNE", [128, 1], F32)
    P.memset(GNE, 64e-5)
    LR = P.sb("LR", [128, 2, 1536], BF16)
    P.dma(LR, dv(lora.ap().rearrange("l p n -> p l n"), "lora"), "lr", eng="pool")

    PSB = [P.ps("psb%d" % i, [128, 512]) for i in range(8)]

    def bank(i):
        return V(PSB[i].ap, [("psb%d" % i,)])

    def half(i, h):
        return V(PSB[i].ap[:, h * 256:(h + 1) * 256], [("psb%d" % i,)])

    rr = {"d": 0, "h": 0, "ev": 0}

    rr.update({"pp": 0, "p1": 0, "p2": 0})

    def pd():
        rr["d"] = (rr["d"] + 1) % 4
        return bank((0, 1, 6, 7)[rr["d"]])

    def pd_prep():
        return bank(0)

    def pb1():
        rr["p1"] = (rr["p1"] + 1) % 3
        return bank(5 + rr["p1"])

    def pb2():
        rr["p2"] ^= 1
        return bank(1 + rr["p2"])

    rr["pa"] = 0

    def pba():
        rr["pa"] ^= 1
        return bank(3 + rr["pa"])

    def ph():
        rr["h"] = (rr["h"] + 1) % 4
        return half(4 + rr["h"], 0)

    SCR = P.sb("SCR", [128, 22, TT], BF16)
    REL = P.sb("REL", [33, 8], F32)
    P.memset(REL[32:33, :], -30000.0)
    P.dma(REL[0:32, :], dv(rel.ap(), "rel"), "rel")
    OH = P.sb("OH", [33, 512], F32)
    P.dma(OH, dv(ohv.ap(), "ohv"), "oh")
    pg = pd()
    P.mm(pg[0:8, :], REL, OH)
    GS = P.sb("GS", [8, 512], F32)
    P.act(GS, pg[0:8, :], AF.Exp)
    P.dma(dv(gscr.ap(), "gscr"), GS, "gscr")
    EB = P.sb("EB", [128, 2, 8, 128], BF16)
    for ty in range(2):
        for h in range(8):
            i_ = ty * 8 + h
            st = V(SCR.ap[:, i_, :].bitcast(F32)[:, 0:128], [("SCR", i_)])
            src = V(bass.AP(gscr, h * 512 + ty * 256, [[1, 128], [1, 128]]), ("gscr",))
            P.dma(st, src, "ebf%d" % i_, eng="sp")
            pj = pd()
            P.mm(pj[:, 0:128], JM, st)
            P.copy(EB[:, ty, h, :], pj[:, 0:128], eng=("dve" if i_ % 2 else "act"))
    EBSC = P.sb("EBSC", [64, 2, 16, 4, 4], BF16)
    for j in range(2):
        P.tt(EBSC[:, j], EB[0:64, 0, 4 * j:4 * j + 4, 0:64].re("p n (b t) -> p b n t", t=4),
             V(BD4.ap.rearrange("p (b t) -> p b t", t=4).unsqueeze(2).to_broadcast([64, 16, 4, 4]), BD4.keys), ALU.mult)
    ES = P.sb("ES", [128, 16], F32)
    P.dma(ES, V(bass.AP(sink, 0, [[0, 128], [1, 16]]), ("sink",)), "es")
    P.act(ES, ES, AF.Exp)

    X = P.sb("X", [128, 8, TT], F32)
    HB = P.sb("HB", [128, 8, TT], BF16)
    WR = [P.sb("WR%d" % i, [128, 4096], BF16) for i in range(NSLOT)]
    QBD = P.sb("QBD", [128, 2, TT // 128, 2, 2, 128], BF16)
    P.memset(QBD, 0.0)
    QBP = P.sb("QBP", [128, 2, 16, 2, 2, 4], BF16)
    P.memset(QBP, 0.0)
    KD = [P.sb("KD%d_%d" % (l, j), [128, 128 + TT], BF16) for l in range(2) for j in range(2)]
    VT = P.sb("VT", [128, TT], BF16)
    VD = [P.sb("VD%d" % l, [128, 5, 2, 2, 64], BF16) for l in range(2)]
    VDS = P.sb("VDS", [64, 2, 2, 64], BF16)
    KO32 = P.sb("KO32", [128, 128], F32)
    VO32 = P.sb("VO32", [128, 128], F32)
    KVO = [P.sb("KVO%d" % i, [128, 128], F32) for i in range(2)]
    ZRW = P.sb("ZRW", [128, 14, TT + 1], BF16)
    ZC = [P.sb("ZC%d" % l, [128, 14], BF16) for l in range(2)]
    MIX0 = P.sb("MIX", [128, 8, TT], BF16)
    MIX = V(MIX0.ap, [MIX0.keys[0] + (c,) for c in range(8)])

    def MIXc(c0, c1):
        return V(MIX0.ap[:, c0:c1], [MIX0.keys[0] + (c,) for c in range(c0, c1)])
    H32 = [P.sb("H32_%d" % l, [128, 4, 128], F32) for l in range(2)]
    _qf = QBD.ap.rearrange("p j b c h q -> p (j b c h q)").bitcast(F32)
    HS32 = [V(_qf[:, 512 * i:512 * (i + 1)].rearrange("p (a v) -> p a v", a=4), QBD.keys) for i in range(4)]
    HBF4 = [P.sb("HBF4_%d" % i, [128, 4, 128], BF16) for i in range(2)]
    RD = P.sb("RD", [128, 512], F32)
    RSTD = RD
    HL32 = P.sb("HL32", [128, 8, 16], F32)
    YO = vjoin(SCR.ap[:, 0:16, :].rearrange("p a t -> p (a t)").bitcast(F32).rearrange("p (k t) -> p k t", k=8),
               [SCR.k(i) for i in range(16)])
    GC = 256
    SG = P.sb("SG", [128, 4, TT], F32)
    AA = P.sb("AA", [128, 4, TT], BF16)
    GATE = P.sb("GATE", [128, 4, TT], BF16)
    Fb = {i: P.sb("F%d" % i, [128, GC], F32) for i in (2, 4, 5, 6, 7, 8, 9, 10, 11, 12, 13, 14)}
    Bb = [P.sb("B%d" % i, [128, GC], BF16) for i in range(6)]
    ET = [Fb[i].bc(BF16) for i in (8, 9, 10, 11)]
    ATb = P.sb("ATb", [128, 4, 2, 64], BF16)
    RT4 = [P.sb("RT4_%d" % i, [128, 4, 2, 64], BF16) for i in range(3)]
    BONb = [P.sb("BON%d" % i, [128, GC], F32) for i in range(3)]
    PCt2 = [P.sb("PCt%d" % i, [128, 64], F32) for i in range(3)]
    BTb = P.sb("BTb", [128, 4, 2, 64], BF16)
    KTb = P.sb("KTb", [128, 4, 2, 64], BF16)
    BHb = P.sb("BHb", [128, 4, 2, 64], BF16)
    KHb = P.sb("KHb", [128, 4, 2, 64], BF16)
    VVb = P.sb("VVb", [128, 4, 2, 64], BF16)
    for t_ in (ATb, RT4[0], RT4[1], RT4[2], BTb, KTb, BHb, KHb, VVb):
        P.memset(t_, 0.0)
    def scrv(c0, n_):
        return vjoin(SCR.ap[:, c0:c0 + n_, :].rearrange("p a t -> p (a t)"), [SCR.k(i) for i in range(c0, c0 + n_)])
    TM4s = [[scrv(8 + c, 1).re("p (q t) -> p q t", q=4) for c in range(4)],
            [scrv(c, 1).re("p (q t) -> p q t", q=4) for c in range(4)]]
    TM4pairs = [[scrv(8, 2), scrv(10, 2)], [scrv(0, 2), scrv(2, 2)]]
    NT4 = [scrv(12, 1).re("p (c t) -> p c t", c=4), scrv(13, 1).re("p (c t) -> p c t", c=4)]
    NN4 = [scrv(14, 1).re("p (c t) -> p c t", c=4), scrv(15, 1).re("p (c t) -> p c t", c=4)]
    S4 = [scrv(16, 1).re("p (c t) -> p c t", c=4), scrv(17, 1).re("p (c t) -> p c t", c=4)]
    ETS = [scrv(18, 1), scrv(19, 1)]
    N4 = P.sb("N4", [128, 4, 128], BF16)
    AAK4 = P.sb("AAK4", [128, 4, 128], BF16)
    W14 = P.sb("W14", [128, 4, 128], BF16)
    MRB4s = [P.sb("MRB4", [128, 4, 128], BF16), scrv(4, 1).re("p (c t) -> p c t", c=4)]
    MRK4s = [P.sb("MRK4", [128, 4, 128], BF16), scrv(5, 1).re("p (c t) -> p c t", c=4)]
    APT4s = [P.sb("APT4", [128, 4, 128], BF16), scrv(6, 1).re("p (c t) -> p c t", c=4)]
    UP4s = [P.sb("UP4", [128, 4, 128], BF16), scrv(7, 1).re("p (c t) -> p c t", c=4)]
    TMPB = P.sb("TMPB", [128, 4, 128], BF16)
    U4 = P.sb("U4", [128, 4, 128], BF16)
    TMPH = P.sb("TMPH", [128, 4, 128], F32)
    MUs_b = V(CB.ap[:, 384:512].unsqueeze(1).to_broadcast([128, 4, 128]), CB.keys)
    MUi_b = V(CB.ap[:, 512:640].unsqueeze(1).to_broadcast([128, 4, 128]), CB.keys)
    MLs_b = V(CB.ap[:, 640:768].unsqueeze(1).to_broadcast([128, 4, 128]), CB.keys)
    ID_b = V(CB.ap[:, 0:128].unsqueeze(1).to_broadcast([128, 4, 128]), CB.keys)
    OT = P.sb("OT", [128, GC], F32)
    CPB = vjoin(SCR.ap[:, 16:20, :].rearrange("p a t -> p (a t)").bitcast(F32).rearrange("p (b f) -> p b f", b=8),
                [SCR.k(i) for i in range(16, 20)])

    wblocks = []

    def layer_blocks(l):
        r = []
        wv = w_in.ap()[l].rearrange("(kc p) n -> p kc n", p=128)
        for b in range(6):
            c0, c1 = 512 * b, min(512 * (b + 1), 2688)
            r.append((wv[:, :, c0:c1], "k8", c1 - c0))
        wv = w_out.ap()[l].rearrange("(kc p) n -> p kc n", p=128)
        for b in range(2):
            r.append((wv[:, :, 512 * b:512 * (b + 1)], "k8", 512))
        wv = w_gu.ap()[l].rearrange("(kc p) n -> p kc n", p=128)
        for b in range(11):
            r.append((wv[:, :, 512 * b:512 * (b + 1)], "k8", 512))
        for oc in range(8):
            r.append((w_dn.ap()[l, oc], "flat", 2816))
        return r

    ntiles = n_ptiles + (1 if do_sample else 0)
    for _t in range(ntiles):
        for l in range(nlayers):
            wblocks.extend(layer_blocks(l))
    wst = {"issued": 0, "next": 0}

    def wslot_view(i):
        src, kind, w = wblocks[i]
        s = WR[i % NSLOT]
        if kind == "k8":
            return s.re("p (kc n) -> p kc n", kc=8)[:, :, 0:w]
        return s[:, 0:w]

    nbl = 27 * nlayers
    wscr = nc.dram_tensor("wscr", [nbl, 128, 4096], BF16, kind="Internal")

    def wget():
        i = wst["next"]
        wst["next"] += 1
        while wst["issued"] < min(len(wblocks), i + NSLOT):
            j = wst["issued"]
            sl = j % NSLOT
            if j < nbl:
                P.dma(wslot_view(j), V(wblocks[j][0], ("wdram",)), "w%d" % sl, eng="pool")
                if ntiles > 1:
                    P.dma(V(wscr.ap()[j], ("wscr", j)), WR[sl], "wb%d" % sl, eng="sp")
            else:
                P.dma(WR[sl], V(wscr.ap()[j % nbl], ("wscr", j % nbl)), "wh%d" % sl, eng="sp")
            wst["issued"] += 1
        return wslot_view(i)

    def rmsnorm(xv, n, gname, l, outv, f32_cols=None):
        sq = SCR[:, 0:8, 0:n]
        for kc in range(8):
            P.act(SCR[:, kc, 0:n].k(kc), xv[:, kc, :], AF.Square)
        pn = pd()
        for kc in range(8):
            P.mm(pn[:, 0:n], ones, SCR[:, kc, 0:n].k(kc), start=(kc == 0), stop=(kc == 7))
        P.act(RSTD[:, 0:n], pn[:, 0:n], AF.Ln, bias=EPS6, scale=1.0 / 1024)
        P.act(RSTD[:, 0:n], RSTD[:, 0:n], AF.Exp, scale=-0.5)
        for kc in range(8):
            P.stt(outv[:, kc, :], xv[:, kc, :], pv(l, gname, kc), RSTD[:, 0:n], ALU.mult, ALU.mult,
                  eng=("dve" if kc % 2 == 0 else "pool"))
        if f32_cols is not None:
            cols, dst = f32_cols
            for ci, c in enumerate(cols):
                P.stt(dst[:, :, ci:ci + 1], xv[:, :, c:c + 1], 1.0, RSTD[:, c:c + 1].bcast(1, 8), ALU.mult, ALU.mult)
            for kc in range(8):
                P.ts(dst[:, kc, 0:len(cols)], dst[:, kc, 0:len(cols)], pv(l, gname, kc), ALU.mult)

    def dense8(lhs_fn, rhs_v, n, alloc=None):
        ps = (alloc or pd)()
        for kc in range(8):
            P.mm(ps[:, 0:n], lhs_fn(kc), rhs_v[:, kc, 0:n], start=(kc == 0), stop=(kc == 7))
        return ps

    def attention_block(l, j, qcols, ncur, kcur, kprev, vcur, vprev, mixcols, ebc, ebp, sample=False):
        nq = ncur
        W4 = 4 * nq
        etc = ET[rr["ev"] % 2]
        etp = ET[2 + rr["ev"] % 2]
        rr["ev"] += 1
        pa = bank(2)
        P.mm(pa[0:ncur, 0:W4], kcur, QBD[:, j, qcols].re("p c h q -> p (c h q)"))
        P.act(etc[0:ncur, 0:W4], pa[0:ncur, 0:W4], AF.Exp, scale=0.125)
        P.tt(etc[0:ncur, 0:W4].re("p (n q) -> p n q", n=4), etc[0:ncur, 0:W4].re("p (n q) -> p n q", n=4), ebc, ALU.mult, eng="pool")
        po = bank(4)
        pdn = bank(5)
        if kprev is not None:
            pb = bank(3)
            P.mm(pb[:, 0:W4], kprev, QBD[:, j, qcols].re("p c h q -> p (c h q)"))
            P.act(etp[:, 0:W4], pb[:, 0:W4], AF.Exp, scale=0.125)
            P.tt(etp[:, 0:W4].re("p (n q) -> p n q", n=4), etp[:, 0:W4].re("p (n q) -> p n q", n=4), ebp, ALU.mult, eng="pool")
            P.mm(po[:, 0:W4], vprev, etp[:, 0:W4], start=True, stop=False)
            P.mm(po[:, 0:W4], vcur, etc[0:ncur, 0:W4], start=False, stop=True)
            P.mm(pdn[:, 0:W4], ones, etp[:, 0:W4], start=True, stop=False)
            P.mm(pdn[:, 0:W4], ones[0:ncur, :], etc[0:ncur, 0:W4], start=False, stop=True)
        else:
            P.mm(po[:, 0:W4], vcur, etc[0:ncur, 0:W4])
            P.mm(pdn[:, 0:W4], ones[0:ncur, :], etc[0:ncur, 0:W4])
        finish_attn(l, j, po, pdn, nq, mixcols)

    def finish_attn(l, j, po, pdn, nq, mixcols):
        W4 = 4 * nq
        P.tt(RD[:, 0:W4].re("p (n q) -> p n q", n=4), pdn[:, 0:W4].re("p (n q) -> p n q", n=4),
             ES[:, 8 * l + 4 * j:8 * l + 4 * j + 4].bcast(2, nq), ALU.add)
        P.act(RD[:, 0:W4], RD[:, 0:W4], AF.Ln)
        P.act(RD[:, 0:W4], RD[:, 0:W4], AF.Exp, scale=-1.0)
        for hh in range(2):
            ps_ = slice(hh * 64, (hh + 1) * 64)
            src = po[ps_, 0:W4].re("p (c h q) -> p c h q", c=2, h=2)[:, :, hh, :]
            rdv = RD[ps_, 0:W4].re("p (c h q) -> p c h q", c=2, h=2)[:, :, hh, :]
            P.tt(MIX[ps_, 2 * j:2 * j + 2, mixcols], src, rdv, ALU.mult)

    def pbank():
        rr["h"] = (rr["h"] + 1) % 6
        return bank(2 + rr["h"])

    def c4(b_):
        return b_.re("p (c t) -> p c t", c=4)

    def F4(i, m):
        return Fb[i][:, 0:4 * m].re("p (a m) -> p a m", a=4)

    def B4(i, m):
        return Bb[i][:, 0:4 * m].re("p (a m) -> p a m", a=4)

    def pv4(l, name, m):
        base = {"mur": 16, "muk": 20, "muv": 24, "kk": 38, "ka": 42, "rk": 46, "gw": 50, "gb": 54}
        b0 = (NPV + 4 * l) if name == "omka" else 58 * l + base[name]
        return V(PVt.ap[:, b0:b0 + 4].unsqueeze(2).to_broadcast([128, 4, m]), PVt.keys)

    def run_streams(*gens, bg=None):
        gens = [g for g in gens if g is not None]
        nfg = len(gens)
        if bg is not None and not bg["done"]:
            gens.append(bg["gen"])
        ng = len(gens)
        done = [False] * ng
        hold = [False] * ng
        started = [False] * ng
        clock = [0.0] * ng
        if ng > nfg:
            clock[nfg] = bg["clock"]
        nread = 0
        while not all(done[:nfg]):
            cand = [i for i in range(ng) if not done[i] and not (hold[i] and nread > 0)]
            assert cand, "stream scheduler stuck"
            i = min(cand, key=lambda j: (clock[j], j))
            hold[i] = False
            n0 = len(P.ops)
            try:
                v = next(gens[i])
            except StopIteration:
                done[i] = True
                if i >= nfg:
                    bg["done"] = True
                if started[i]:
                    nread -= 1
                    started[i] = False
                continue
            if len(P.ops) > n0:
                clock[i] = max(P.fin[n0:])
                if i >= nfg:
                    bg["clock"] = clock[i]
            if v == "GSTART":
                nread += 1
                started[i] = True
            elif v == "GDONE":
                nread -= 1
                started[i] = False
            elif v == "BD":
                hold[i] = True

    def rwkv_gates(l, zc_fn, n, o=0):
        zwa = zc_fn(12)
        th = Bb[0][:, 0:n]
        zab = zwa
        P.act(th, zwa, AF.Tanh)
        for j in range(4):
            p1 = pd()
            P.mm(p1[:, 0:n], LR[:, l, j * 128:(j + 1) * 128], th)
            P.act(SG[:, j, o:o + n], p1[:, 0:n], AF.Sigmoid, bias=pv(l, "w0", j))
            p2 = pd()
            P.mm(p2[:, 0:n], LR[:, l, 512 + j * 128:512 + (j + 1) * 128], zab)
            P.act(AA[:, j, o:o + n], p2[:, 0:n], AF.Sigmoid, bias=pv(l, "a0", j))
        zg = zc_fn(13)
        sgz = Bb[1][:, 0:n]
        P.act(sgz, zg, AF.Sigmoid)
        for j in range(4):
            p1 = pd()
            P.mm(p1[:, 0:n], LR[:, l, 1024 + j * 128:1024 + (j + 1) * 128], sgz)
            P.copy(GATE[:, j, o:o + n], p1[:, 0:n], eng="act")

    def elem_gen(l, z4, sg4, aa4, m, CL, bs, msk):
        G = m // CL

        def g3(v):
            return v.re("p a (g t) -> p (a g) t", t=CL)
        R, Kraw, Vv, K = z4("r"), z4("k"), z4("v"), F4(2, m)
        KAP, T2, BETA = F4(4, m), F4(5, m), F4(6, m)
        BON = BONb[bs][:, 0:4 * m].re("p (a m) -> p a m", a=4)
        CS, CSX, CSH, ENEG, EPOS = F4(8, m), F4(9, m), F4(10, m), F4(11, m), F4(12, m)
        sqk = B4(2, m)
        P.tt(KAP, Kraw, pv4(l, "kk", m), ALU.mult)
        P.act(sqk, KAP, AF.Square)
        yield
        p1 = pd_prep()
        P.mm(p1[:, 0:4 * m], bones, sqk.re("p a m -> p (a m)"))
        P.ts(T2.re("p a m -> p (a m)"), p1[:, 0:4 * m], 1e-24, ALU.max)
        yield
        P.act(T2, T2, AF.Ln)
        P.act(T2, T2, AF.Exp, scale=-0.5)
        yield
        P.tt(KAP, KAP, T2, ALU.mult)
        P.tt(T2, aa4, pv4(l, "ka", m), ALU.mult, eng="pool")
        yield
        P.tt(T2, T2, pv4(l, "omka", m), ALU.add, eng="pool")
        P.tt(K, Kraw, T2, ALU.mult)
        yield
        P.stt(BETA, KAP, -1.0, aa4, ALU.mult, ALU.mult)
        rk = B4(3, m)
        P.tt(T2, R, pv4(l, "rk", m), ALU.mult, eng="pool")
        yield
        P.tt(rk, T2, K, ALU.mult)
        p2 = pd_prep()
        P.mm(p2[:, 0:4 * m], bones, rk.re("p a m -> p (a m)"))
        yield
        P.tt(BON, p2[:, 0:4 * m].re("p (a m) -> p a m", a=4), Vv, ALU.mult)
        SGc = F4(7, m)
        P.copy(SGc, sg4, eng="act")
        yield
        csf, sgf, mkf = CS.re("p a m -> p (a m)"), SGc.re("p a m -> p (a m)"), msk[:, 0:4 * m]
        P.add("dve", lambda e, o=csf, m_=mkf, s_=sgf: e.tensor_tensor_scan(o.ap, m_.ap, s_.ap, 0.0, ALU.mult, ALU.add),
              [mkf, sgf], [csf])
        yield
        P.tt(CSX, CS, SGc, ALU.subtract, eng="pool")
        last = g3(CS)[:, :, CL - 1:CL].re("p g o -> p (g o)")
        P.tt(g3(CSH), last.bcast(2, CL), g3(CS), ALU.subtract, eng="pool")
        yield
        P.act(CSX, CSX, AF.Exp, scale=-C0)
        P.act(ENEG, CS, AF.Exp, scale=C0)
        yield
        P.act(EPOS, CS, AF.Exp, scale=-C0)
        P.act(CSH, CSH, AF.Exp, scale=-C0)
        P.act(PCt2[bs][:, 0:4 * G], last, AF.Exp, scale=-C0)
        yield

    def bd_gen(m, CL, cols, rs, z4):
        R, K, Vv = z4("r"), F4(2, m), z4("v")
        KAP, BETA = F4(4, m), F4(6, m)
        CSX, CSH, ENEG, EPOS = F4(9, m), F4(10, m), F4(11, m), F4(12, m)
        yield "BD"
        for hh in range(2):
            ps_ = slice(hh * 64, (hh + 1) * 64)
            eng = "dve" if hh == 0 else "pool"

            def sv(x):
                return x[ps_, :, cols]
            P.tt(ATb[ps_, :, hh, 0:CL], sv(KAP), sv(CSX), ALU.mult, eng=eng)
            P.tt(RT4[rs][ps_, :, hh, 0:CL], sv(R), sv(EPOS), ALU.mult, eng=eng)
            yield
            P.tt(BTb[ps_, :, hh, 0:CL], sv(BETA), sv(ENEG), ALU.mult, eng=eng)
            P.tt(KTb[ps_, :, hh, 0:CL], sv(K), sv(ENEG), ALU.mult, eng=eng)
            yield
            P.tt(BHb[ps_, :, hh, 0:CL], sv(BETA), sv(CSH), ALU.mult, eng=eng)
            P.tt(KHb[ps_, :, hh, 0:CL], sv(K), sv(CSH), ALU.mult, eng=eng)
            yield
            P.copy(VVb[ps_, :, hh, 0:CL], sv(Vv), eng="act")
        yield

    def ph1_gen(k, rs):
        yield "GSTART"
        atv = [ATb[:, p].re("p h t -> p (h t)") for p in range(4)]
        rtv = [RT4[rs][:, p].re("p h t -> p (h t)") for p in range(4)]
        btv = [BTb[:, p].re("p h t -> p (h t)") for p in range(4)]
        ktv = [KTb[:, p].re("p h t -> p (h t)") for p in range(4)]
        for half_ in range(2):
            ptr = pb1()
            ptb = ptr.bc(BF16)
            for c_ in range(2):
                p = 2 * half_ + c_
                for qi, src in enumerate((atv[p], BHb[:, p].re("p h t -> p (h t)"), KHb[:, p].re("p h t -> p (h t)"),
                                          VVb[:, p].re("p h t -> p (h t)"))):
                    P.tr(ptb[:, (c_ * 4 + qi) * 128:(c_ * 4 + qi + 1) * 128], src, ident)
                yield
            P.copy(TM4pairs[k][half_], ptb, eng="act")
        def gprod(lf, rf):
            b_ = pb1()
            for p in range(4):
                P.mm(b_[:, p * 128:(p + 1) * 128], lf[p], rf[p])
            return b_
        bE = gprod(atv, btv)
        P.tt(NT4[0], c4(bE), MLs_b, ALU.mult)
        yield
        bA = gprod(btv, atv)
        P.tt(N4, c4(bA), MUs_b, ALU.mult)
        P.tt(S4[0], N4, ID_b, ALU.add)
        yield
        bC = gprod(ktv, atv)
        P.tt(AAK4, c4(bC), MUs_b, ALU.mult)
        yield
        bB = gprod(btv, rtv)
        P.tt(MRB4s[k], c4(bB), MUi_b, ALU.mult)
        yield
        bD = gprod(ktv, rtv)
        yield "GDONE"
        P.tt(MRK4s[k], c4(bD), MUi_b, ALU.mult)
        yield
        Nk, NTk, Sk = N4, NT4[0], S4[0]
        for lvl in range(5):
            NTn = NT4[(lvl + 1) % 2]
            q1 = pb1()
            for p in range(4):
                P.mm(q1[:, p * 128:(p + 1) * 128], Nk[:, p], NTk[:, p])
            P.copy(NTn, c4(q1), eng="act")
            yield
            if lvl < 4:
                Nn = NN4[lvl % 2]
                q2 = pb1()
                for p in range(4):
                    P.mm(q2[:, p * 128:(p + 1) * 128], NTk[:, p], Nk[:, p])
                P.copy(Nn, c4(q2), eng="act")
                yield
            Sn = S4[(lvl + 1) % 2]
            q3 = pb1()
            for p in range(4):
                P.mm(q3[:, p * 128:(p + 1) * 128], NTn[:, p], Sk[:, p])
            P.tt(Sn, c4(q3), Sk, ALU.add)
            yield
            NTk, Sk = NTn, Sn
            if lvl < 4:
                Nk = Nn
        q4 = pb1()
        q5 = pb1()
        for p in range(4):
            P.mm(q5[:, p * 128:(p + 1) * 128], AAK4[:, p], TM4s[k][p][:, 3])
        P.copy(W14, c4(q5), eng="act")
        yield
        for p in range(4):
            P.mm(q4[:, p * 128:(p + 1) * 128], TM4s[k][p][:, 0], Sk[:, p])
        P.copy(APT4s[k], c4(q4), eng="act")
        yield
        q6 = pb1()
        for p in range(4):
            P.mm(q6[:, p * 128:(p + 1) * 128], Sk[:, p], W14[:, p])
        P.copy(UP4s[k], c4(q6), eng="act")
        yield
    def ph2_gen(l, CL, k, rs, pc4, H32v, hb_old, hb_new, ot_dst, gn=None):
        rtv = [RT4[rs][:, p].re("p h t -> p (h t)") for p in range(4)]
        P.tt(TMPH, H32v, pc4.bcast(2, 128), ALU.mult, eng="pool")
        q7 = pb2()
        for p in range(4):
            P.mm(q7[:, p * 128:(p + 1) * 128], APT4s[k][:, p], hb_old[:, p])
        P.tt(U4, c4(q7), UP4s[k], ALU.add)
        yield
        q9 = pb2()
        for p in range(4):
            P.mm(q9[:, p * 128:(p + 1) * 128], TM4s[k][p][:, 2], TM4s[k][p][:, 3], start=True, stop=False)
            P.mm(q9[:, p * 128:(p + 1) * 128], TM4s[k][p][:, 1], U4[:, p], start=False, stop=True)
        yield
        P.tt(hb_new, TMPH, c4(q9), ALU.add)
        P.tt(H32v, TMPH, c4(q9), ALU.add)
        yield
        q8 = pb2()
        for p in range(4):
            cs_ = slice(p * 128, (p + 1) * 128)
            P.mm(q8[:, cs_], hb_old[:, p], rtv[p], start=True, stop=False)
            P.mm(q8[:, cs_], U4[:, p], MRB4s[k][:, p], start=False, stop=False)
            P.mm(q8[:, cs_], TM4s[k][p][:, 3], MRK4s[k][:, p], start=False, stop=True)
        yield
        for hh in range(2):
            ps_ = slice(hh * 64, (hh + 1) * 64)
            P.copy(ot_dst[ps_], c4(q8)[ps_, :, hh * 64:hh * 64 + CL], eng="act")
        yield
        if gn is not None:
            for _ in gn():
                yield

    def gn_gen(l, m, bon4, gate4, mixv):
        otv = OT[:, 0:4 * m]
        ob = Bb[4][:, 0:4 * m]
        P.copy(ob, otv, eng="act")
        p1 = pb2()
        P.mm(p1[:, 0:4 * m], bones, ob)
        yield
        CEN = Fb[13][:, 0:4 * m]
        P.stt(CEN, p1[:, 0:4 * m], -1.0 / 64, otv, ALU.mult, ALU.add)
        sq = Bb[5][:, 0:4 * m]
        P.act(sq, CEN, AF.Square)
        yield
        p2 = pb2()
        P.mm(p2[:, 0:4 * m], bones, sq)
        RS = Fb[14][:, 0:4 * m]
        P.act(RS, p2[:, 0:4 * m], AF.Ln, bias=GNE, scale=1.0 / 64)
        yield
        P.act(RS, RS, AF.Exp, scale=-0.5)
        P.tt(CEN, CEN, RS, ALU.mult)
        yield
        C3 = CEN.re("p (a m) -> p a m", a=4)
        P.tt(C3, C3, pv4(l, "gw", m), ALU.mult, eng="pool")
        P.tt(C3, C3, pv4(l, "gb", m), ALU.add, eng="pool")
        yield
        P.tt(C3, C3, bon4, ALU.add, eng="pool")
        P.tt(mixv, C3, gate4, ALU.mult)
        yield

    def out_ffn(l, n):
        for oc in range(8):
            if oc % 4 == 0:
                w = wget()
            o = (oc % 4) * 128
            ps = dense8(lambda kc: w[:, kc, o:o + 128], MIX, n)
            P.tt(X[:, oc, 0:n], ps[:, 0:n], X[:, oc, 0:n], ALU.add)
        if debug and dbgs["on"]:
            P.dma(dv(dbg_x1.ap(), "dbg_x1"), X.re("p a t -> p (a t)"), "dbgx1")
        rmsnorm(X[:, :, 0:n], n, "n2", l, HB[:, :, 0:n])
        if debug and dbgs["on"]:
            P.dma(dv(dbg_hf.ap(), "dbg_hf"), HB.re("p a t -> p (a t)"), "dbghf")
        for b in range(11):
            w = wget()
            for jj in range(2):
                j = 2 * b + jj
                pg_ = dense8(lambda kc: w[:, kc, jj * 256:jj * 256 + 128], HB, n)
                pu_ = dense8(lambda kc: w[:, kc, jj * 256 + 128:jj * 256 + 256], HB, n)
                sg_ = RD[:, 0:n]
                P.act(sg_, pg_[:, 0:n], AF.Silu)
                P.tt(SCR[:, j, 0:n].k(j), pu_[:, 0:n], sg_, ALU.mult)
        for oc in range(8):
            w = wget().re("p (j m) -> p j m", j=22)
            ps = pd()
            for j in range(22):
                P.mm(ps[:, 0:n], w[:, j, :], SCR[:, j, 0:n].k(j), start=(j == 0), stop=(j == 21))
            P.tt(X[:, oc, 0:n], ps[:, 0:n], X[:, oc, 0:n], ALU.add)

    def in_proj_gen(l, n, next_, evac, blks, alloc=None):
        for blk in blks:
            w = wget()
            for off in range(4):
                c = 4 * blk + off
                if c >= 21:
                    break
                role = c + 7 if c < 14 else c - 14
                ncol = n if role < 7 else next_
                ps = dense8(lambda kc: w[:, kc, off * 128:(off + 1) * 128], HB, ncol, alloc=alloc)
                evac(role, ps)
                yield

    def in_proj(l, n, next_, evac, blks=range(6)):
        for _ in in_proj_gen(l, n, next_, evac, blks):
            pass

    okv = {"i": 0}

    def kv_out(src32, ncols, dst_ap, name, alloc=None):
        pt = (alloc or pd)()
        P.tr(pt[0:ncols, 0:128], src32[:, 0:ncols], ident32)
        st = KVO[okv["i"] % 2]
        P.copy(st[0:ncols, :], pt[0:ncols, 0:128], eng="act")
        P.dma(V(dst_ap, (name,)), st[0:ncols, :], "kvo%d" % (okv["i"] % 2))
        okv["i"] += 1

    def sample_tile():
        n = 64
        for t_ in (ATb, RT4[0], RT4[1], RT4[2], BTb, KTb, BHb, KHb, VVb):
            P.memset(t_, 0.0)
        P.dma(X[:, :, 0:n], dv(xsT.ap().rearrange("(kc p) t -> p kc t", p=128), "xsT"), "x")
        for l in range(nlayers):
            rmsnorm(X[:, :, 0:n], n, "n1", l, HB[:, :, 0:n], f32_cols=([4 * b + 3 for b in range(16)], HL32))
            P.dma(dv(shs.ap()[l].rearrange("(kc p) b -> p kc b", p=128), "shs"), HL32, "hl")
            P.dma(HB[:, :, 64:80], dv(shT.ap()[l].rearrange("(kc p) b -> p kc b", p=128), "shT"), "sh", eng="pool")
            CK = vjoin(SCR.ap[:, 0:8, :].rearrange("p a t -> p (a t)"), [SCR.k(i) for i in range(8)])
            CVv = vjoin(SCR.ap[:, 8:16, :].rearrange("p a t -> p (a t)"), [SCR.k(i) for i in range(8, 16)])
            P.dma(CK, dv(ckT.ap()[l], "ckT"), "ck", eng="pool")
            P.dma(CVv, dv(cvd.ap()[l], "cvd"), "cv", eng="pool")
            CK4 = CK.re("p (b j s) -> p b j s", b=16, j=2)
            CV4 = CVv.re("p (b j s) -> p b j s", b=16, j=2)
            for src_, dst_, nm_ in ((ckr, ks, "ks"), (cvr, vs, "vs")):
                sv = src_.ap()[l].rearrange("b (s f) -> s b f", f=128)[4:128]
                dvv = dst_.ap()[l].rearrange("b (s f) -> s b f", f=128)[0:124]
                for hb_ in range(2):
                    P.dma(CPB[0:124], dv(sv[:, 8 * hb_:8 * hb_ + 8, :], nm_ + "src"), "cpb")
                    P.dma(dv(dvv[:, 8 * hb_:8 * hb_ + 8, :], nm_), CPB[0:124], "cpb")

            def evac(c, ps):
                if c < 4:
                    for hh in range(2):
                        ps_ = slice(hh * 64, (hh + 1) * 64)
                        P.copy(QBP[ps_, c // 2, :, c % 2, hh, :], ps[ps_, 0:n].re("p (b t) -> p b t", t=4), eng="act")
                elif c < 6:
                    j = c - 4
                    P.copy(KD[2 * l + j][:, 128:128 + n], ps[:, 0:n], eng="dve")
                    P.copy(KO32[j * 64:(j + 1) * 64, 0:n].re("p (t b) -> p b t", t=4),
                           ps[j * 64:(j + 1) * 64, 0:n].re("p (b t) -> p b t", t=4), eng="dve")
                elif c == 6:
                    P.copy(VT[:, 0:n], ps[:, 0:n], eng="dve")
                    P.copy(VO32[:, 0:n].re("p (t b) -> p b t", t=4), ps[:, 0:n].re("p (b t) -> p b t", t=4), eng="dve")
                else:
                    rc = c - 7
                    P.act(ZRW[:, rc, 0:64], ps[:, 0:64], AF.Copy, scale=pv(l, "omu", rc))
                    zv = ZRW[:, rc, 0:64].re("p (b t) -> p b t", t=4)
                    pv_ = ps[:, 0:64].re("p (b t) -> p b t", t=4)
                    P.stt(zv[:, :, 1:4], pv_[:, :, 0:3], pv(l, "mu", rc), zv[:, :, 1:4], ALU.mult, ALU.add)
                    P.stt(zv[:, :, 0:1], ps[:, 64:80].re("p (b o) -> p b o", o=1), pv(l, "mu", rc), zv[:, :, 0:1], ALU.mult, ALU.add)
            in_proj(l, n, 80, evac)
            for src32, dst, nm in ((KO32, ks, "ks"), (VO32, vs, "vs")):
                pt = pd()
                P.tr(pt[0:64, 0:128], src32[:, 0:64], ident32)
                st = KVO[okv["i"] % 2]
                P.copy(st[0:64, :], pt[0:64, 0:128], eng="act")
                d3 = dst.ap()[l].rearrange("b (s f) -> b s f", f=128)
                for t_ in range(4):
                    P.dma(V(d3[:, 124 + t_, :], (nm,)), st[16 * t_:16 * t_ + 16, :], "kvo%d" % (okv["i"] % 2))
                okv["i"] += 1
            pt = ph()
            ptb = pt.bc(BF16)
            P.tr(ptb[0:64, 0:128], VT[:, 0:64], ident)
            for j in range(2):
                P.copy(VDS[:, j], ptb[0:64, j * 64:(j + 1) * 64].bcast(1, 2), eng="dve")
            for j in range(2):
                etp, etc = ET[2], ET[0]
                pa = bank(3)
                for b in range(16):
                    P.mm(pa[:, b * 16:b * 16 + 16], CK4[:, b, j, :], QBP[:, j, b].re("p c h t -> p (c h t)"), skip=True)
                P.act(etp[:, 0:256], pa[:, 0:256], AF.Exp, scale=0.125)
                ebp = V(EB.ap[:, 1, 4 * j:4 * j + 4, 0:4].unsqueeze(1).to_broadcast([128, 16, 4, 4]), EB.keys)
                e4 = etp[:, 0:256].re("p (b n t) -> p b n t", b=16, n=4)
                P.tt(e4, e4, ebp, ALU.mult, eng="pool")
                pb = bank(2)
                P.mm(pb[0:64, 0:256], KD[2 * l + j][:, 128:192], QBP[:, j].re("p b c h t -> p (b c h t)"))
                P.act(etc[0:64, 0:256], pb[0:64, 0:256], AF.Exp, scale=0.125)
                e3 = etc[0:64, 0:256].re("p (b n t) -> p b n t", b=16, n=4)
                P.tt(e3, e3, EBSC[:, j], ALU.mult, eng="pool")
                po = bank(4)
                pdn = bank(5)
                vds = VDS[:, j].re("p a d -> p (a d)")
                P.mm(po[:, 0:256], vds[0:32], etc[0:32, 0:256], start=True, stop=False, skip=True)
                for b in range(16):
                    P.mm(po[:, b * 16:b * 16 + 16], CV4[:, b, j, :], etp[:, b * 16:b * 16 + 16], start=False, stop=False, skip=True)
                P.mm(po[:, 0:256], vds[32:64], etc[32:64, 0:256], start=False, stop=True, skip=True)
                P.mm(pdn[:, 0:256], ones[0:64, :], etc[0:64, 0:256], start=True, stop=False)
                P.mm(pdn[:, 0:256], ones, etp[:, 0:256], start=False, stop=True)
                P.tt(RD[:, 0:256].re("p (b n t) -> p b n t", b=16, n=4), pdn[:, 0:256].re("p (b n t) -> p b n t", b=16, n=4),
                     V(ES.ap[:, 8 * l + 4 * j:8 * l + 4 * j + 4].unsqueeze(1).unsqueeze(3).to_broadcast([128, 16, 4, 4]), ES.keys), ALU.add)
                P.act(RD[:, 0:256], RD[:, 0:256], AF.Ln)
                P.act(RD[:, 0:256], RD[:, 0:256], AF.Exp, scale=-1.0)
                for hh in range(2):
                    ps_ = slice(hh * 64, (hh + 1) * 64)
                    src = po[ps_, 0:256].re("p (b c h t) -> p c h b t", b=16, c=2, h=2)[:, :, hh]
                    rdv = RD[ps_, 0:256].re("p (b c h t) -> p c h b t", b=16, c=2, h=2)[:, :, hh]
                    P.tt(MIX[ps_, 2 * j:2 * j + 2, 0:64].re("p c (b t) -> p c b t", t=4), src, rdv, ALU.mult)
            cs64 = slice(0, 64)
            rwkv_gates(l, lambda rc: ZRW[:, rc, cs64], 64)
            zoff = {"r": 0, "k": 4, "v": 8}
            for _ in elem_gen(l, lambda kind: ZRW[:, zoff[kind]:zoff[kind] + 4, 0:64],
                              SG[:, :, 0:64], AA[:, :, 0:64], 64, 4, 0, M4):
                pass
            pc_all = PCt2[0].re("p (a b) -> p a b", a=4)
            ot4 = OT[:, 0:256].re("p (a m) -> p a m", a=4)

            def load_state(b):
                P.dma(HS32[b % 4], dv(hbd.ap()[l, b].rearrange("p (a v) -> p a v", a=4), "hbd"), "hs%d" % (b % 4))

            def ph2S(b):
                hb_old, hb_new = HBF4[b % 2], HBF4[1 - b % 2]
                P.copy(hb_old, HS32[b % 4], eng="dve")
                for v_ in ph2_gen(l, 4, b % 2, b % 3, pc_all[:, :, b], HS32[b % 4], hb_old, hb_new,
                                  ot4[:, :, 4 * b:4 * b + 4]):
                    yield v_
                P.dma(dv(wkvs.ap()[l, b].rearrange("p (a v) -> p a v", a=4), "wkvs"), HS32[b % 4], "hs%d" % (b % 4))
                if b + 3 < 16:
                    load_state(b + 3)
                yield
            for b in range(3):
                load_state(b)
            zsel = lambda kind: ZRW[:, zoff[kind]:zoff[kind] + 4, 0:64]
            for t in range(16 + 2):
                run_streams(ph2S(t - 2) if 0 <= t - 2 < 16 else None,
                            ph1_gen((t - 1) % 2, (t - 1) % 3) if 0 <= t - 1 < 16 else None,
                            bd_gen(64, 4, slice(4 * t, 4 * t + 4), t % 3, zsel) if t < 16 else None)
            bon4 = BONb[0][:, 0:256].re("p (a m) -> p a m", a=4)
            for _ in gn_gen(l, 64, bon4, GATE[:, :, 0:64], MIX[:, 4:8, 0:64]):
                pass
            if debug and l == 0:
                P.dma(dv(dbg_mixs.ap(), "dbg_mixs"), MIX[:, :, 0:64], "dbgms")
            out_ffn(l, n)
        rmsnorm(X[:, :, 0:n], n, "nf", 0, YO[:, :, 0:n])
        P.dma(dv(ysT.ap().rearrange("(kc p) t -> p kc t", p=128), "ysT"), YO[:, :, 0:n], "yo")

    def prompt_tile(s, ti):
        n = TT
        last = (ti == 2048 // TT - 1)
        cols = slice(ti * TT, (ti + 1) * TT)
        P.dma(X, dv(xT.ap()[s].rearrange("(kc p) t -> p kc t", p=128)[:, :, cols], "xT"), "x")
        for l in range(nlayers):
            if last:
                rmsnorm(X, n, "n1", l, HB, f32_cols=([n - 1], HL32))
                P.dma(dv(shp.ap()[l, s].rearrange("kc p -> p kc"), "shp"), HL32[:, :, 0:1].re("p k o -> p (k o)"), "hl",
                      allow_slow_non_contiguous=True)
            else:
                rmsnorm(X, n, "n1", l, HB)
            if ti == 0:
                P.memset(ZC[l], 0.0)
                P.memset(H32[l], 0.0)

            def evac(c, ps):
                if c < 4:
                    for hh in range(2):
                        ps_ = slice(hh * 64, (hh + 1) * 64)
                        P.copy(QBD[ps_, c // 2, :, c % 2, hh, :], ps[ps_, 0:n].re("p (b q) -> p b q", q=128), eng="act")
                elif c < 6:
                    j = c - 4
                    P.copy(KD[2 * l + j][:, 128:128 + n], ps[:, 0:n], eng="dve" if last else "act")
                    if last:
                        P.copy(KO32[j * 64:(j + 1) * 64, :], ps[j * 64:(j + 1) * 64, n - 128:n], eng="dve")
                elif c == 6:
                    P.copy(VT[:, 0:n], ps[:, 0:n], eng="dve" if last else "act")
                    if last:
                        P.copy(VO32, ps[:, n - 128:n], eng="dve")
                else:
                    rc = c - 7
                    P.act(ZRW[:, rc, 0:n], ps[:, 0:n], AF.Copy, scale=pv(l, "omu", rc))
                    P.stt(ZRW[:, rc, 1:n], ps[:, 0:n - 1], pv(l, "mu", rc), ZRW[:, rc, 1:n], ALU.mult, ALU.add)
                    P.stt(ZRW[:, rc, 0:1], ZC[l][:, rc:rc + 1], pv(l, "mu", rc), ZRW[:, rc, 0:1], ALU.mult, ALU.add)
                    P.copy(ZC[l][:, rc:rc + 1], ps[:, n - 1:n], eng="dve")
            in_proj(l, n, n, evac, blks=range(4))

            def att_gen():
                for v_ in in_proj_gen(l, n, n, evac, (4, 5), alloc=pba):
                    yield v_
                if last:
                    kv_out(KO32, 128, kp.ap()[l, s], "kp", alloc=pba)
                    yield
                    kv_out(VO32, 128, vp.ap()[l, s], "vp", alloc=pba)
                    yield
                for i in range(4):
                    gb = ti * 4 + i
                    pt = pba()
                    ptb = pt.bc(BF16)
                    P.tr(ptb[:, 0:128], VT[:, i * 128:(i + 1) * 128], ident)
                    for j in range(2):
                        P.copy(VD[l][:, gb % 5, j], ptb[:, j * 64:(j + 1) * 64].bcast(1, 2), eng="dve")
                    yield
                un = 0
                for i in range(4):
                    gb = ti * 4 + i
                    bc = slice(i * 128, (i + 1) * 128)
                    for j in range(2):
                        kd = KD[2 * l + j]
                        kcur = kd[:, 128 + i * 128:128 + (i + 1) * 128]
                        kprev = kd[:, i * 128:(i + 1) * 128]
                        vcur = VD[l][:, gb % 5, j].re("p a d -> p (a d)")
                        vprev = VD[l][:, (gb - 1) % 5, j].re("p a d -> p (a d)")
                        for c2 in range(2):
                            h0 = 4 * j + 2 * c2
                            E = ETS[un % 2]
                            rd = RD[:, (un % 2) * 256:(un % 2) * 256 + 256]
                            un += 1
                            qv = QBD[:, j, i, c2].re("p h q -> p (h q)")
                            W = 512 if gb > 0 else 256
                            pa = pba()
                            P.mm(pa[:, 0:256], kcur, qv)
                            if gb > 0:
                                P.mm(pa[:, 256:512], kprev, qv)
                            P.act(E[:, 0:W], pa[:, 0:W], AF.Exp, scale=0.125)
                            yield
                            P.tt(E[:, 0:256].re("p (h q) -> p h q", h=2), E[:, 0:256].re("p (h q) -> p h q", h=2),
                                 EB[:, 0, h0:h0 + 2, :], ALU.mult, eng="pool")
                            if gb > 0:
                                P.tt(E[:, 256:512].re("p (h q) -> p h q", h=2), E[:, 256:512].re("p (h q) -> p h q", h=2),
                                     EB[:, 1, h0:h0 + 2, :], ALU.mult, eng="pool")
                            yield
                            po = pba()
                            if gb > 0:
                                P.mm(po[:, 0:256], vprev, E[:, 256:512], start=True, stop=False)
                                P.mm(po[:, 0:256], vcur, E[:, 0:256], start=False, stop=True)
                                P.mm(po[:, 256:512], ones, E[:, 256:512], start=True, stop=False)
                                P.mm(po[:, 256:512], ones, E[:, 0:256], start=False, stop=True)
                            else:
                                P.mm(po[:, 0:256], vcur, E[:, 0:256])
                                P.mm(po[:, 256:512], ones, E[:, 0:256])
                            P.tt(rd.re("p (h q) -> p h q", h=2), po[:, 256:512].re("p (h q) -> p h q", h=2),
                                 ES[:, 8 * l + h0:8 * l + h0 + 2].bcast(2, 128), ALU.add)
                            yield
                            P.act(rd, rd, AF.Ln)
                            P.act(rd, rd, AF.Exp, scale=-1.0)
                            yield
                            for hh in range(2):
                                ps_ = slice(hh * 64, (hh + 1) * 64)
                                P.tt(MIXc(2 * j + c2, 2 * j + c2 + 1)[ps_, 0, bc], po[ps_, hh * 128:(hh + 1) * 128],
                                     rd[ps_, hh * 128:(hh + 1) * 128], ALU.mult)
                            yield
                for j in range(2):
                    kd = KD[2 * l + j]
                    P.copy(kd[:, 0:128], kd[:, n:n + 128], eng="pool")
                yield
            att = {"gen": att_gen(), "clock": 0.0, "done": False}
            P.copy(HBF4[0], H32[l], eng="dve")
            zoff = {"r": 0, "k": 4, "v": 8}
            for g in range(2):
                rwkv_gates(l, lambda rc: ZRW[:, rc, 256 * g:256 * g + 256], 256, o=256 * g)

            def zs(c):
                return lambda kind: ZRW[:, zoff[kind]:zoff[kind] + 4, 64 * c:64 * c + 64]

            def prepS(c):
                cc = slice(64 * c, 64 * c + 64)
                for v_ in elem_gen(l, zs(c), SG[:, :, cc], AA[:, :, cc], 64, 64, c % 3, M64):
                    yield v_
                for v_ in bd_gen(64, 64, slice(0, 64), c % 3, zs(c)):
                    yield v_

            def ph2S(c):
                cc = slice(64 * c, 64 * c + 64)
                bon4 = BONb[c % 3][:, 0:256].re("p (a m) -> p a m", a=4)
                gn = lambda: gn_gen(l, 64, bon4, GATE[:, :, cc], MIXc(4, 8)[:, :, cc])
                return ph2_gen(l, 64, c % 2, c % 3, PCt2[c % 3][:, 0:4], H32[l], HBF4[c % 2], HBF4[1 - c % 2],
                               OT[:, 0:256].re("p (a m) -> p a m", a=4), gn=gn)
            NCH = n // 64
            for t in range(NCH + 2):
                run_streams(ph2S(t - 2) if 0 <= t - 2 < NCH else None,
                            ph1_gen((t - 1) % 2, (t - 1) % 3) if 0 <= t - 1 < NCH else None,
                            prepS(t) if t < NCH else None, bg=att)
            for _ in att["gen"]:
                pass
            if last:
                P.dma(dv(wkvp.ap()[l, s].rearrange("p (a v) -> p a v", a=4), "wkvp"), H32[l], "wkvp")
            if debug and s == 0 and ti == 0 and l == 0:
                P.dma(dv(dbg_mix.ap(), "dbg_mix"), MIX.re("p a t -> p (a t)"), "dbgm")
                P.dma(dv(dbg_z.ap(), "dbg_z"), ZRW.re("p a t -> p (a t)"), "dbgz")
                dbgs["on"] = True
            out_ffn(l, n)
            if debug and dbgs["on"]:
                P.dma(dv(dbg_x2.ap(), "dbg_x2"), X.re("p a t -> p (a t)"), "dbgx2")
                dbgs["on"] = False
        rmsnorm(X, n, "nf", 0, YO)
        P.dma(dv(yT.ap()[s].rearrange("(kc p) t -> p kc t", p=128)[:, :, cols], "yT"), YO, "yo")

    for t in range(n_ptiles):
        prompt_tile(t // (2048 // TT), t % (2048 // TT))
    if do_sample:
        sample_tile()
    P.emit()
    return nc, P


def _t5_bucket(d):
    n = np.maximum(d, 0)
    large = 16 + (np.log(np.maximum(n, 1) / 16) / np.log(128 / 16) * 16).astype(np.int32)
    large = np.minimum(large, 31)
    return np.where(n < 16, n, large).astype(np.int32)


def _constants():
    i = np.arange(128)
    hs, ss = i // 64, i % 64
    same = hs[:, None] == hs[None, :]
    ident = np.eye(128, dtype=np.float32)
    ones = np.ones((128, 128), np.float32)
    bones = same.astype(np.float32)
    MUs = (same & (ss[:, None] < ss[None, :])).astype(np.float32)
    MUi = (same & (ss[:, None] <= ss[None, :])).astype(np.float32)
    MLs = (same & (ss[:, None] > ss[None, :])).astype(np.float32)
    bd4 = np.zeros((128, 64), np.float32)
    j = np.arange(64)
    bd4[:64] = (j[:, None] // 4 == j[None, :] // 4).astype(np.float32)
    m64 = np.ones(256, np.float32)
    m64[::64] = 0
    m4 = np.ones(256, np.float32)
    m4[::4] = 0
    cbf = np.concatenate([ident, ones, bones, MUs, MUi, MLs, bd4, np.tile(m64[None], (128, 1)), np.tile(m4[None], (128, 1))], 1)
    c32 = np.concatenate([ident, ident[:, ::-1]], 1).astype(np.float32)
    ohv = np.zeros((33, 512), np.float32)
    for ty in range(2):
        for ii in range(256):
            if ty == 0:
                d = 127 - ii
                valid = (0 <= d <= 127)
            else:
                d = 255 - ii
                valid = (1 <= d <= 128)
            col = ty * 256 + (254 - ii if ii < 255 else 255)
            if valid:
                ohv[_t5_bucket(np.array(d))[()], col] = 1.0
            else:
                ohv[32, col] = 1.0
    return cbf, c32, ohv


_CACHE = {}


def kernel(x_prompt, x_sample, cache_k, cache_v, state_shift, state_wkv, norm1, w_in, w_out, sink,
           rel_table, mu, w0, w2, a0, a2, g2, k_k, k_a, r_k, gn_w, gn_b, norm2, w_gate, w_up,
           w_down, norm_f):
    f = lambda a: np.ascontiguousarray(np.asarray(a, dtype=np.float32))
    x_prompt, x_sample, cache_k, cache_v, state_shift, state_wkv = map(f, (x_prompt, x_sample, cache_k, cache_v, state_shift, state_wkv))
    norm1, w_in, w_out, sink, rel_table, mu, w0, w2, a0, a2, g2 = map(f, (norm1, w_in, w_out, sink, rel_table, mu, w0, w2, a0, a2, g2))
    k_k, k_a, r_k, gn_w, gn_b, norm2, w_gate, w_up, w_down, norm_f = map(f, (k_k, k_a, r_k, gn_w, gn_b, norm2, w_gate, w_up, w_down, norm_f))
    nc = _CACHE.get("nc_override")
    if nc is None:
        if "nc" not in _CACHE:
            _CACHE["nc"] = build_program()[0]
        nc = _CACHE["nc"]
    cbf, c32, ohv = _constants()
    q_, k_, v_, rw_ = w_in[:, :, 0:512], w_in[:, :, 512:640], w_in[:, :, 640:768], w_in[:, :, 768:]
    w_in_aug = np.concatenate([rw_, q_, k_[:, :, 0:64], k_[:, :, 0:64], k_[:, :, 64:128], k_[:, :, 64:128], v_], 2)
    w_gu = np.stack([w_gate.reshape(2, 1024, 22, 128), w_up.reshape(2, 1024, 22, 128)], 3).reshape(2, 1024, 5632)
    w_dn = w_down.reshape(2, 22, 128, 8, 128).transpose(0, 3, 2, 1, 4).reshape(2, 8, 128, 2816)
    lora = np.zeros((2, 128, 1536), np.float32)
    lora[:, 0:64, 0:512] = w2
    lora[:, 64:128, 512:1024] = a2
    lora[:, :, 1024:1536] = g2
    pvec = np.zeros((128, NPV), np.float32)

    def cols(vec, nchunk):
        return vec.reshape(nchunk, 128).T

    for l in range(2):
        b0 = 58 * l
        pvec[:, b0:b0 + 8] = cols(norm1[l], 8)
        pvec[:, b0 + 8:b0 + 16] = cols(norm2[l], 8)
        pvec[:, b0 + 16:b0 + 30] = cols(mu[l], 14)
        for nm, off in ((w0, 30), (a0, 34), (k_k, 38), (k_a, 42), (r_k, 46), (gn_w, 50), (gn_b, 54)):
            pvec[:, b0 + off:b0 + off + 4] = cols(nm[l], 4)
    pvec[:, 116:124] = cols(norm_f, 8)
    shared = dict(w_in=np.ascontiguousarray(w_in_aug), w_out=w_out, w_gu=np.ascontiguousarray(w_gu),
                  w_dn=np.ascontiguousarray(w_dn), lora=lora, pvec=pvec, rel=rel_table,
                  sink=np.ascontiguousarray(sink.reshape(1, 16)), cbf=cbf, c32=c32, ohv=ohv)
    in_maps = []
    for c in range(NCORES):
        xs = x_sample[16 * c:16 * c + 16]
        ck = cache_k[:, 16 * c:16 * c + 16]
        cv = cache_v[:, 16 * c:16 * c + 16]
        ckT = np.repeat(ck.transpose(0, 1, 3, 4, 2)[:, :, :, None], 2, 3)
        ckT = ckT.transpose(0, 3, 4, 1, 2, 5).reshape(2, 128, 4096)
        cvdd = np.repeat(cv[:, :, :, :, None], 2, 4)
        cvdd = cvdd.transpose(0, 2, 1, 3, 4, 5).reshape(2, 128, 4096)
        sw = state_wkv[:, 16 * c:16 * c + 16]
        hb = np.zeros((2, 16, 4, 2, 64, 2, 64), np.float32)
        swp = sw.reshape(2, 16, 4, 2, 64, 64)
        for hh in range(2):
            hb[:, :, :, hh, :, hh, :] = swp[:, :, :, hh].transpose(0, 1, 2, 4, 3)
        hb = hb.transpose(0, 1, 3, 4, 2, 5, 6).reshape(2, 16, 128, 512)
        m = dict(shared)
        m.update(xT=np.ascontiguousarray(x_prompt[2 * c:2 * c + 2].transpose(0, 2, 1)),
                 xsT=np.ascontiguousarray(xs.reshape(64, 1024).T),
                 shT=np.ascontiguousarray(state_shift[:, 16 * c:16 * c + 16].transpose(0, 2, 1)),
                 ckT=np.ascontiguousarray(ckT), cvd=np.ascontiguousarray(cvdd),
                 ckr=np.ascontiguousarray(ck.reshape(2, 16, 128 * 128)), cvr=np.ascontiguousarray(cv.reshape(2, 16, 128 * 128)),
                 hbd=np.ascontiguousarray(hb))
        in_maps.append(m)
    res = run_bass_kernel_spmd(nc, in_maps, core_ids=list(range(NCORES)))
    R = res.results
    y_prompt = np.concatenate([np.asarray(r["yT"]).transpose(0, 2, 1) for r in R], 0)
    y_sample = np.concatenate([np.asarray(r["ysT"]).T.reshape(16, 4, 1024) for r in R], 0)
    k_prompt = np.concatenate([np.asarray(r["kp"]).reshape(2, 2, 128, 2, 64) for r in R], 1)
    v_prompt = np.concatenate([np.asarray(r["vp"]).reshape(2, 2, 128, 2, 64) for r in R], 1)
    shift_prompt = np.concatenate([np.asarray(r["shp"]).reshape(2, 2, 1024) for r in R], 1)

    def unbd(a, nb):
        a = np.asarray(a).reshape(2, nb, 2, 64, 4, 2, 64)
        out = np.zeros((2, nb, 4, 2, 64, 64), np.float32)
        for hh in range(2):
            out[:, :, :, hh] = a[:, :, hh, :, :, hh, :].transpose(0, 1, 3, 4, 2)
        return out.reshape(2, nb, 8, 64, 64)
    wkv_prompt = np.concatenate([unbd(r["wkvp"], 2) for r in R], 1)
    k_sample = np.concatenate([np.asarray(r["ks"]).reshape(2, 16, 128, 2, 64) for r in R], 1)
    v_sample = np.concatenate([np.asarray(r["vs"]).reshape(2, 16, 128, 2, 64) for r in R], 1)
    shift_sample = np.concatenate([np.asarray(r["shs"]).transpose(0, 2, 1) for r in R], 1)
    wkv_sample = np.concatenate([unbd(r["wkvs"], 16) for r in R], 1)
    outs = (y_prompt, y_sample, k_prompt, v_prompt, shift_prompt, wkv_prompt, k_sample, v_sample, shift_sample, wkv_sample)
    return tuple(np.ascontiguousarray(o, dtype=np.float32) for o in outs)
```
